# Optimizing a Trainium2 kernel written in Bass

```python
import math
import jax, jax.numpy as jnp
from jax import lax
import numpy as np

D_MODEL = 1024
BATCH = 4
SEQ = 4096
DEPTH = 4
DEC_BATCH = 32
DEC_SEQ = 1
PAST_LEN = 8192
PAGE_SIZE = 128

N_MIXERS = 2
N_ATTN_LAYERS = (DEPTH + 1) // 2
N_SSM_LAYERS = DEPTH // 2

ATTN_GROUPS = ((128, 1), (512, 4), (2048, 16))
N_ATTN_GROUPS = len(ATTN_GROUPS)
ATTN_HEADS = 16
HEAD_DIM = D_MODEL // ATTN_HEADS
ATTN_WIDTH = ATTN_HEADS * HEAD_DIM
ATTN_IN_WIDTH = N_ATTN_GROUPS * 3 * ATTN_WIDTH
ROPE_THETA = 10000.0
SCALE = HEAD_DIM ** -0.5

D_INNER = 2 * D_MODEL
SSM_HEAD_DIM = 64
SSM_HEADS = D_INNER // SSM_HEAD_DIM
SSM_GROUPS = 8
HEADS_PER_GROUP = SSM_HEADS // SSM_GROUPS
D_STATE = 128
CONV_K = 4
CONV_DIM = D_INNER + 2 * SSM_GROUPS * D_STATE
SSM_IN_WIDTH = D_INNER + CONV_DIM + SSM_HEADS
SSD_CHUNK = 128

D_FF = 4 * D_MODEL
ALPHA = (2 * DEPTH) ** 0.25
BETA = (8 * DEPTH) ** -0.25
LN_EPS = 1e-5
RMS_EPS = 1e-5

kernel_name = 'hybrid_dilated_attn_ssd_decoder_step'

F32 = jnp.float32


def layer_norm(x, g, b):
    xf = x.astype(F32)
    mu = jnp.mean(xf, -1, keepdims=True)
    xc = xf - mu
    var = jnp.mean(xc * xc, -1, keepdims=True)
    return (xc * lax.rsqrt(var + LN_EPS) * g + b).astype(x.dtype)


def rope(x, pos):
    half = HEAD_DIM // 2
    inv = jnp.power(ROPE_THETA, -jnp.arange(half, dtype=F32) * (2.0 / HEAD_DIM))
    ang = pos.astype(F32)[:, None] * inv[None, :]
    cos = jnp.cos(ang)[None, :, None, :]
    sin = jnp.sin(ang)[None, :, None, :]
    xf = x.astype(F32)
    x1, x2 = xf[..., :half], xf[..., half:]
    return jnp.concatenate([x1 * cos - x2 * sin, x2 * cos + x1 * sin], -1).astype(x.dtype)


def attn_project(x, w_in, pos):
    b, s, _ = x.shape
    qkv = (x @ w_in).reshape(b, s, N_ATTN_GROUPS, 3, ATTN_HEADS, HEAD_DIM)
    return [(rope(qkv[:, :, g, 0], pos), rope(qkv[:, :, g, 1], pos), qkv[:, :, g, 2])
            for g in range(N_ATTN_GROUPS)]


def attn_merge(outs, lses, w_out, dtype):
    wts = jax.nn.softmax(jnp.stack(lses, -1), axis=-1)
    o = jnp.einsum('gbshe,bshg->bshe', jnp.stack(outs, 0), wts)
    b, s = o.shape[:2]
    return o.astype(dtype).reshape(b, s, ATTN_WIDTH) @ w_out


def dilated_band_attention(q, k, v, dil, nstep):
    b, s, h, e = q.shape
    n = s // dil
    nb = -(-n // nstep)
    pad = nb * nstep - n

    def blocks(t):
        t = t.astype(F32).reshape(b, n, dil, h, e).transpose(0, 2, 1, 3, 4)
        t = jnp.pad(t, ((0, 0), (0, 0), (0, pad), (0, 0), (0, 0)))
        return t.reshape(b, dil, nb, nstep, h, e)

    def with_prev(t):
        prev = jnp.pad(t, ((0, 0), (0, 0), (1, 0), (0, 0), (0, 0), (0, 0)))[:, :, :nb]
        return jnp.concatenate([prev, t], axis=3)

    qb = blocks(q)
    kx = with_prev(blocks(k))
    vx = with_prev(blocks(v))
    scores = jnp.einsum('brnqhe,brnkhe->brnhqk', qb, kx) * SCALE
    qi = jnp.arange(nstep)[:, None]
    kj = jnp.arange(2 * nstep)[None, :]
    steps = qi + nstep - kj
    key_m = jnp.arange(nb)[:, None, None] * nstep - nstep + kj[None]
    mask = (steps >= 0) & (steps <= nstep) & (key_m >= 0)
    scores = jnp.where(mask[None, None, :, None], scores, -jnp.inf)
    mx = jnp.max(scores, -1, keepdims=True)
    p = jnp.exp(scores - mx)
    den = jnp.sum(p, -1)
    o = jnp.einsum('brnhqk,brnkhe->brnqhe', p, vx) / jnp.swapaxes(den, -1, -2)[..., None]
    lse = jnp.swapaxes(mx[..., 0] + jnp.log(den), -1, -2)
    o = o.reshape(b, dil, nb * nstep, h, e)[:, :, :n].transpose(0, 2, 1, 3, 4).reshape(b, s, h, e)
    lse = lse.reshape(b, dil, nb * nstep, h)[:, :, :n].transpose(0, 2, 1, 3).reshape(b, s, h)
    return o, lse


def dilated_gather_attention(q, k, v, buf, dil, nstep, window):
    t = q.shape[1]
    L = buf.shape[1]
    ext = jnp.concatenate([buf, jnp.stack([k, v], axis=2)], axis=1)
    idx = L + jnp.arange(t)[:, None] - dil * jnp.arange(nstep + 1)[None, :]
    valid = idx >= 0
    g = ext[:, jnp.maximum(idx, 0)]
    kg = g[:, :, :, 0].astype(F32)
    vg = g[:, :, :, 1].astype(F32)
    scores = jnp.einsum('bthe,btshe->bths', q.astype(F32), kg) * SCALE
    scores = jnp.where(valid[None, :, None, :], scores, -jnp.inf)
    mx = jnp.max(scores, -1, keepdims=True)
    p = jnp.exp(scores - mx)
    den = jnp.sum(p, -1)
    o = jnp.einsum('bths,btshe->bthe', p, vg) / den[..., None]
    lse = mx[..., 0] + jnp.log(den)
    new_buf = ext[:, -min(window, L + t):]
    return o, lse, new_buf


def attn_prompt(x, pos, w_in, w_out):
    qkv = attn_project(x, w_in, pos)
    outs, lses, bufs = [], [], []
    for (win, dil), (q, k, v) in zip(ATTN_GROUPS, qkv):
        o, l = dilated_band_attention(q, k, v, dil, win // dil)
        outs.append(o)
        lses.append(l)
        bufs.append(jnp.stack([k, v], axis=2)[:, -min(win, x.shape[1]):])
    return attn_merge(outs, lses, w_out, x.dtype), bufs


def attn_sample(x, pos, caches, w_in, w_out):
    qkv = attn_project(x, w_in, pos)
    outs, lses, bufs = [], [], []
    for (win, dil), (q, k, v), buf in zip(ATTN_GROUPS, qkv, caches):
        o, l, nbuf = dilated_gather_attention(q, k, v, buf, dil, win // dil, win)
        outs.append(o)
        lses.append(l)
        bufs.append(nbuf)
    return attn_merge(outs, lses, w_out, x.dtype), bufs


def ssd_chunked(x, dt, a, bm, cm, h0):
    b, s, g, r, p = x.shape
    n = bm.shape[-1]
    L = SSD_CHUNK if s % SSD_CHUNK == 0 else s
    nc = s // L
    xc = x.astype(F32).reshape(b, nc, L, g, r, p)
    dtc = dt.reshape(b, nc, L, g, r)
    bc = bm.astype(F32).reshape(b, nc, L, g, n)
    cc = cm.astype(F32).reshape(b, nc, L, g, n)
    acum = jnp.cumsum(dtc * a, axis=2)
    xdt = xc * dtc[..., None]
    at = jnp.moveaxis(acum, 2, -1)
    diff = at[..., :, None] - at[..., None, :]
    causal = jnp.tril(jnp.ones((L, L), dtype=bool))
    decay = jnp.exp(jnp.where(causal, diff, -jnp.inf))
    cb = jnp.einsum('bclgn,bcsgn->bcgls', cc, bc)
    y_diag = jnp.einsum('bcgrls,bcsgrp->bclgrp', cb[:, :, :, None] * decay, xdt)
    last = acum[:, :, -1]
    to_end = jnp.exp(last[:, :, None] - acum)
    states = jnp.einsum('bclgn,bclgrp->bcgrpn', bc, xdt * to_end[..., None])

    def step(h, inp):
        dec, st = inp
        return dec[..., None, None] * h + st, h

    h_final, h_in = lax.scan(step, h0.astype(F32),
                             (jnp.moveaxis(jnp.exp(last), 1, 0), jnp.moveaxis(states, 1, 0)))
    h_in = jnp.moveaxis(h_in, 0, 1)
    y_off = jnp.einsum('bclgn,bcgrpn->bclgrp', cc, h_in) * jnp.exp(acum)[..., None]
    return (y_diag + y_off).reshape(b, s, g, r, p), h_final


def ssd_mixer(x, conv_state, h0, w_in, conv_w, conv_b, dt_bias, a_log, d_skip, norm_w, w_out):
    b, s, _ = x.shape
    proj = x @ w_in
    z = proj[..., :D_INNER]
    xbc = proj[..., D_INNER:D_INNER + CONV_DIM]
    dt_raw = proj[..., D_INNER + CONV_DIM:]
    xbc_ext = jnp.concatenate([conv_state.astype(xbc.dtype), xbc], axis=1)
    new_conv = xbc_ext[:, -(CONV_K - 1):]
    xbc = lax.conv_general_dilated(xbc_ext, conv_w.astype(xbc.dtype)[:, None, :], (1,), 'VALID',
                                   dimension_numbers=('NWC', 'WIO', 'NWC'),
                                   feature_group_count=CONV_DIM)
    xbc = jax.nn.silu(xbc + conv_b)
    gn = SSM_GROUPS * D_STATE
    xs = xbc[..., :D_INNER].reshape(b, s, SSM_GROUPS, HEADS_PER_GROUP, SSM_HEAD_DIM)
    bm = xbc[..., D_INNER:D_INNER + gn].reshape(b, s, SSM_GROUPS, D_STATE)
    cm = xbc[..., D_INNER + gn:].reshape(b, s, SSM_GROUPS, D_STATE)
    dt = jax.nn.softplus(dt_raw.astype(F32) + dt_bias.astype(F32)).reshape(b, s, SSM_GROUPS, HEADS_PER_GROUP)
    a = -jnp.exp(a_log.astype(F32)).reshape(SSM_GROUPS, HEADS_PER_GROUP)
    y, h = ssd_chunked(xs, dt, a, bm, cm,
                       h0.reshape(b, SSM_GROUPS, HEADS_PER_GROUP, SSM_HEAD_DIM, D_STATE))
    y = y + xs.astype(F32) * d_skip.astype(F32).reshape(SSM_GROUPS, HEADS_PER_GROUP)[..., None]
    y = y.reshape(b, s, D_INNER) * jax.nn.silu(z.astype(F32))
    yg = y.reshape(b, s, SSM_GROUPS, D_INNER // SSM_GROUPS)
    yg = yg * lax.rsqrt(jnp.mean(yg * yg, -1, keepdims=True) + RMS_EPS)
    out = (yg.reshape(b, s, D_INNER) * norm_w).astype(x.dtype) @ w_out
    return out, new_conv, h.reshape(b, SSM_HEADS, SSM_HEAD_DIM, D_STATE)


def sq_relu_mlp(x, w1, w2):
    h = jax.nn.relu(x @ w1)
    return (h * h) @ w2


def setup_inputs(seed: int = 0) -> dict:
    key = jax.random.key(seed)
    ks = jax.random.split(key, 24)

    def nrm(k, shape, scale):
        return jax.random.normal(k, shape, F32) * scale

    kv_shape = lambda win: (N_ATTN_LAYERS, DEC_BATCH, min(win, PAST_LEN), 2, ATTN_HEADS, HEAD_DIM)
    dt0 = jnp.exp(jax.random.uniform(ks[12], (N_SSM_LAYERS, SSM_HEADS), F32,
                                     minval=math.log(1e-3), maxval=math.log(1e-1)))
    return {
        'x_prompt': nrm(ks[0], (BATCH, SEQ, D_MODEL), 1.0),
        'x_sample': nrm(ks[1], (DEC_BATCH, DEC_SEQ, D_MODEL), 1.0),
        'cache_kv_w128': nrm(ks[2], kv_shape(ATTN_GROUPS[0][0]), 1.0),
        'cache_kv_w512': nrm(ks[3], kv_shape(ATTN_GROUPS[1][0]), 1.0),
        'cache_kv_w2048': nrm(ks[4], kv_shape(ATTN_GROUPS[2][0]), 1.0),
        'state_ssm': nrm(ks[5], (N_SSM_LAYERS, DEC_BATCH, SSM_HEADS, SSM_HEAD_DIM, D_STATE), 0.1),
        'state_conv': nrm(ks[6], (N_SSM_LAYERS, DEC_BATCH, CONV_K - 1, CONV_DIM), 1.0),
        'attn_w_in': nrm(ks[7], (N_ATTN_LAYERS, D_MODEL, ATTN_IN_WIDTH), D_MODEL ** -0.5),
        'attn_w_out': nrm(ks[8], (N_ATTN_LAYERS, ATTN_WIDTH, D_MODEL), BETA * ATTN_WIDTH ** -0.5),
        'ssm_w_in': nrm(ks[9], (N_SSM_LAYERS, D_MODEL, SSM_IN_WIDTH), D_MODEL ** -0.5),
        'ssm_conv_w': nrm(ks[10], (N_SSM_LAYERS, CONV_K, CONV_DIM), CONV_K ** -0.5),
        'ssm_conv_b': nrm(ks[11], (N_SSM_LAYERS, CONV_DIM), 0.02),
        'ssm_dt_bias': dt0 + jnp.log(-jnp.expm1(-dt0)),
        'ssm_a_log': jnp.log(jax.random.uniform(ks[13], (N_SSM_LAYERS, SSM_HEADS), F32, minval=1.0, maxval=16.0)),
        'ssm_d': 1.0 + nrm(ks[14], (N_SSM_LAYERS, SSM_HEADS), 0.02),
        'ssm_norm_w': 1.0 + nrm(ks[15], (N_SSM_LAYERS, D_INNER), 0.02),
        'ssm_w_out': nrm(ks[16], (N_SSM_LAYERS, D_INNER, D_MODEL), BETA * D_INNER ** -0.5),
        'mlp_w1': nrm(ks[17], (DEPTH, D_MODEL, D_FF), D_MODEL ** -0.5),
        'mlp_w2': nrm(ks[18], (DEPTH, D_FF, D_MODEL), BETA * D_FF ** -0.5),
        'ln_mix_g': 1.0 + nrm(ks[19], (DEPTH, D_MODEL), 0.02),
        'ln_mix_b': nrm(ks[20], (DEPTH, D_MODEL), 0.02),
        'ln_ffn_g': 1.0 + nrm(ks[21], (DEPTH, D_MODEL), 0.02),
        'ln_ffn_b': nrm(ks[22], (DEPTH, D_MODEL), 0.02),
    }


def reference(x_prompt, x_sample, cache_kv_w128, cache_kv_w512, cache_kv_w2048, state_ssm, state_conv,
              attn_w_in, attn_w_out, ssm_w_in, ssm_conv_w, ssm_conv_b, ssm_dt_bias, ssm_a_log, ssm_d,
              ssm_norm_w, ssm_w_out, mlp_w1, mlp_w2, ln_mix_g, ln_mix_b, ln_ffn_g, ln_ffn_b):
    caches = (cache_kv_w128, cache_kv_w512, cache_kv_w2048)
    pos_p = jnp.arange(x_prompt.shape[1], dtype=jnp.int32)
    pos_s = PAST_LEN + jnp.arange(x_sample.shape[1], dtype=jnp.int32)
    xp, xs = x_prompt, x_sample
    bp, bs = xp.shape[0], xs.shape[0]
    kv_p = [[] for _ in ATTN_GROUPS]
    kv_s = [[] for _ in ATTN_GROUPS]
    ssm_p, conv_p, ssm_s, conv_s = [], [], [], []
    for i in range(DEPTH):
        j = i // N_MIXERS
        if i % N_MIXERS == 0:
            hp, bufs_p = attn_prompt(xp, pos_p, attn_w_in[j], attn_w_out[j])
            hs, bufs_s = attn_sample(xs, pos_s, [c[j] for c in caches], attn_w_in[j], attn_w_out[j])
            for g in range(N_ATTN_GROUPS):
                kv_p[g].append(bufs_p[g])
                kv_s[g].append(bufs_s[g])
        else:
            prm = (ssm_w_in[j], ssm_conv_w[j], ssm_conv_b[j], ssm_dt_bias[j], ssm_a_log[j],
                   ssm_d[j], ssm_norm_w[j], ssm_w_out[j])
            conv0 = jnp.zeros((bp, CONV_K - 1, CONV_DIM), xp.dtype)
            h0 = jnp.zeros((bp, SSM_HEADS, SSM_HEAD_DIM, D_STATE), F32)
            hp, cp, sp = ssd_mixer(xp, conv0, h0, *prm)
            hs, cs, ss = ssd_mixer(xs, state_conv[j], state_ssm[j], *prm)
            ssm_p.append(sp)
            conv_p.append(cp)
            ssm_s.append(ss)
            conv_s.append(cs)
        xp = layer_norm(ALPHA * xp + hp, ln_mix_g[i], ln_mix_b[i])
        xs = layer_norm(ALPHA * xs + hs, ln_mix_g[i], ln_mix_b[i])
        xp = layer_norm(ALPHA * xp + sq_relu_mlp(xp, mlp_w1[i], mlp_w2[i]), ln_ffn_g[i], ln_ffn_b[i])
        xs = layer_norm(ALPHA * xs + sq_relu_mlp(xs, mlp_w1[i], mlp_w2[i]), ln_ffn_g[i], ln_ffn_b[i])
    kv128_p, kv512_p, kv2048_p = jnp.stack(kv_p[0]), jnp.stack(kv_p[1]), jnp.stack(kv_p[2])
    kv128_s, kv512_s, kv2048_s = jnp.stack(kv_s[0]), jnp.stack(kv_s[1]), jnp.stack(kv_s[2])
    new_ssm_p, new_conv_p = jnp.stack(ssm_p), jnp.stack(conv_p)
    new_ssm_s, new_conv_s = jnp.stack(ssm_s), jnp.stack(conv_s)
    return (xp, xs, kv128_p, kv512_p, kv2048_p, new_ssm_p, new_conv_p,
            kv128_s, kv512_s, kv2048_s, new_ssm_s, new_conv_s)
```

```python
import numpy as np
import concourse.bass as bass
import concourse.mybir as mybir

F32 = mybir.dt.float32
BF16 = mybir.dt.bfloat16
AF = mybir.ActivationFunctionType
ALU = mybir.AluOpType
AX = mybir.AxisListType

ENGS = ['pe', 'act', 'dve', 'pool', 'sp']
EPOCH = 20000
NDMA = 24


class Sched:
    def __init__(self, nc, es):
        self.nc = nc
        self.es = es
        self.q = {e: [] for e in ENGS}
        self.nev = {e: 0 for e in ENGS}
        self.esem = {}
        self.seen = {e: {} for e in ENGS}
        self.res = {}
        self.dsem = [es.enter_context(nc.semaphore(f"dma{i}")) for i in range(NDMA)]
        self.dcnt = [0] * NDMA
        self.drr = 0
        self.nops = 0
        self.own = set()

    def _esem(self, eng, epoch):
        k = (eng, epoch)
        if k not in self.esem:
            self.esem[k] = self.es.enter_context(self.nc.semaphore(f"ev_{eng}_{epoch}"))
        return self.esem[k]

    def _ev_semval(self, ev):
        if ev[0] == 'e':
            _, eng, idx = ev
            epoch = (idx - 1) // EPOCH
            return ('e', eng, epoch), self._esem(eng, epoch), idx - epoch * EPOCH
        _, si, val = ev
        return ('d', si), self.dsem[si], val

    def _deps(self, reads, writes, stream=None):
        deps = []
        for r in reads:
            st = self.res.get(r)
            if st and st['w']:
                deps.append(st['w'])
            if st and isinstance(r, tuple) and r[0] == 'ps':
                deps.extend(ev for sname, ev in st['r'].items() if sname != stream)
        for w in writes:
            st = self.res.get(w)
            if st:
                if st['w']:
                    deps.append(st['w'])
                deps.extend(st['r'].values())
        return deps

    def _waits(self, eng, deps):
        waits = []
        for ev in deps:
            if ev[0] == 'e' and ev[1] == 'pe' and eng == 'pe':
                continue
            key, sem, val = self._ev_semval(ev)
            if self.seen[eng].get(key, 0) >= val:
                continue
            self.seen[eng][key] = val
            waits.append((sem, val))
        return waits

    def _mark(self, stream, ev, reads, writes):
        for r in reads:
            st = self.res.setdefault(r, {'w': None, 'r': {}})
            st['r'][stream] = ev
        for w in writes:
            self.res[w] = {'w': ev, 'r': {}}

    def op(self, eng, name, *args, reads=(), writes=(), signal=True, **kw):
        fn = (name, args, kw)
        deps = self._deps(reads, writes, eng)
        waits = self._waits(eng, deps)
        if signal:
            self.nev[eng] += 1
            idx = self.nev[eng]
            epoch = (idx - 1) // EPOCH
            sem = self._esem(eng, epoch)
        else:
            idx = self.nev[eng] + 1
            sem = None
        ev = ('e', eng, idx)
        self._mark(eng, ev, reads, writes)
        self.q[eng].append((waits, fn, sem, 1))
        self.nops += 1
        return ev

    def dma(self, qeng, reads=(), writes=(), own=False, **kw):
        fn = ('dma_start', (), kw)
        if own:
            si = len(self.dsem)
            self.dsem.append(self.es.enter_context(self.nc.semaphore(f"dmaown{si}")))
            self.dcnt.append(0)
            self.own.add(si)
        else:
            si = self.drr
            self.drr = (self.drr + 1) % NDMA
        deps = self._deps(reads, writes)
        if self.dcnt[si] > 0:
            deps.append(('d', si, self.dcnt[si]))
        waits = self._waits(qeng, deps)
        self.dcnt[si] += 16
        ev = ('d', si, self.dcnt[si])
        self._mark(('d', si), ev, reads, writes)
        self.q[qeng].append((waits, fn, self.dsem[si], 16))
        self.nops += 1
        return ev

    def barrier(self):
        deps = [('d', si, c) for si, c in enumerate(self.dcnt) if c > 0 and si not in self.own]
        for e in ENGS:
            if self.nev[e] > 0:
                deps.append(('e', e, self.nev[e]))
        for eng in ENGS:
            waits = []
            for ev in deps:
                key, sem, val = self._ev_semval(ev)
                if self.seen[eng].get(key, 0) >= val:
                    continue
                self.seen[eng][key] = val
                waits.append((sem, val))
            if waits:
                self.q[eng].append((waits, None, None, 0))

    def final_wait(self, eng='sp'):
        deps = [('d', si, c) for si, c in enumerate(self.dcnt) if c > 0]
        for e in ENGS:
            if e != eng and self.nev[e] > 0:
                deps.append(('e', e, self.nev[e]))
        waits = self._waits(eng, deps)
        self.q[eng].append((waits, None, None, 0))

    def emit(self):
        nc = self.nc
        with nc.Block() as block:
            def run(e_obj, name):
                for waits, fn, sem, inc in self.q[name]:
                    for s, v in waits:
                        e_obj.wait_ge(s, v)
                    if fn is None:
                        continue
                    ins = getattr(e_obj, fn[0])(*fn[1], **fn[2])
                    if sem is not None:
                        ins.then_inc(sem, inc)

            @block.tensor
            def _(e):
                run(e, 'pe')

            @block.scalar
            def _(e):
                run(e, 'act')

            @block.vector
            def _(e):
                run(e, 'dve')

            @block.gpsimd
            def _(e):
                run(e, 'pool')

            @block.sync
            def _(e):
                run(e, 'sp')


import contextlib
import numpy as np

D = 1024
DFF = 4096
ALPHA = 8 ** 0.25
LN_EPS = 1e-5
RMS_EPS = 1e-5
SCALE = 0.125
NS = 4


def sb_ap(t, rowsz, off, dims, p0=0, np_=128):
    return bass.AP(t, p0 * rowsz + off, [[rowsz, np_]] + [list(d) for d in dims])


class Ring:
    def __init__(self, nc, es, name, shape, dt, n):
        self.t = [es.enter_context(nc.sbuf_tensor(f"{name}{i}", shape, dt)) for i in range(n)]
        self.name = name
        self.i = 0
        self.n = n

    def next(self):
        i = self.i
        self.i = (i + 1) % self.n
        return self.t[i], (self.name, i)


class KBCore:
    def __init__(self, T=4096, only=None):
        self.only = only
        self.T = T
        self.NT = T // 128
        self.nc = bass.Bass("TRN2", target_bir_lowering=False)
        self.es = contextlib.ExitStack()
        self.S = Sched(self.nc, self.es)
        self.io = {}
        self.pes = None

    def phase_begin(self):
        self.S.barrier()
        self.pes = contextlib.ExitStack()

    def phase_end(self):
        self.S.barrier()
        self.pes.close()
        self.pes = None

    def din(self, name, shape, dt=F32):
        if self.only is not None and name not in self.only:
            return None
        a = self.nc.dram_tensor(name, list(shape), dt, kind="ExternalInput").ap()
        self.io[name] = a
        return a

    def dout(self, name, shape, dt=F32):
        if self.only is not None and name not in self.only:
            return None
        a = self.nc.dram_tensor(name, list(shape), dt, kind="ExternalOutput").ap()
        self.io[name] = a
        return a

    def dscr(self, name, shape, dt):
        a = self.nc.dram_tensor(name, list(shape), dt, kind="ExternalOutput").ap()
        self.io[name] = a
        return a

    def _uniq(self, name):
        self.ncount = getattr(self, 'ncount', 0) + 1
        return "%s_u%d" % (name, self.ncount)

    def sb(self, name, shape, dt=F32):
        es = self.pes if self.pes is not None else self.es
        return es.enter_context(self.nc.sbuf_tensor(self._uniq(name), list(shape), dt))

    def ring(self, name, shape, dt, n):
        es = self.pes if self.pes is not None else self.es
        return Ring(self.nc, es, self._uniq(name), list(shape), dt, n)

    def declare(self):
        T = self.T
        d = self.din
        d("x_prompt", [T, D]); d("x_sample", [NS, D])
        d("cache_kv_w128", [2, NS, 128, 2048]); d("cache_kv_w512", [2, NS, 512, 2048])
        d("cache_kv_w2048", [2, NS, 2048, 2048])
        d("state_ssm", [2, NS, 2048, 128]); d("state_conv", [2, NS, 3, 4096])
        d("attn_w_in", [2, D, 9216]); d("attn_w_out", [2, D, D])
        d("ssm_w_in", [2, D, 6176]); d("ssm_conv_w", [2, 4, 4096]); d("ssm_conv_b", [2, 4096])
        d("ssm_dt_bias", [2, 32]); d("ssm_a_log", [2, 32]); d("ssm_d", [2, 32])
        d("ssm_norm_w", [2, 2048]); d("ssm_w_out", [2, 2048, D])
        d("mlp_w1", [4, D, DFF]); d("mlp_w2", [4, DFF, D])
        d("ln_mix_g", [4, D]); d("ln_mix_b", [4, D]); d("ln_ffn_g", [4, D]); d("ln_ffn_b", [4, D])
        d("c_ident", [128, 128]); d("c_tri", [128, 128]); d("c_ustrict", [128, 128])
        d("c_amask", [128, 512]); d("c_rope", [3, self.NT, 128, 64]); d("c_rope_s", [1, 64])
        d("c_sel", [NS, NS * 128]); d("c_onescol", [128, NS * NS])
        o = self.dout
        o("y_prompt", [T, D]); o("y_sample", [NS, D])
        o("kv128_p", [2, 128, 2048]); o("kv512_p", [2, 512, 2048]); o("kv2048_p", [2, 2048, 2048])
        o("ssm_p", [2, 2048, 128]); o("conv_p", [2, 3, 4096])
        o("kv128_s", [2, NS, 128, 2048]); o("kv512_s", [2, NS, 512, 2048]); o("kv2048_s", [2, NS, 2048, 2048])
        o("ssm_s", [2, NS, 2048, 128]); o("conv_s", [2, NS, 3, 4096])

    def setup(self):
        nc, S, T = self.nc, self.S, self.T
        self.PS = [self.es.enter_context(nc.psum_tensor(f"ps{i}", [128, 512], F32)) for i in range(8)]
        self.psi = 0
        self.xT = self.sb("xT", [128, 8, T], BF16)
        self.xsT = self.sb("xsT", [128, 8, NS], BF16)
        self.xs = self.sb("xs", [NS, D], F32)
        self.identf = self.sb("identf", [128, 128], F32)
        self.identb = self.sb("identb", [128, 128], BF16)
        io = self.io
        S.dma('sp', out=self.identf[:], in_=io["c_ident"], writes=['identf'])
        S.op('dve', 'tensor_copy', self.identb[:], self.identf[:], reads=['identf'], writes=['identb'])

    def ln_alloc(self):
        self.lnp = self.sb("lnp", [128, 2, D], F32)
        self.r_x32 = self.ring("x32_", [128, D], F32, 2)
        self.r_s32 = self.ring("s32_", [128, D], F32, 2)
        self.r_xb = self.ring("xb_", [128, D], BF16, 2)
        self.r_st = self.ring("st_", [128, 16], F32, 4)

    def ps(self):
        i = self.psi
        self.psi = (i + 1) % 8
        return self.PS[i], ('ps', i)

    def to_feat(self, src_t, src_res, rows, dstT, dst_res, dst_rowsz, col0):
        S = self.S
        ps, pr = self.ps()
        psb = ps.bitcast(BF16)
        for k in range(8):
            S.op('pe', 'transpose', psb[:, k * 128:k * 128 + rows], src_t[0:rows, k * 128:(k + 1) * 128],
                 self.identb[0:rows, 0:rows], reads=[src_res, 'identb'], writes=[pr], signal=(k == 7))
        src = sb_ap(psb, 1024, 0, [[128, 8], [1, rows]])
        dst = sb_ap(dstT, dst_rowsz, col0, [[dst_rowsz // 8, 8], [1, rows]])
        S.op('act', 'copy', dst, src, reads=[pr], writes=[dst_res])

    def load_ln_params(self, layer, which):
        S, io = self.S, self.io
        for j, nm in enumerate(["ln_%s_g" % which, "ln_%s_b" % which]):
            src = io[nm][layer:layer + 1, :].partition_broadcast(128)
            S.dma('sp', out=self.lnp[:, j, :], in_=src, writes=[('lnp', j)])

    def ln_tile(self, h_aps, h_res, xold_ap, xold_res, which, rows):
        S = self.S
        s32, sr = self.r_s32.next()
        st, str_ = self.r_st.next()
        xn, xnr = s32, sr
        xb, xbr = self.r_xb.next()
        for hf in range(2):
            S.op('dve', 'scalar_tensor_tensor', out=s32[0:rows, hf * 512:(hf + 1) * 512],
                 in0=xold_ap[0:rows, hf * 512:(hf + 1) * 512], scalar=ALPHA, in1=h_aps[hf], op0=ALU.mult, op1=ALU.add,
                 reads=[xold_res] + list(h_res), writes=[sr])
        for hf in range(2):
            S.op('dve', 'bn_stats', out=st[0:rows, hf * 6:(hf + 1) * 6], in_=s32[0:rows, hf * 512:(hf + 1) * 512],
                 reads=[sr], writes=[str_])
        S.op('dve', 'bn_aggr', out=st[0:rows, 12:14], in_=st[0:rows, 0:12], reads=[str_], writes=[str_])
        S.op('dve', 'tensor_scalar', out=st[0:rows, 15:16], in0=st[0:rows, 13:14], scalar1=LN_EPS, scalar2=None,
             op0=ALU.add, reads=[str_], writes=[str_])
        S.op('act', 'activation', out=st[0:rows, 15:16], in_=st[0:rows, 15:16], func=AF.Ln, reads=[str_], writes=[str_])
        S.op('act', 'activation', out=st[0:rows, 14:15], in_=st[0:rows, 15:16], func=AF.Exp, scale=-0.5, reads=[str_], writes=[str_])
        S.op('dve', 'tensor_scalar', out=xn[0:rows, :], in0=s32[0:rows, :], scalar1=st[0:rows, 12:13],
             scalar2=st[0:rows, 14:15], op0=ALU.subtract, op1=ALU.mult, reads=[sr, str_], writes=[xnr])
        gi = 0
        S.op('dve', 'tensor_tensor', out=xn[0:rows, :], in0=xn[0:rows, :], in1=self.lnp[0:rows, gi, :], op=ALU.mult,
             reads=[xnr, ('lnp', gi)], writes=[xnr])
        S.op('dve', 'tensor_tensor', out=xn[0:rows, :], in0=xn[0:rows, :], in1=self.lnp[0:rows, gi + 1, :], op=ALU.add,
             reads=[xnr, ('lnp', gi + 1)], writes=[xnr])
        S.op('act', 'copy', xb[0:rows, :], xn[0:rows, :], reads=[xnr], writes=[xbr])
        return xn, xnr, xb, xbr

    def ln_prompt_tile(self, tt, h_aps, h_res, which, first=False):
        S, io = self.S, self.io
        src = io["x_prompt"] if first else io["y_prompt"]
        x32, xr = self.r_x32.next()
        S.dma('sp', out=x32[:], in_=src[tt * 128:(tt + 1) * 128, :], reads=[('yp', tt)], writes=[xr])
        xn, xnr, xb, xbr = self.ln_tile(h_aps, h_res, x32, xr, which, 128)
        S.dma('sp', out=io["y_prompt"][tt * 128:(tt + 1) * 128, :], in_=xn[:], reads=[xnr], writes=[('yp', tt)])
        self.to_feat(xb, xbr, 128, self.xT, ('xT', tt), 8 * self.T, tt * 128)

    def ln_sample(self, h_aps, h_res, which):
        S = self.S
        xn, xnr, xb, xbr = self.ln_tile(h_aps, h_res, self.xs, 'xs', which, NS)
        S.op('dve', 'tensor_copy', self.xs[0:NS, :], xn[0:NS, :], reads=[xnr], writes=['xs'])
        self.to_feat(xb, xbr, NS, self.xsT, 'xsT', 8 * NS, 0)

    def phase_init(self):
        S, io = self.S, self.io
        self.phase_begin()
        self.ln_alloc()
        for tt in range(self.NT):
            x32, xr = self.r_x32.next()
            xb, xbr = self.r_xb.next()
            S.dma('sp', out=x32[:], in_=io["x_prompt"][tt * 128:(tt + 1) * 128, :], writes=[xr])
            S.op('act', 'copy', xb[:], x32[:], reads=[xr], writes=[xbr])
            self.to_feat(xb, xbr, 128, self.xT, ('xT', tt), 8 * self.T, tt * 128)
        S.dma('sp', out=self.xs[:], in_=io["x_sample"], writes=['xs'])
        xb, xbr = self.r_xb.next()
        S.op('act', 'copy', xb[0:NS, :], self.xs[0:NS, :], reads=['xs'], writes=[xbr])
        self.to_feat(xb, xbr, NS, self.xsT, 'xsT', 8 * NS, 0)
        self.phase_end()

    def mlp_alloc(self):
        self.TB = min(1024, self.T)
        self.acc = self.sb("mlp_acc", [128, self.TB // 128, D], F32)
        self.accs = self.sb("mlp_accs", [NS, D], F32)
        self.r_w1 = self.ring("w1_", [128, 8, 512], BF16, 3)
        self.r_w2 = self.ring("w2_", [128, 4, 1024], BF16, 3)
        self.r_hT = self.ring("hT_", [128, 4, self.TB], BF16, 2)
        self.r_hs = self.ring("hs_", [128, 4, NS], BF16, 2)
        self.r_relu = self.ring("relu_", [128, 512], F32, 3)

    def phase_mlp(self, layer, first=False):
        self.phase_begin()
        self.ln_alloc()
        self.mlp_alloc()
        self.load_ln_params(layer, 'ffn')
        S, io, T, TB = self.S, self.io, self.T, self.TB
        nblk = T // TB
        ntile = TB // 128
        w1d = io["mlp_w1"][layer].rearrange("(k q) c -> q k c", q=128)
        w2d = io["mlp_w2"][layer]
        pieces = [(b, p) for b in range(nblk) for p in range(8)]
        loaded = {}

        def issue(idx):
            b, p = pieces[idx]
            w1, w1r = self.r_w1.next()
            w2, w2r = self.r_w2.next()
            S.dma('pool', out=w1[:], in_=w1d[:, :, p * 512:(p + 1) * 512], writes=[w1r])
            S.dma('pool', out=w2[:], in_=w2d[p * 512:(p + 1) * 512, :].rearrange("(c q) n -> q c n", q=128), writes=[w2r])
            loaded[idx] = (w1, w1r, w2, w2r)

        issue(0)
        issue(1)
        for idx, (b, p) in enumerate(pieces):
            if idx + 2 < len(pieces):
                issue(idx + 2)
            w1, w1r, w2, w2r = loaded.pop(idx)
            tok0 = b * TB
            hT, hTr = self.r_hT.next()
            for c in range(4):
                for ts in range(TB // 512):
                    ps, pr = self.ps()
                    for k in range(8):
                        S.op('pe', 'matmul', ps[:, 0:512], lhsT=w1[:, k, c * 128:(c + 1) * 128],
                             rhs=self.xT[:, k, tok0 + ts * 512: tok0 + (ts + 1) * 512], start=(k == 0), stop=(k == 7),
                             reads=[w1r] + [('xT', (tok0 + ts * 512) // 128 + j) for j in range(4)], writes=[pr], signal=(k == 7))
                    rl, rlr = self.r_relu.next()
                    S.op('act', 'activation', out=rl[:], in_=ps[:, 0:512], func=AF.Relu, reads=[pr], writes=[rlr])
                    S.op('dve', 'tensor_tensor', out=hT[:, c, ts * 512:(ts + 1) * 512], in0=rl[:], in1=rl[:], op=ALU.mult,
                         reads=[rlr], writes=[(hTr, ts)])
            if b == 0:
                hs, hsr = self.r_hs.next()
                ps, pr = self.ps()
                for c in range(4):
                    for k in range(8):
                        S.op('pe', 'matmul', ps[:, c * NS:(c + 1) * NS], lhsT=w1[:, k, c * 128:(c + 1) * 128], rhs=self.xsT[:, k, :],
                             start=(k == 0), stop=(k == 7), reads=[w1r, 'xsT'], writes=[pr], signal=(c == 3 and k == 7))
                rl, rlr = self.r_relu.next()
                S.op('act', 'activation', out=rl[:, 0:4 * NS], in_=ps[:, 0:4 * NS], func=AF.Relu, reads=[pr], writes=[rlr])
                S.op('dve', 'tensor_tensor', out=hs[:].rearrange("p c n -> p (c n)"), in0=rl[:, 0:4 * NS], in1=rl[:, 0:4 * NS],
                     op=ALU.mult, reads=[rlr], writes=[hsr])
                for hf in range(2):
                    pp, ppr = self.ps()
                    for c in range(4):
                        S.op('pe', 'matmul', pp[0:NS, 0:512], lhsT=hs[:, c, :], rhs=w2[:, c, hf * 512:(hf + 1) * 512],
                             start=(c == 0), stop=(c == 3), reads=[hsr, w2r], writes=[ppr], signal=(c == 3))
                    dst = self.accs[0:NS, hf * 512:(hf + 1) * 512]
                    if p == 0:
                        S.op('act', 'copy', dst, pp[0:NS, 0:512], reads=[ppr], writes=[('accs', hf)])
                    else:
                        S.op('dve', 'tensor_tensor', out=dst, in0=dst, in1=pp[0:NS, 0:512], op=ALU.add,
                             reads=[ppr, ('accs', hf)], writes=[('accs', hf)])
            for t in range(ntile):
                for hf in range(2):
                    pp, ppr = self.ps()
                    for c in range(4):
                        S.op('pe', 'matmul', pp[:, 0:512], lhsT=hT[:, c, t * 128:(t + 1) * 128], rhs=w2[:, c, hf * 512:(hf + 1) * 512],
                             start=(c == 0), stop=(c == 3), reads=[(hTr, t // 4), w2r], writes=[ppr], signal=(c == 3))
                    dst = self.acc[:, t, hf * 512:(hf + 1) * 512]
                    if p == 0:
                        S.op('act', 'copy', dst, pp[:, 0:512], reads=[ppr], writes=[('acc', t, hf)])
                    else:
                        S.op('dve', 'tensor_tensor', out=dst, in0=dst, in1=pp[:, 0:512], op=ALU.add,
                             reads=[ppr, ('acc', t, hf)], writes=[('acc', t, hf)])
            if p == 7:
                for t in range(ntile):
                    tt = b * ntile + t
                    self.ln_prompt_tile(tt, [self.acc[:, t, 0:512], self.acc[:, t, 512:1024]], [('acc', t, 0), ('acc', t, 1)],
                                        'ffn', first=first)
                if b == 0:
                    self.ln_sample([self.accs[0:NS, 0:512], self.accs[0:NS, 512:1024]], [('accs', 0), ('accs', 1)], 'ffn')
        self.phase_end()

    def finish(self):
        S, io = self.S, self.io
        S.dma('sp', out=io["y_sample"], in_=self.xs[:], reads=['xs'], writes=['y_sample'])
        S.final_wait('sp')
        S.emit()
        self.es.close()
        return self.nc


GROUPS = ((128, 1), (512, 4), (2048, 16))


class KBAttn(KBCore):
    def attn_alloc(self):
        T, NT, S, io = self.T, self.NT, self.S, self.io
        self.r_aw = self.ring("aw_", [128, 8, 384], BF16, 3)
        self.r_qk = self.ring("qkT_", [128, 2, T], BF16, 2)
        self.r_v = self.ring("vaug_", [128, NT, 2, 65], BF16, 2)
        self.aacc = self.sb("aacc", [65, 2, T], F32)
        self.r_rt = self.ring("rt_", [128, 64], F32, 4)
        self.r_t1 = self.ring("rt1_", [128, 256], F32, 2)
        self.r_t2 = self.ring("rt2_", [128, 256], F32, 2)
        self.r_qkr = self.ring("qkr_", [128, 256], F32, 3)
        self.r_qkb = self.ring("qkb_", [128, 256], BF16, 3)
        self.r_v32 = self.ring("v32_", [128, 128], F32, 2)
        self.r_P = self.ring("P_", [128, 512], BF16, 4)
        self.amaskf = self.sb("amaskf", [128, 512], F32)
        self.amask = self.sb("amask", [128, 512], BF16)
        self.onesf = self.sb("onesf", [128, 64], F32)
        self.r_rden = self.ring("rden_", [64, 512], F32, 2)
        self.r_on = self.ring("on_", [64, 512], BF16, 3)
        self.r_qs = self.ring("qs_", [NS, 384], F32, 2)
        if "oT_d" not in io:
            self.dscr("oT_d", [8, 128, T], BF16)
            self.dscr("qkvs_d", [NS, 9, D], F32)
        S.dma('sp', out=self.amaskf[:], in_=io["c_amask"], writes=['amaskf'])
        S.op('dve', 'tensor_copy', self.amask[:], self.amaskf[:], reads=['amaskf'], writes=['amask'])
        S.op('dve', 'memset', self.onesf[:], 1.0, writes=['onesf'])
        for i in range(2):
            S.op('dve', 'memset', self.r_v.t[i][:], 1.0, writes=[((self.r_v.name, i), 'ones')])

    def tile_geom(self, g, ct):
        W, d = GROUPS[g]
        nbn = self.NT // d
        r, nb = ct // nbn, ct % nbn
        base = nb * 128 * d + r
        Weff = min(W, self.T)
        tail = (nb * 128 * d) >= self.T - Weff
        return d, r, nb, base, Weff, tail

    def attn_prod_tile(self, layer, g, hp, ct, w, wr, qk, qkr_, v, vr):
        S, io, T = self.S, self.io, self.T
        d, r, nb, base, Weff, tail = self.tile_geom(g, ct)
        W = GROUPS[g][0]
        ps, pr = self.ps()
        pv, pvr = self.ps()
        toks = list(range(base // 128, (base + 127 * d) // 128 + 1))
        for k in range(8):
            S.op('pe', 'matmul', ps[:, 0:256], lhsT=sb_ap(self.xT, 8 * T, k * T + base, [[d, 128]]), rhs=w[:, k, 0:256],
                 start=(k == 0), stop=(k == 7), reads=list(wr) + [('xT', t) for t in toks], writes=[pr], signal=(k == 7))
        for k in range(8):
            S.op('pe', 'matmul', pv[:, 0:128], lhsT=sb_ap(self.xT, 8 * T, k * T + base, [[d, 128]]), rhs=w[:, k, 256:384],
                 start=(k == 0), stop=(k == 7), reads=list(wr) + [('xT', t) for t in toks], writes=[pvr], signal=(k == 7))
        dbg = getattr(self, 'dbg', 9)
        if dbg < 1.1:
            return
        rt, rtr = self.r_rt.next()
        S.dma('sp', out=rt[:], in_=io["c_rope"][g, ct], writes=[rtr])
        t1, t1r = self.r_t1.next()
        t2, t2r = self.r_t2.next()
        qkr, qkrr = self.r_qkr.next()
        cosb = sb_ap(rt, 64, 0, [[0, 8], [1, 32]])
        sinb = sb_ap(rt, 64, 32, [[0, 4], [1, 32]])
        S.op('dve', 'tensor_tensor', out=sb_ap(t1, 256, 0, [[32, 8], [1, 32]]), in0=sb_ap(ps, 512, 0, [[32, 8], [1, 32]]),
             in1=cosb, op=ALU.mult, reads=[pr, rtr], writes=[t1r])
        S.op('dve', 'scalar_tensor_tensor', out=sb_ap(t2, 256, 0, [[64, 4], [1, 32]]), in0=sb_ap(ps, 512, 32, [[64, 4], [1, 32]]),
             scalar=-1.0, in1=sinb, op0=ALU.mult, op1=ALU.mult, reads=[pr, rtr], writes=[(t2r, 0)])
        S.op('dve', 'tensor_tensor', out=sb_ap(t2, 256, 32, [[64, 4], [1, 32]]), in0=sb_ap(ps, 512, 0, [[64, 4], [1, 32]]),
             in1=sinb, op=ALU.mult, reads=[pr, rtr], writes=[(t2r, 1)])
        qkb, qkbr = self.r_qkb.next()
        S.op('dve', 'tensor_tensor', out=qkb[:], in0=t1[:], in1=t2[:], op=ALU.add, reads=[t1r, (t2r, 0), (t2r, 1)], writes=[qkbr])
        if tail:
            S.op('dve', 'tensor_tensor', out=qkr[:, 128:256], in0=t1[:, 128:256], in1=t2[:, 128:256], op=ALU.add,
                 reads=[t1r, (t2r, 0), (t2r, 1)], writes=[qkrr])
        if dbg < 1.2:
            return
        S.op('act', 'copy', sb_ap(v, self.NT * 130, ct * 130, [[65, 2], [1, 64]]), sb_ap(pv, 512, 0, [[64, 2], [1, 64]]),
             reads=[pvr, (vr, 'ones')], writes=[(vr, ct)])
        if tail and dbg >= 1.3:
            wname = {128: "kv128_p", 512: "kv512_p", 2048: "kv2048_p"}[W]
            row0 = base - (T - Weff)
            dst = io[wname][layer // 2].rearrange("(m s) c -> m s c", s=d)
            S.dma('sp', out=dst[row0 // d:row0 // d + 128, row0 % d, hp * 128:(hp + 1) * 128], in_=qkr[:, 128:256], reads=[qkrr],
                  writes=[(wname, layer, hp, ct, 'k')])
            v32, v32r = self.r_v32.next()
            S.op('act', 'copy', v32[:], pv[:, 0:128], reads=[pvr], writes=[v32r])
            S.dma('sp', out=dst[row0 // d:row0 // d + 128, row0 % d, 1024 + hp * 128:1024 + (hp + 1) * 128], in_=v32[:], reads=[v32r],
                  writes=[(wname, layer, hp, ct, 'v')])
        if dbg < 1.4:
            return
        pt, ptr_ = self.ps()
        ptb = pt.bitcast(BF16)
        if dbg == 1.41:
            S.op('pe', 'transpose', ptb[:, 0:128], self.identb[:], self.identb[:], reads=['identb'], writes=[ptr_], signal=False)
            S.op('pe', 'transpose', ptb[:, 128:256], self.identb[:], self.identb[:], reads=['identb'], writes=[ptr_])
            return
        if dbg == 1.42:
            S.op('pe', 'transpose', ptb[:, 0:128], qkb[:, 0:128], self.identb[:], reads=[qkbr, 'identb'], writes=[ptr_])
            return
        S.op('pe', 'transpose', ptb[:, 0:128], qkb[:, 0:128], self.identb[:], reads=[qkbr, 'identb'], writes=[ptr_], signal=False)
        S.op('pe', 'transpose', ptb[:, 128:256], qkb[:, 128:256], self.identb[:], reads=[qkbr, 'identb'], writes=[ptr_])
        if dbg < 1.5:
            return
        S.op('act', 'copy', sb_ap(qk, 2 * T, ct * 128, [[T, 2], [1, 128]]), sb_ap(ptb, 1024, 0, [[128, 2], [1, 128]]),
             reads=[ptr_], writes=[(qkr_, ct)])

    def attn_s_block(self, g, ct, qk, qkr_):
        S, T = self.S, self.T
        d, r, nb, base, Weff, tail = self.tile_geom(g, ct)
        P, Pr = self.r_P.next()
        for a in range(2):
            ps, pr = self.ps()
            js = [1] if nb == 0 else [0, 1]
            for n, j in enumerate(js):
                kt = ct - 1 if j == 0 else ct
                S.op('pe', 'matmul', ps[:, j * 128:(j + 1) * 128],
                     lhsT=sb_ap(qk, 2 * T, T + kt * 128, [[1, 128]], p0=a * 64, np_=64),
                     rhs=sb_ap(qk, 2 * T, ct * 128, [[1, 128]], p0=a * 64, np_=64), start=True, stop=True,
                     reads=[(qkr_, kt), (qkr_, ct)], writes=[pr], signal=(n == len(js) - 1))
            c0 = 128 if nb == 0 else 0
            S.op('act', 'activation', out=P[:, a * 256 + c0:(a + 1) * 256], in_=ps[:, c0:256], func=AF.Exp, scale=SCALE,
                 reads=[pr], writes=[(Pr, a)])
            S.op('dve', 'tensor_tensor', out=P[:, a * 256 + c0:(a + 1) * 256], in0=P[:, a * 256 + c0:(a + 1) * 256],
                 in1=self.amask[:, a * 256 + c0:(a + 1) * 256], op=ALU.mult, reads=[(Pr, a), 'amask'], writes=[(Pr, a)])
        return P, Pr

    def attn_pv_block(self, g, ct, P, Pr, v, vr):
        S, T = self.S, self.T
        d, r, nb, base, Weff, tail = self.tile_geom(g, ct)
        ps, pr = self.ps()
        for a in range(2):
            js = [1] if nb == 0 else [0, 1]
            for n, j in enumerate(js):
                kt = ct - 1 if j == 0 else ct
                S.op('pe', 'matmul', ps[0:65, a * 128:(a + 1) * 128], lhsT=sb_ap(v, self.NT * 130, kt * 130 + a * 65, [[1, 65]]),
                     rhs=P[:, (a * 2 + j) * 128:(a * 2 + j + 1) * 128], start=(n == 0), stop=(n == len(js) - 1),
                     reads=[(vr, kt), (Pr, a)], writes=[pr], signal=(a == 1 and n == len(js) - 1))
        dst = sb_ap(self.aacc, 2 * T, base, [[T, 2], [d, 128]], np_=65)
        src = sb_ap(ps, 512, 0, [[128, 2], [1, 128]], np_=65)
        accres = [('aacc', t) for t in range(base // 128, (base + 127 * d) // 128 + 1)]
        if g == 0:
            S.op('dve', 'tensor_copy', dst, src, reads=[pr], writes=accres)
        else:
            S.op('dve', 'tensor_tensor', out=dst, in0=dst, in1=src, op=ALU.add, reads=[pr] + accres, writes=accres)

    def attn_norm_hp(self, hp):
        S, io, T = self.S, self.io, self.T
        for a in range(2):
            for c in range(T // 512):
                ps, pr = self.ps()
                accres = [('aacc', c * 4 + j) for j in range(4)]
                S.op('pe', 'matmul', ps[0:64, 0:512], lhsT=self.onesf[64:65, 0:64],
                     rhs=sb_ap(self.aacc, 2 * T, a * T + c * 512, [[1, 512]], p0=64, np_=1), start=True, stop=True,
                     reads=['onesf'] + accres, writes=[pr])
                rden, rdr = self.r_rden.next()
                S.op('dve', 'reciprocal', rden[:], ps[0:64, 0:512], reads=[pr], writes=[rdr])
                on, onr = self.r_on.next()
                S.op('dve', 'tensor_tensor', out=on[:], in0=sb_ap(self.aacc, 2 * T, a * T + c * 512, [[1, 512]], np_=64), in1=rden[:],
                     op=ALU.mult, reads=[rdr] + accres, writes=[onr])
                S.dma('sp', out=io["oT_d"][hp, a * 64:(a + 1) * 64, c * 512:(c + 1) * 512], in_=on[:], reads=[onr], writes=[('oT_d', hp, a, c)])

    def phase_attn(self, layer, first=False):
        S, io, T, NT = self.S, self.io, self.T, self.NT
        j = layer // 2
        self.phase_begin()
        self.attn_alloc()
        wind = io["attn_w_in"][j]
        steps = [(hp, g) for hp in range(8) for g in range(3)]
        import os
        if os.environ.get("NSTEPS"):
            steps = steps[:int(os.environ["NSTEPS"])]
        wl = {}

        def issue_w(si):
            hp, g = steps[si]
            w, wr = self.r_aw.next()
            for x in range(3):
                c0 = g * 3072 + x * 1024 + hp * 128
                S.dma('pool', out=w[:, :, x * 128:(x + 1) * 128], in_=wind[:, c0:c0 + 128].rearrange("(k q) c -> q k c", q=128),
                      writes=[(wr, x)])
            wl[si] = (w, [(wr, 0), (wr, 1), (wr, 2)])

        issue_w(0)
        if len(steps) > 1:
            issue_w(1)
        bufs = {}
        LAG = 2
        for si in range(len(steps) + 1):
            if si + 2 < len(steps):
                issue_w(si + 2)
            if si < len(steps):
                hp, g = steps[si]
                w, wres = wl.pop(si)
                qk, qkr_ = self.r_qk.next()
                v, vr = self.r_v.next()
                bufs[si] = (qk, qkr_, v, vr)
                ps, pr = self.ps()
                for k in range(8):
                    S.op('pe', 'matmul', ps[0:NS, 0:384], lhsT=self.xsT[:, k, :], rhs=w[:, k, :], start=(k == 0), stop=(k == 7),
                         reads=wres + ['xsT'], writes=[pr], signal=(k == 7))
                qs, qsr = self.r_qs.next()
                S.op('act', 'copy', qs[:], ps[0:NS, 0:384], reads=[pr], writes=[qsr])
                S.dma('sp', out=io["qkvs_d"][:, g * 3:(g + 1) * 3, hp * 128:(hp + 1) * 128], in_=qs[:].rearrange("p (x c) -> p x c", x=3),
                      reads=[qsr], writes=[('qkvs_d', g, hp)])
            pend = []
            for i in range(NT + LAG):
                if si < len(steps) and i < NT:
                    self.attn_prod_tile(layer, g, hp, i, w, wres, qk, qkr_, v, vr)
                if si >= 1:
                    php, pg = steps[si - 1]
                    pqk, pqkr, pv, pvr = bufs[si - 1]
                    if getattr(self, 'dbg', 9) < 2:
                        continue
                    if i < NT:
                        pend.append((i,) + self.attn_s_block(pg, i, pqk, pqkr))
                    if i >= LAG:
                        ct, P, Pr = pend.pop(0)
                        self.attn_pv_block(pg, ct, P, Pr, pv, pvr)
            if si >= 1:
                php, pg = steps[si - 1]
                del bufs[si - 1]
                if pg == 2 and getattr(self, 'dbg', 9) >= 3:
                    self.attn_norm_hp(php)
        self.phase_end()
        self.phase_begin()
        self.ln_alloc()
        self.load_ln_params(layer, 'mix')
        self.wo = self.sb("wo", [128, 8, D], BF16)
        self.r_oTt = self.ring("oTt_", [128, 8, 128], BF16, 3)
        S.dma('pool', out=self.wo[:], in_=io["attn_w_out"][j].rearrange("(k q) n -> q k n", q=128), writes=['wo'])
        oTd = io["oT_d"]
        for tt in range(NT if getattr(self, 'dbg', 9) >= 4 else 0):
            oTt, oTr = self.r_oTt.next()
            S.dma('sp', out=oTt[:], in_=oTd[:, :, tt * 128:(tt + 1) * 128].rearrange("hp q t -> q hp t"),
                  reads=[('oT_d', hp, a, tt // 4) for hp in range(8) for a in range(2)], writes=[oTr])
            hps = []
            for hf in range(2):
                ps, pr = self.ps()
                for k in range(8):
                    S.op('pe', 'matmul', ps[:, 0:512], lhsT=oTt[:, k, :], rhs=self.wo[:, k, hf * 512:(hf + 1) * 512],
                         start=(k == 0), stop=(k == 7), reads=[oTr, 'wo'], writes=[pr], signal=(k == 7))
                hps.append((ps, pr))
            self.ln_prompt_tile(tt, [hps[0][0][:, 0:512], hps[1][0][:, 0:512]], [hps[0][1], hps[1][1]], 'mix', first=first)
        if hasattr(self, 'sample_attn') and getattr(self, 'do_sample', True):
            self.sample_attn(layer, self.wo)
        self.phase_end()


class KBSsd(KBAttn):
    def phase_ssd(self, layer, first=False):
        S, io, T, NT = self.S, self.io, self.T, self.NT
        j = layer // 2
        win = io["ssm_w_in"][j]
        if "yn_d" not in io:
            self.dscr("yn_d", [T, 2048], BF16)
            self.dscr("sproj_d", [NS, 6176], F32)
        self.phase_begin()
        tri = self.sb("tri", [128, 128], F32)
        ust = self.sb("ust", [128, 128], F32)
        ones = self.sb("ones", [128, 128], F32)
        S.dma('sp', out=tri[:], in_=io["c_tri"], writes=['tri'])
        S.dma('sp', out=ust[:], in_=io["c_ustrict"], writes=['ust'])
        S.op('dve', 'memset', ones[:], 1.0, writes=['ones'])
        dtt = self.sb("dtt", [128, NT, 32], F32)
        dta = self.sb("dta", [128, NT, 32], F32)
        ea = self.sb("ea", [128, NT, 32], F32)
        el = self.sb("el", [128, NT, 32], F32)
        dte = self.sb("dte", [128, NT, 32], F32)
        hp_ = self.sb("hpar", [128, 4, 32], F32)
        cwT = self.sb("cwT", [128, 32, 5], F32)
        self.dts = self.sb("dts", [NS, 64], F32)
        for i, nm in enumerate(["ssm_dt_bias", "ssm_a_log", "ssm_d"]):
            S.dma('sp', out=hp_[:, i, :], in_=io[nm][j:j + 1, :].partition_broadcast(128), writes=[('hpar', i)])
        S.op('act', 'activation', out=hp_[:, 1, :], in_=hp_[:, 1, :], func=AF.Exp, reads=[('hpar', 1)], writes=[('hpar', 1)])
        S.op('dve', 'tensor_scalar', out=hp_[:, 1, :], in0=hp_[:, 1, :], scalar1=-1.0, scalar2=None, op0=ALU.mult,
             reads=[('hpar', 1)], writes=[('hpar', 1)])
        wdt = self.sb("wdt", [128, 8, 32], BF16)
        tmpa = self.ring("tmpa", [128, 512], F32, 2)
        self.pes2 = contextlib.ExitStack()
        cw = self.pes2.enter_context(self.nc.sbuf_tensor(self._uniq("cw"), [5, 4096], F32))
        S.dma('sp', out=cw[0:4, :], in_=io["ssm_conv_w"][j], writes=[('cw', 0)])
        S.dma('sp', out=cw[4:5, :], in_=io["ssm_conv_b"][j:j + 1, :], writes=[('cw', 1)])
        for q in range(8):
            ps, pr = self.ps()
            for c in range(4):
                ch = q * 4 + c
                S.op('pe', 'matmul', ps[:, c * 8:c * 8 + 5], lhsT=cw[0:5, ch * 128:(ch + 1) * 128], rhs=self.identf[0:5, 0:5],
                     start=True, stop=True, reads=[('cw', 0), ('cw', 1), 'identf'], writes=[pr], signal=(c == 3))
            S.op('act', 'copy', cwT[:, q * 4:(q + 1) * 4, :], sb_ap(ps, 512, 0, [[8, 4], [1, 5]]), reads=[pr], writes=[('cwT', q)])
        S.dma('pool', out=wdt[:], in_=win[:, 6144:6176].rearrange("(k q) c -> q k c", q=128), writes=['wdt'])
        for b16 in range((NT + 15) // 16):
            nt_ = min(16, NT - b16 * 16)
            ps, pr = self.ps()
            for t in range(nt_):
                tt = b16 * 16 + t
                for k in range(8):
                    S.op('pe', 'matmul', ps[:, t * 32:(t + 1) * 32], lhsT=self.xT[:, k, tt * 128:(tt + 1) * 128], rhs=wdt[:, k, :],
                         start=(k == 0), stop=(k == 7), reads=['wdt', ('xT', tt)], writes=[pr], signal=(t == nt_ - 1 and k == 7))
            sl = slice(b16 * 16, b16 * 16 + nt_)
            S.op('dve', 'tensor_tensor', out=dtt[:, sl, :], in0=sb_ap(ps, 512, 0, [[32, nt_], [1, 32]]),
                 in1=sb_ap(hp_, 128, 0, [[0, nt_], [1, 32]]), op=ALU.add, reads=[pr, ('hpar', 0)], writes=[('dtt', b16)])
            S.op('act', 'activation', out=dtt[:, sl, :], in_=dtt[:, sl, :], func=AF.Exp, reads=[('dtt', b16)], writes=[('dtt', b16)])
            S.op('dve', 'tensor_scalar', out=dtt[:, sl, :], in0=dtt[:, sl, :], scalar1=1.0, scalar2=None, op0=ALU.add,
                 reads=[('dtt', b16)], writes=[('dtt', b16)])
            S.op('act', 'activation', out=dtt[:, sl, :], in_=dtt[:, sl, :], func=AF.Ln, reads=[('dtt', b16)], writes=[('dtt', b16)])
            S.op('dve', 'tensor_tensor', out=dta[:, sl, :], in0=dtt[:, sl, :], in1=sb_ap(hp_, 128, 32, [[0, nt_], [1, 32]]), op=ALU.mult,
                 reads=[('dtt', b16), ('hpar', 1)], writes=[('dta', b16)])
        ps, pr = self.ps()
        for k in range(8):
            S.op('pe', 'matmul', ps[0:NS, 0:32], lhsT=self.xsT[:, k, :], rhs=wdt[:, k, :], start=(k == 0), stop=(k == 7),
                 reads=['wdt', 'xsT'], writes=[pr], signal=(k == 7))
        S.op('act', 'copy', self.dts[0:NS, 0:32], ps[0:NS, 0:32], reads=[pr], writes=['dts'])
        S.dma('sp', out=io["sproj_d"][:, 6144:6176], in_=self.dts[0:NS, 0:32], reads=['dts'], writes=[('sproj_d', 'dt')])
        for b8 in range((NT + 7) // 8):
            nt_ = min(8, NT - b8 * 8)
            ps, pr = self.ps()
            for t in range(nt_):
                tt = b8 * 8 + t
                S.op('pe', 'matmul', ps[:, t * 64:t * 64 + 32], lhsT=tri[:], rhs=dta[:, tt, :], start=True, stop=True,
                     reads=['tri', ('dta', tt // 16)], writes=[pr], signal=False)
                S.op('pe', 'matmul', ps[:, t * 64 + 32:t * 64 + 64], lhsT=ones[:], rhs=dta[:, tt, :], start=True, stop=True,
                     reads=['ones', ('dta', tt // 16)], writes=[pr], signal=(t == nt_ - 1))
            ta, tar = tmpa.next()
            S.op('act', 'copy', ta[:, 0:nt_ * 64], ps[:, 0:nt_ * 64], reads=[pr], writes=[tar])
            sl = slice(b8 * 8, b8 * 8 + nt_)
            cum = sb_ap(ta, 512, 0, [[64, nt_], [1, 32]])
            tot = sb_ap(ta, 512, 32, [[64, nt_], [1, 32]])
            S.op('act', 'activation', out=ea[:, sl, :], in_=cum, func=AF.Exp, reads=[tar], writes=[('ea', b8)])
            S.op('act', 'activation', out=el[:, sl, :], in_=tot, func=AF.Exp, reads=[tar], writes=[('el', b8)])
            S.op('dve', 'tensor_tensor', out=dte[:, sl, :], in0=tot, in1=cum, op=ALU.subtract, reads=[tar], writes=[('dte', b8)])
            S.op('act', 'activation', out=dte[:, sl, :], in_=dte[:, sl, :], func=AF.Exp, reads=[('dte', b8)], writes=[('dte', b8)])
            S.op('dve', 'tensor_tensor', out=dte[:, sl, :], in0=dte[:, sl, :], in1=dtt[:, sl, :], op=ALU.mult,
                 reads=[('dte', b8), ('dtt', b8 // 2)], writes=[('dte', b8)])
        S.barrier()
        self.pes2.close()
        r_w = self.ring("sw_", [128, 8, 768], BF16, 2)
        pre = [self.sb("pre%d" % c, [128, 515], F32) for c in range(4)]
        r_cacc = self.ring("cacc_", [128, 512], F32, 2)
        r_xc = [self.ring("xc%d_" % c, [128, 512], BF16, 2) for c in range(4)]
        r_xtok = self.ring("xtok_", [128, 384], BF16, 3)
        r_sz = self.ring("sz_", [128, 256], F32, 2)
        hst = self.sb("hst", [128, 256], F32)
        hbf = self.sb("hbf", [128, 256], BF16)
        r_R = self.ring("R_", [128, 512], F32, 2)
        r_E = self.ring("E_", [128, 512], BF16, 2)
        r_MT = self.ring("MT_", [128, 512], BF16, 2)
        r_CB = self.ring("CB_", [128, 128], BF16, 2)
        r_xdt = self.ring("xdt_", [128, 512], BF16, 2)
        r_y = self.ring("y_", [128, 512], F32, 2)
        r_yb = self.ring("yb_", [128, 256], BF16, 3)
        r_sm = self.ring("sm_", [128, 8], F32, 3)
        gpar = self.ring("gpar_", [128, 512], F32, 2)
        r_fs = self.ring("fs_", [128, 128], F32, 2)
        r_cp = self.ring("cp_", [3, 512], F32, 2)
        nblk = T // 512

        def issue_w(g):
            w, wr = r_w.next()
            for x, (c0, n) in enumerate([(g * 256, 256), (2048 + g * 256, 256), (4096 + g * 128, 128), (5120 + g * 128, 128)]):
                o0 = [0, 256, 512, 640][x]
                S.dma('pool', out=w[:, :, o0:o0 + n], in_=win[:, c0:c0 + n].rearrange("(k q) c -> q k c", q=128), writes=[(wr, x)])
            return w, [(wr, x) for x in range(4)]

        wnext = issue_w(0)
        for g in range(8):
            w, wres = wnext
            if g + 1 < 8:
                wnext = issue_w(g + 1)
            gp, gpr = gpar.next()
            S.dma('sp', out=gp[:, 0:256], in_=io["ssm_norm_w"][j:j + 1, g * 256:(g + 1) * 256].partition_broadcast(128), writes=[gpr])
            S.op('dve', 'memset', hst[:], 0.0, writes=['hst'])
            S.op('dve', 'memset', hbf[:], 0.0, writes=['hbf'])
            for c in range(4):
                S.op('dve', 'memset', pre[c][:, 0:3], 0.0, writes=[('pre', c)])
            for (c0, n, d0) in [(0, 512, None), (512, 256, None)]:
                ps, pr = self.ps()
                for k in range(8):
                    S.op('pe', 'matmul', ps[0:NS, 0:n], lhsT=self.xsT[:, k, :], rhs=w[:, k, c0:c0 + n], start=(k == 0), stop=(k == 7),
                         reads=wres + ['xsT'], writes=[pr], signal=(k == 7))
                sz, szr = r_sz.next()
                if c0 == 0:
                    S.op('act', 'copy', sz[0:NS, 0:256], ps[0:NS, 0:256], reads=[pr], writes=[szr])
                    S.dma('sp', out=io["sproj_d"][:, g * 256:(g + 1) * 256], in_=sz[0:NS, 0:256], reads=[szr], writes=[('sproj_d', g, 'z')])
                    sz2, sz2r = r_sz.next()
                    S.op('act', 'copy', sz2[0:NS, 0:256], ps[0:NS, 256:512], reads=[pr], writes=[sz2r])
                    S.dma('sp', out=io["sproj_d"][:, 2048 + g * 256:2048 + (g + 1) * 256], in_=sz2[0:NS, 0:256], reads=[sz2r],
                          writes=[('sproj_d', g, 'x')])
                else:
                    S.op('act', 'copy', sz[0:NS, 0:256], ps[0:NS, 0:256], reads=[pr], writes=[szr])
                    S.dma('sp', out=io["sproj_d"][:, 4096 + g * 128:4096 + (g + 1) * 128], in_=sz[0:NS, 0:128], reads=[szr],
                          writes=[('sproj_d', g, 'B')])
                    S.dma('sp', out=io["sproj_d"][:, 5120 + g * 128:5120 + (g + 1) * 128], in_=sz[0:NS, 128:256], reads=[szr],
                          writes=[('sproj_d', g, 'C')])
            for tb in range(nblk):
                tok0 = tb * 512
                xts = [('xT', tb * 4 + i) for i in range(4)]
                xcs = []
                for c in range(4):
                    ch = [2 * g, 2 * g + 1, 16 + g, 24 + g][c]
                    ps, pr = self.ps()
                    for k in range(8):
                        S.op('pe', 'matmul', ps[:, 0:512], lhsT=w[:, k, 256 + c * 128:256 + (c + 1) * 128], rhs=self.xT[:, k, tok0:tok0 + 512],
                             start=(k == 0), stop=(k == 7), reads=wres + xts, writes=[pr], signal=(k == 7))
                    if tb > 0:
                        S.op('dve', 'tensor_copy', pre[c][:, 0:3], pre[c][:, 512:515], reads=[('pre', c)], writes=[('pre', c)])
                    S.op('act', 'copy', pre[c][:, 3:515], ps[:, 0:512], reads=[pr], writes=[('pre', c)])
                    ca, car = r_cacc.next()
                    S.op('dve', 'tensor_scalar', out=ca[:], in0=pre[c][:, 0:512], scalar1=cwT[:, ch, 0:1], scalar2=None, op0=ALU.mult,
                         reads=[('pre', c), ('cwT', ch // 4)], writes=[car])
                    for tap in range(1, 4):
                        S.op('dve', 'scalar_tensor_tensor', out=ca[:], in0=pre[c][:, tap:tap + 512], scalar=cwT[:, ch, tap:tap + 1], in1=ca[:],
                             op0=ALU.mult, op1=ALU.add, reads=[('pre', c), ('cwT', ch // 4), car], writes=[car])
                    xc, xcr = r_xc[c].next()
                    S.op('act', 'activation', out=xc[:], in_=ca[:], func=AF.Silu, bias=cwT[:, ch, 4:5], reads=[car, ('cwT', ch // 4)], writes=[xcr])
                    xcs.append((xc, xcr))
                    if tb == nblk - 1:
                        pc, pcr = self.ps()
                        S.op('pe', 'matmul', pc[0:3, 0:128], lhsT=pre[c][:, 512:515], rhs=self.identf[:], start=True, stop=True,
                             reads=[('pre', c), 'identf'], writes=[pcr])
                        cp, cpr = r_cp.next()
                        S.op('act', 'copy', cp[0:3, 0:128], pc[0:3, 0:128], reads=[pcr], writes=[cpr])
                        S.dma('sp', out=io["conv_p"][j, :, ch * 128:(ch + 1) * 128], in_=cp[0:3, 0:128], reads=[cpr], writes=[('conv_p', j, ch)])
                BT, BTr = xcs[2]
                CT, CTr = xcs[3]
                for c4 in range(4):
                    tt = tb * 4 + c4
                    cs = slice(c4 * 128, (c4 + 1) * 128)
                    hs = slice(g * 4, g * 4 + 4)
                    pt, ptr_ = self.ps()
                    ptb = pt.bitcast(BF16)
                    for i, (src, srcr) in enumerate([xcs[0], xcs[1], xcs[2]]):
                        S.op('pe', 'transpose', ptb[:, i * 128:(i + 1) * 128], src[:, cs], self.identb[:], reads=[srcr, 'identb'],
                             writes=[ptr_], signal=(i == 2))
                    xtok, xtr = r_xtok.next()
                    S.op('act', 'copy', xtok[:], ptb[:, 0:384], reads=[ptr_], writes=[xtr])
                    pz, pzr = self.ps()
                    for k in range(8):
                        S.op('pe', 'matmul', pz[:, 0:256], lhsT=self.xT[:, k, tt * 128:(tt + 1) * 128], rhs=w[:, k, 0:256],
                             start=(k == 0), stop=(k == 7), reads=wres + [('xT', tt)], writes=[pzr], signal=(k == 7))
                    sz, szr = r_sz.next()
                    S.op('act', 'activation', out=sz[:], in_=pz[:, 0:256], func=AF.Silu, reads=[pzr], writes=[szr])
                    pcb, pcbr = self.ps()
                    S.op('pe', 'matmul', pcb[:, 0:128], lhsT=BT[:, cs], rhs=CT[:, cs], start=True, stop=True, reads=[BTr, CTr], writes=[pcbr])
                    CB, CBr = r_CB.next()
                    S.op('dve', 'tensor_tensor', out=CB[:], in0=pcb[:, 0:128], in1=tri[:], op=ALU.mult, reads=[pcbr, 'tri'], writes=[CBr])
                    R, Rr = r_R.next()
                    S.op('dve', 'tensor_tensor', out=sb_ap(R, 512, 0, [[128, 4], [1, 128]]), in0=sb_ap(tri, 128, 0, [[0, 4], [1, 128]]),
                         in1=sb_ap(dta, NT * 32, tt * 32 + g * 4, [[1, 4], [0, 128]]), op=ALU.mult, reads=['tri', ('dta', tt // 16)], writes=[Rr])
                    pd, pdr = self.ps()
                    S.op('pe', 'matmul', pd[:, 0:512], lhsT=ust[:], rhs=R[:], start=True, stop=True, reads=['ust', Rr], writes=[pdr])
                    E, Er = r_E.next()
                    S.op('act', 'activation', out=E[:], in_=pd[:, 0:512], func=AF.Exp, reads=[pdr], writes=[Er])
                    MT, MTr = r_MT.next()
                    S.op('dve', 'tensor_tensor', out=sb_ap(MT, 512, 0, [[128, 4], [1, 128]]), in0=sb_ap(E, 512, 0, [[128, 4], [1, 128]]),
                         in1=sb_ap(CB, 128, 0, [[0, 4], [1, 128]]), op=ALU.mult, reads=[Er, CBr], writes=[MTr])
                    xdt, xdr = r_xdt.next()
                    S.op('dve', 'tensor_tensor', out=sb_ap(xdt, 512, 0, [[64, 4], [1, 64]]), in0=sb_ap(xtok, 384, 0, [[64, 4], [1, 64]]),
                         in1=sb_ap(dtt, NT * 32, tt * 32 + g * 4, [[1, 4], [0, 64]]), op=ALU.mult, reads=[xtr, ('dtt', tt // 16)], writes=[(xdr, 0)])
                    S.op('dve', 'tensor_tensor', out=sb_ap(xdt, 512, 256, [[64, 4], [1, 64]]), in0=sb_ap(xtok, 384, 0, [[64, 4], [1, 64]]),
                         in1=sb_ap(dte, NT * 32, tt * 32 + g * 4, [[1, 4], [0, 64]]), op=ALU.mult, reads=[xtr, ('dte', tt // 8)], writes=[(xdr, 1)])
                    py, pyr = self.ps()
                    for h in range(4):
                        S.op('pe', 'matmul', py[:, h * 64:(h + 1) * 64], lhsT=MT[:, h * 128:(h + 1) * 128], rhs=xdt[:, h * 64:(h + 1) * 64],
                             start=True, stop=True, reads=[MTr, (xdr, 0)], writes=[pyr], signal=False)
                    S.op('pe', 'matmul', py[:, 256:512], lhsT=CT[:, cs], rhs=hbf[:], start=True, stop=True, reads=[CTr, 'hbf'], writes=[pyr])
                    y, yr = r_y.next()
                    S.op('dve', 'tensor_tensor', out=sb_ap(y, 512, 0, [[64, 4], [1, 64]]), in0=sb_ap(py, 512, 256, [[64, 4], [1, 64]]),
                         in1=sb_ap(ea, NT * 32, tt * 32 + g * 4, [[1, 4], [0, 64]]), op=ALU.mult, reads=[pyr, ('ea', tt // 8)], writes=[yr])
                    S.op('dve', 'tensor_tensor', out=y[:, 0:256], in0=y[:, 0:256], in1=py[:, 0:256], op=ALU.add, reads=[yr, pyr], writes=[yr])
                    S.op('dve', 'tensor_tensor', out=sb_ap(y, 512, 256, [[64, 4], [1, 64]]), in0=sb_ap(xtok, 384, 0, [[64, 4], [1, 64]]),
                         in1=sb_ap(hp_, 128, 64 + g * 4, [[1, 4], [0, 64]]), op=ALU.mult, reads=[xtr, ('hpar', 2)], writes=[(yr, 'd')])
                    S.op('dve', 'tensor_tensor', out=y[:, 0:256], in0=y[:, 0:256], in1=y[:, 256:512], op=ALU.add, reads=[yr, (yr, 'd')], writes=[yr])
                    S.op('dve', 'tensor_tensor', out=y[:, 0:256], in0=y[:, 0:256], in1=sz[:], op=ALU.mult, reads=[yr, szr], writes=[yr])
                    sm, smr = r_sm.next()
                    S.op('dve', 'tensor_tensor', out=y[:, 256:512], in0=y[:, 0:256], in1=y[:, 0:256], op=ALU.mult,
                         reads=[yr, (yr, 'd')], writes=[(yr, 'd')])
                    S.op('dve', 'tensor_reduce', out=sm[:, 0:1], in_=y[:, 256:512], axis=AX.X, op=ALU.add, reads=[(yr, 'd')], writes=[smr])
                    S.op('dve', 'tensor_scalar', out=sm[:, 1:2], in0=sm[:, 0:1], scalar1=1.0 / 256, scalar2=RMS_EPS, op0=ALU.mult, op1=ALU.add,
                         reads=[smr], writes=[smr])
                    S.op('act', 'activation', out=sm[:, 1:2], in_=sm[:, 1:2], func=AF.Ln, reads=[smr], writes=[smr])
                    S.op('act', 'activation', out=sm[:, 2:3], in_=sm[:, 1:2], func=AF.Exp, scale=-0.5, reads=[smr], writes=[smr])
                    yb, ybr = r_yb.next()
                    S.op('dve', 'scalar_tensor_tensor', out=yb[:], in0=y[:, 0:256], scalar=sm[:, 2:3], in1=gp[:, 0:256], op0=ALU.mult, op1=ALU.mult,
                         reads=[yr, smr, gpr], writes=[ybr])
                    S.dma('sp', out=io["yn_d"][tt * 128:(tt + 1) * 128, g * 256:(g + 1) * 256], in_=yb[:], reads=[ybr], writes=[('yn_d', tt, g)])
                    pst, pstr = self.ps()
                    S.op('pe', 'matmul', pst[:, 0:256], lhsT=xtok[:, 256:384], rhs=xdt[:, 256:512], start=True, stop=True,
                         reads=[xtr, (xdr, 1)], writes=[pstr])
                    S.op('dve', 'tensor_tensor', out=sb_ap(hst, 256, 0, [[64, 4], [1, 64]]), in0=sb_ap(hst, 256, 0, [[64, 4], [1, 64]]),
                         in1=sb_ap(el, NT * 32, tt * 32 + g * 4, [[1, 4], [0, 64]]), op=ALU.mult, reads=['hst', ('el', tt // 8)], writes=['hst'])
                    S.op('dve', 'tensor_tensor', out=hst[:], in0=hst[:], in1=pst[:, 0:256], op=ALU.add, reads=['hst', pstr], writes=['hst'])
                    S.op('act', 'copy', hbf[:], hst[:], reads=['hst'], writes=['hbf'])
            for half in range(2):
                pf, pfr = self.ps()
                S.op('pe', 'matmul', pf[:, 0:128], lhsT=hst[:, half * 128:(half + 1) * 128], rhs=self.identf[:], start=True, stop=True,
                     reads=['hst', 'identf'], writes=[pfr])
                fs, fsr = r_fs.next()
                S.op('act', 'copy', fs[:], pf[:, 0:128], reads=[pfr], writes=[fsr])
                S.dma('sp', out=io["ssm_p"][j, g * 256 + half * 128:g * 256 + (half + 1) * 128, :], in_=fs[:], reads=[fsr],
                      writes=[('ssm_p', j, g, half)])
        self.phase_end()
        self.phase_begin()
        self.ln_alloc()
        self.load_ln_params(layer, 'mix')
        wo = self.sb("swo", [128, 16, D], BF16)
        S.dma('pool', out=wo[:, 0:8, :], in_=io["ssm_w_out"][j][0:1024, :].rearrange("(k q) n -> q k n", q=128), writes=[('swo', 0)])
        S.dma('pool', out=wo[:, 8:16, :], in_=io["ssm_w_out"][j][1024:2048, :].rearrange("(k q) n -> q k n", q=128), writes=[('swo', 1)])
        outer = self.pes
        self.pes = contextlib.ExitStack()
        r_yt = self.ring("yt_", [128, 2048], BF16, 2)
        r_ynT = self.ring("ynT_", [128, 16, 128], BF16, 2)
        for tt in range(NT):
            yt, ytr = r_yt.next()
            S.dma('sp', out=yt[:], in_=io["yn_d"][tt * 128:(tt + 1) * 128, :], reads=[('yn_d', tt, g) for g in range(8)], writes=[ytr])
            ynT, ynTr = r_ynT.next()
            for hf in range(2):
                ps, pr = self.ps()
                psb = ps.bitcast(BF16)
                for k in range(8):
                    kk = hf * 8 + k
                    S.op('pe', 'transpose', psb[:, k * 128:(k + 1) * 128], yt[:, kk * 128:(kk + 1) * 128], self.identb[:],
                         reads=[ytr, 'identb'], writes=[pr], signal=(k == 7))
                S.op('act', 'copy', ynT[:, hf * 8:(hf + 1) * 8, :], sb_ap(psb, 1024, 0, [[128, 8], [1, 128]]), reads=[pr], writes=[(ynTr, hf)])
            hps = []
            for hf in range(2):
                ps, pr = self.ps()
                for k in range(16):
                    S.op('pe', 'matmul', ps[:, 0:512], lhsT=ynT[:, k, :], rhs=wo[:, k, hf * 512:(hf + 1) * 512], start=(k == 0), stop=(k == 15),
                         reads=[(ynTr, 0), (ynTr, 1), ('swo', 0), ('swo', 1)], writes=[pr], signal=(k == 15))
                hps.append((ps, pr))
            self.ln_prompt_tile(tt, [hps[0][0][:, 0:512], hps[1][0][:, 0:512]], [hps[0][1], hps[1][1]], 'mix', first=first)
        S.barrier()
        self.pes.close()
        self.pes = outer
        if hasattr(self, 'sample_ssd') and getattr(self, 'do_sample', True):
            self.sample_ssd(layer, wo)
        self.phase_end()


class KBFull(KBSsd):
    def sample_attn(self, layer, wo):
        S, io = self.S, self.io
        j = layer // 2
        sel = self.sb("sel", [NS, NS * 128], F32)
        ocol = self.sb("ocol", [128, NS * NS], F32)
        rts = self.sb("rts", [NS, 64], F32)
        S.dma('sp', out=sel[:], in_=io["c_sel"], writes=['sel'])
        S.dma('sp', out=ocol[:], in_=io["c_onescol"], writes=['ocol'])
        S.dma('sp', out=rts[:], in_=io["c_rope_s"].partition_broadcast(NS), writes=['rts'])
        r_qkv = self.ring("sqkv_", [NS, 3, D], F32, 2)
        r_rp = self.ring("srp_", [NS, 2, D], F32, 2)
        r_t = self.ring("st1_", [NS, 2, D], F32, 2)
        r_kv = self.ring("skv_", [128, 2048], F32, 2)
        r_pr = self.ring("spr_", [128, D], F32, 2)
        r_s = self.ring("ssc_", [128, 64], F32, 3)
        nacc = self.sb("nacc", [NS, D + 16], F32)
        cnames = ["cache_kv_w128", "cache_kv_w512", "cache_kv_w2048"]
        onames = ["kv128_s", "kv512_s", "kv2048_s"]
        first = True
        for g, (W, dil) in enumerate(GROUPS):
            qkv, qr = r_qkv.next()
            S.dma('sp', out=qkv[:], in_=io["qkvs_d"][:, g * 3:(g + 1) * 3, :], reads=[('qkvs_d', g, hp) for hp in range(8)], writes=[qr])
            rp, rpr = r_rp.next()
            t2, t2r = r_t.next()
            S.op('dve', 'tensor_tensor', out=sb_ap(rp, 2 * D, 0, [[32, 64], [1, 32]], np_=NS), in0=sb_ap(qkv, 3 * D, 0, [[32, 64], [1, 32]], np_=NS),
                 in1=sb_ap(rts, 64, 0, [[0, 64], [1, 32]], np_=NS), op=ALU.mult, reads=[qr, 'rts'], writes=[rpr])
            S.op('dve', 'scalar_tensor_tensor', out=sb_ap(t2, 2 * D, 0, [[64, 32], [1, 32]], np_=NS), in0=sb_ap(qkv, 3 * D, 32, [[64, 32], [1, 32]], np_=NS),
                 scalar=-1.0, in1=sb_ap(rts, 64, 32, [[0, 32], [1, 32]], np_=NS), op0=ALU.mult, op1=ALU.mult, reads=[qr, 'rts'], writes=[(t2r, 0)])
            S.op('dve', 'tensor_tensor', out=sb_ap(t2, 2 * D, 32, [[64, 32], [1, 32]], np_=NS), in0=sb_ap(qkv, 3 * D, 0, [[64, 32], [1, 32]], np_=NS),
                 in1=sb_ap(rts, 64, 32, [[0, 32], [1, 32]], np_=NS), op=ALU.mult, reads=[qr, 'rts'], writes=[(t2r, 1)])
            S.op('dve', 'tensor_tensor', out=rp[:], in0=rp[:], in1=t2[:], op=ALU.add, reads=[rpr, (t2r, 0), (t2r, 1)], writes=[rpr])
            S.dma('sp', out=io[onames[g]][j, :, W - 1, 0:D], in_=rp[:, 1, :], reads=[rpr], writes=[(onames[g], j, 'k')])
            S.dma('sp', out=io[onames[g]][j, :, W - 1, D:2 * D], in_=qkv[:, 2, :], reads=[qr], writes=[(onames[g], j, 'v')])
            sc, scr = r_s.next()
            t1, t1r = r_t.next()
            S.op('dve', 'tensor_tensor', out=t1[:, 0, :], in0=rp[:, 0, :], in1=rp[:, 1, :], op=ALU.mult, reads=[rpr], writes=[t1r])
            S.op('dve', 'tensor_reduce', out=sc[0:NS, 0:16], in_=sb_ap(t1, 2 * D, 0, [[64, 16], [1, 64]], np_=NS), axis=AX.X, op=ALU.add,
                 reads=[t1r], writes=[scr])
            S.op('act', 'activation', out=sc[0:NS, 16:32], in_=sc[0:NS, 0:16], func=AF.Exp, scale=SCALE, reads=[scr], writes=[scr])
            S.op('dve', 'tensor_tensor', out=sb_ap(t1, 2 * D, D, [[64, 16], [1, 64]], np_=NS), in0=sb_ap(qkv, 3 * D, 2 * D, [[64, 16], [1, 64]], np_=NS),
                 in1=sb_ap(sc, 64, 16, [[1, 16], [0, 64]], np_=NS), op=ALU.mult, reads=[qr, scr, t1r], writes=[t1r])
            if first:
                S.op('dve', 'tensor_copy', nacc[:, 0:D], t1[:, 1, :], reads=[t1r], writes=['nacc'])
                S.op('dve', 'tensor_copy', nacc[:, D:D + 16], sc[0:NS, 16:32], reads=[scr, 'nacc'], writes=['nacc'])
                first = False
            else:
                S.op('dve', 'tensor_tensor', out=nacc[:, 0:D], in0=nacc[:, 0:D], in1=t1[:, 1, :], op=ALU.add, reads=[t1r, 'nacc'], writes=['nacc'])
                S.op('dve', 'tensor_tensor', out=nacc[:, D:D + 16], in0=nacc[:, D:D + 16], in1=sc[0:NS, 16:32], op=ALU.add,
                     reads=[scr, 'nacc'], writes=['nacc'])
            for b in range(NS):
                kv, kvr = r_kv.next()
                S.dma('sp', out=kv[:], in_=io[cnames[g]][j, b].rearrange("(m s) c -> m s c", s=dil)[:, 0, :], writes=[kvr])
                qb = []
                for hf in range(2):
                    ps, pr = self.ps()
                    S.op('pe', 'matmul', ps[:, 0:512], lhsT=sel[:, b * 128:(b + 1) * 128], rhs=rp[:, 0, hf * 512:(hf + 1) * 512], start=True, stop=True,
                         reads=['sel', rpr], writes=[pr])
                    qb.append((ps, pr))
                prd, prr = r_pr.next()
                for hf in range(2):
                    S.op('dve', 'tensor_tensor', out=prd[:, hf * 512:(hf + 1) * 512], in0=kv[:, hf * 512:(hf + 1) * 512], in1=qb[hf][0][:, 0:512],
                         op=ALU.mult, reads=[kvr, qb[hf][1]], writes=[(prr, hf)])
                s2, s2r = r_s.next()
                S.op('dve', 'tensor_reduce', out=s2[:, 0:16], in_=sb_ap(prd, D, 0, [[64, 16], [1, 64]]), axis=AX.X, op=ALU.add,
                     reads=[(prr, 0), (prr, 1)], writes=[s2r])
                S.op('act', 'activation', out=s2[:, 16:32], in_=s2[:, 0:16], func=AF.Exp, scale=SCALE, reads=[s2r], writes=[s2r])
                S.op('dve', 'tensor_tensor', out=sb_ap(prd, D, 0, [[64, 16], [1, 64]]), in0=sb_ap(kv, 2048, D, [[64, 16], [1, 64]]),
                     in1=sb_ap(s2, 64, 16, [[1, 16], [0, 64]]), op=ALU.mult, reads=[kvr, s2r, (prr, 0), (prr, 1)], writes=[(prr, 0), (prr, 1)])
                lhs = ocol[:, b * NS:(b + 1) * NS]
                for hf in range(2):
                    ps, pr = self.ps()
                    S.op('pe', 'matmul', ps[0:NS, 0:512], lhsT=lhs, rhs=prd[:, hf * 512:(hf + 1) * 512], start=True, stop=True,
                         reads=['ocol', (prr, 0), (prr, 1)], writes=[pr])
                    S.op('dve', 'tensor_tensor', out=nacc[:, hf * 512:(hf + 1) * 512], in0=nacc[:, hf * 512:(hf + 1) * 512], in1=ps[0:NS, 0:512],
                         op=ALU.add, reads=[pr, 'nacc'], writes=['nacc'])
                ps, pr = self.ps()
                S.op('pe', 'matmul', ps[0:NS, 0:16], lhsT=lhs, rhs=s2[:, 16:32], start=True, stop=True, reads=['ocol', s2r], writes=[pr])
                S.op('dve', 'tensor_tensor', out=nacc[:, D:D + 16], in0=nacc[:, D:D + 16], in1=ps[0:NS, 0:16], op=ALU.add,
                     reads=[pr, 'nacc'], writes=['nacc'])
        S.op('dve', 'reciprocal', nacc[:, D:D + 16], nacc[:, D:D + 16], reads=['nacc'], writes=['nacc'])
        xb, xbr = self.r_xb.next()
        S.op('dve', 'tensor_tensor', out=sb_ap(xb, D, 0, [[64, 16], [1, 64]], np_=NS), in0=sb_ap(nacc, D + 16, 0, [[64, 16], [1, 64]], np_=NS),
             in1=sb_ap(nacc, D + 16, D, [[1, 16], [0, 64]], np_=NS), op=ALU.mult, reads=['nacc'], writes=[xbr])
        oTs = self.sb("oTs", [128, 8, NS], BF16)
        self.to_feat(xb, xbr, NS, oTs, 'oTs', 8 * NS, 0)
        hps = []
        for hf in range(2):
            ps, pr = self.ps()
            for k in range(8):
                S.op('pe', 'matmul', ps[0:NS, 0:512], lhsT=oTs[:, k, :], rhs=wo[:, k, hf * 512:(hf + 1) * 512], start=(k == 0), stop=(k == 7),
                     reads=['oTs', 'wo'], writes=[pr], signal=(k == 7))
            hps.append((ps, pr))
        self.ln_sample([hps[0][0][0:NS, 0:512], hps[1][0][0:NS, 0:512]], [hps[0][1], hps[1][1]], 'mix')

    def sample_ssd(self, layer, wo):
        S, io = self.S, self.io
        j = layer // 2
        allsp = [('sproj_d', 'dt')] + [('sproj_d', g, x) for g in range(8) for x in 'zxBC']
        sel = self.sb("sel2", [NS, NS * 128], F32)
        S.dma('sp', out=sel[:], in_=io["c_sel"], writes=['sel2'])
        xa = self.sb("xa", [NS, 4096], F32)
        r_cv = self.ring("cv_", [NS, 9, 512], F32, 1)
        S.dma('sp', out=io["conv_s"][j, :, 0:2, :], in_=io["state_conv"][j, :, 1:3, :], writes=[('conv_s', j, 0)])
        S.dma('sp', out=io["conv_s"][j, :, 2, :], in_=io["sproj_d"][:, 2048:6144], reads=allsp, writes=[('conv_s', j, 1)])
        for pc in range(8):
            cv, cvr = r_cv.next()
            cs = slice(pc * 512, (pc + 1) * 512)
            S.dma('sp', out=cv[:, 0:3, :], in_=io["state_conv"][j, :, :, cs], writes=[(cvr, 0)])
            S.dma('sp', out=cv[:, 3, :], in_=io["sproj_d"][:, 2048 + pc * 512:2048 + (pc + 1) * 512], reads=allsp, writes=[(cvr, 1)])
            for k in range(4):
                S.dma('sp', out=cv[:, 4 + k, :], in_=io["ssm_conv_w"][j, k:k + 1, cs].partition_broadcast(NS), writes=[(cvr, 2 + k)])
            S.dma('sp', out=cv[:, 8, :], in_=io["ssm_conv_b"][j:j + 1, cs].partition_broadcast(NS), writes=[(cvr, 6)])
            allcv = [(cvr, i) for i in range(7)]
            S.op('dve', 'tensor_tensor', out=cv[:, 0:4, :], in0=cv[:, 0:4, :], in1=cv[:, 4:8, :], op=ALU.mult, reads=allcv, writes=allcv)
            S.op('dve', 'tensor_tensor', out=cv[:, 0:2, :], in0=cv[:, 0:2, :], in1=cv[:, 2:4, :], op=ALU.add, reads=allcv, writes=allcv)
            S.op('dve', 'tensor_tensor', out=cv[:, 0, :], in0=cv[:, 0, :], in1=cv[:, 1, :], op=ALU.add, reads=allcv, writes=allcv)
            S.op('dve', 'tensor_tensor', out=cv[:, 0, :], in0=cv[:, 0, :], in1=cv[:, 8, :], op=ALU.add, reads=allcv, writes=allcv)
            S.op('act', 'activation', out=xa[:, cs], in_=cv[:, 0, :], func=AF.Silu, reads=allcv, writes=[('xa', pc)])
        xar = [('xa', pc) for pc in range(8)]
        if getattr(self, 'sdbg', 9) <= 1:
            return
        sm = self.sb("ssm_sm", [NS, 128], F32)
        hb = self.sb("ssm_hb", [NS, 96], F32)
        for i, nm in enumerate(["ssm_dt_bias", "ssm_a_log", "ssm_d"]):
            S.dma('sp', out=hb[:, i * 32:(i + 1) * 32], in_=io[nm][j:j + 1, :].partition_broadcast(NS), writes=[('hb', i)])
        S.dma('sp', out=sm[:, 0:32], in_=io["sproj_d"][:, 6144:6176], reads=allsp, writes=['ssm_sm'])
        S.op('dve', 'tensor_tensor', out=sm[:, 0:32], in0=sm[:, 0:32], in1=hb[:, 0:32], op=ALU.add, reads=['ssm_sm', ('hb', 0)], writes=['ssm_sm'])
        S.op('act', 'activation', out=sm[:, 0:32], in_=sm[:, 0:32], func=AF.Exp, reads=['ssm_sm'], writes=['ssm_sm'])
        S.op('dve', 'tensor_scalar', out=sm[:, 0:32], in0=sm[:, 0:32], scalar1=1.0, scalar2=None, op0=ALU.add, reads=['ssm_sm'], writes=['ssm_sm'])
        S.op('act', 'activation', out=sm[:, 0:32], in_=sm[:, 0:32], func=AF.Ln, reads=['ssm_sm'], writes=['ssm_sm'])
        S.op('act', 'activation', out=hb[:, 32:64], in_=hb[:, 32:64], func=AF.Exp, reads=[('hb', 1)], writes=[('hb', 1)])
        S.op('dve', 'tensor_tensor', out=sm[:, 32:64], in0=sm[:, 0:32], in1=hb[:, 32:64], op=ALU.mult, reads=['ssm_sm', ('hb', 1)], writes=['ssm_sm'])
        S.op('act', 'activation', out=sm[:, 32:64], in_=sm[:, 32:64], func=AF.Exp, scale=-1.0, reads=['ssm_sm'], writes=['ssm_sm'])
        tm = self.sb("ssm_tm", [NS, 2, 2048], F32)
        S.op('dve', 'tensor_tensor', out=sb_ap(tm, 4096, 0, [[64, 32], [1, 64]], np_=NS), in0=sb_ap(xa, 4096, 0, [[64, 32], [1, 64]], np_=NS),
             in1=sb_ap(sm, 128, 0, [[1, 32], [0, 64]], np_=NS), op=ALU.mult, reads=xar + ['ssm_sm'], writes=[('tm', 0)])
        S.op('dve', 'tensor_copy', sb_ap(tm, 4096, 2048, [[64, 32], [1, 64]], np_=NS), sb_ap(sm, 128, 32, [[1, 32], [0, 64]], np_=NS),
             reads=['ssm_sm'], writes=[('tm', 1)])
        fm = self.sb("ssm_fm", [128, 2, 16, NS], F32)
        for i in range(2):
            ps, pr = self.ps()
            for c in range(16):
                S.op('pe', 'matmul', ps[:, c * NS:(c + 1) * NS], lhsT=tm[:, i, c * 128:(c + 1) * 128], rhs=self.identf[0:NS, 0:NS], start=True, stop=True,
                     reads=[('tm', i), 'identf'], writes=[pr], signal=(c == 15))
            S.op('act', 'copy', fm[:, i, :, :], sb_ap(ps, 512, 0, [[NS, 16], [1, NS]]), reads=[pr], writes=[('fm', i)])
        yT = self.sb("ssm_yT", [128, 16, NS], F32)
        if getattr(self, 'sdbg', 9) <= 2:
            return
        r_h = self.ring("ssm_h_", [128, 16, 128], F32, 1)
        r_tmp = self.ring("ssm_tmp_", [128, 16, 128], F32, 1)
        for b in range(NS):
            h, hr = r_h.next()
            S.dma('sp', out=h[:], in_=io["state_ssm"][j, b].rearrange("(c q) n -> q c n", q=128), writes=[hr])
            bc = []
            for which in range(2):
                for hf in range(2):
                    ps, pr = self.ps()
                    c0 = 2048 + which * 1024 + hf * 512
                    S.op('pe', 'matmul', ps[:, 0:512], lhsT=sel[:, b * 128:(b + 1) * 128], rhs=xa[:, c0:c0 + 512], start=True, stop=True,
                         reads=['sel2'] + xar, writes=[pr])
                    bc.append((ps, pr))
            tmp, tmr = r_tmp.next()
            for hf in range(2):
                S.op('dve', 'tensor_tensor', out=sb_ap(tmp, 2048, hf * 1024, [[256, 4], [128, 2], [1, 128]]),
                     in0=sb_ap(bc[hf][0], 512, 0, [[128, 4], [0, 2], [1, 128]]),
                     in1=sb_ap(fm, 2 * 16 * NS, (hf * 8) * NS + b, [[2 * NS, 4], [NS, 2], [0, 128]]), op=ALU.mult,
                     reads=[bc[hf][1], ('fm', 0)], writes=[(tmr, hf)])
            S.op('dve', 'tensor_tensor', out=h[:], in0=h[:], in1=sb_ap(fm, 2 * 16 * NS, 16 * NS + b, [[NS, 16], [0, 128]]), op=ALU.mult,
                 reads=[hr, ('fm', 1)], writes=[hr])
            S.op('dve', 'tensor_tensor', out=h[:], in0=h[:], in1=tmp[:], op=ALU.add, reads=[hr, (tmr, 0), (tmr, 1)], writes=[hr])
            S.dma('sp', out=io["ssm_s"][j, b].rearrange("(c q) n -> q c n", q=128), in_=h[:], reads=[hr], writes=[('ssm_s', j, b)])
            for hf in range(2):
                S.op('dve', 'tensor_tensor', out=sb_ap(tmp, 2048, hf * 1024, [[256, 4], [128, 2], [1, 128]]),
                     in0=sb_ap(h, 2048, hf * 1024, [[256, 4], [128, 2], [1, 128]]),
                     in1=sb_ap(bc[2 + hf][0], 512, 0, [[128, 4], [0, 2], [1, 128]]), op=ALU.mult,
                     reads=[hr, bc[2 + hf][1], (tmr, hf)], writes=[(tmr, hf)])
            S.op('dve', 'tensor_reduce', out=sb_ap(yT, 16 * NS, b, [[NS, 16]]), in_=tmp[:], axis=AX.X, op=ALU.add,
                 reads=[(tmr, 0), (tmr, 1)], writes=[('yT', b)])
        if getattr(self, 'sdbg', 9) <= 3:
            return
        ys = tm
        for q4 in range(4):
            ps, pr = self.ps()
            for c in range(4):
                cc = q4 * 4 + c
                S.op('pe', 'matmul', ps[0:NS, c * 128:(c + 1) * 128], lhsT=yT[:, cc, :], rhs=self.identf[:], start=True, stop=True,
                     reads=[('yT', b) for b in range(NS)] + ['identf'], writes=[pr], signal=(c == 3))
            S.op('act', 'copy', ys[:, 0, q4 * 512:(q4 + 1) * 512], ps[0:NS, 0:512], reads=[pr], writes=[('ys', q4)])
        ysr = [('ys', q4) for q4 in range(4)]
        if getattr(self, 'sdbg', 9) <= 4:
            return
        S.op('dve', 'tensor_tensor', out=sb_ap(ys, 4096, 2048, [[64, 32], [1, 64]], np_=NS), in0=sb_ap(xa, 4096, 0, [[64, 32], [1, 64]], np_=NS),
             in1=sb_ap(hb, 96, 64, [[1, 32], [0, 64]], np_=NS), op=ALU.mult, reads=xar + [('hb', 2)], writes=['ys1'])
        S.op('dve', 'tensor_tensor', out=ys[:, 0, :], in0=ys[:, 0, :], in1=ys[:, 1, :], op=ALU.add, reads=ysr + ['ys1'], writes=ysr)
        S.dma('sp', out=ys[:, 1, :], in_=io["sproj_d"][:, 0:2048], reads=allsp + ['ys1'] + ysr, writes=['ys1'])
        S.op('act', 'activation', out=ys[:, 1, :], in_=ys[:, 1, :], func=AF.Silu, reads=['ys1'], writes=['ys1'])
        S.op('dve', 'tensor_tensor', out=ys[:, 0, :], in0=ys[:, 0, :], in1=ys[:, 1, :], op=ALU.mult, reads=ysr + ['ys1'], writes=ysr)
        S.op('dve', 'tensor_tensor', out=ys[:, 1, :], in0=ys[:, 0, :], in1=ys[:, 0, :], op=ALU.mult, reads=ysr + ['ys1'], writes=['ys1'])
        S.op('dve', 'tensor_reduce', out=sm[:, 64:72], in_=sb_ap(ys, 4096, 2048, [[256, 8], [1, 256]], np_=NS), axis=AX.X, op=ALU.add,
             reads=['ys1', 'ssm_sm'], writes=['ssm_sm'])
        S.op('dve', 'tensor_scalar', out=sm[:, 64:72], in0=sm[:, 64:72], scalar1=1.0 / 256, scalar2=RMS_EPS, op0=ALU.mult, op1=ALU.add,
             reads=['ssm_sm'], writes=['ssm_sm'])
        S.op('act', 'activation', out=sm[:, 64:72], in_=sm[:, 64:72], func=AF.Ln, reads=['ssm_sm'], writes=['ssm_sm'])
        S.op('act', 'activation', out=sm[:, 72:80], in_=sm[:, 64:72], func=AF.Exp, scale=-0.5, reads=['ssm_sm'], writes=['ssm_sm'])
        S.op('dve', 'tensor_tensor', out=sb_ap(ys, 4096, 0, [[256, 8], [1, 256]], np_=NS), in0=sb_ap(ys, 4096, 0, [[256, 8], [1, 256]], np_=NS),
             in1=sb_ap(sm, 128, 72, [[1, 8], [0, 256]], np_=NS), op=ALU.mult, reads=ysr + ['ssm_sm'], writes=ysr)
        S.dma('sp', out=ys[:, 1, :], in_=io["ssm_norm_w"][j:j + 1, :].partition_broadcast(NS), reads=['ys1'], writes=['ys1'])
        ysb = self.sb("ssm_ysb", [NS, 2048], BF16)
        S.op('dve', 'tensor_tensor', out=ysb[:], in0=ys[:, 0, :], in1=ys[:, 1, :], op=ALU.mult, reads=ysr + ['ys1'], writes=['ysb'])
        ysT = self.sb("ssm_ysT", [128, 16, NS], BF16)
        for hf in range(2):
            ps, pr = self.ps()
            psb = ps.bitcast(BF16)
            for k in range(8):
                kk = hf * 8 + k
                S.op('pe', 'transpose', psb[:, k * 128:k * 128 + NS], ysb[0:NS, kk * 128:(kk + 1) * 128], self.identb[0:NS, 0:NS],
                     reads=['ysb', 'identb'], writes=[pr], signal=(k == 7))
            S.op('act', 'copy', ysT[:, hf * 8:(hf + 1) * 8, :], sb_ap(psb, 1024, 0, [[128, 8], [1, NS]]), reads=[pr], writes=[('ysT', hf)])
        hps = []
        for hf in range(2):
            ps, pr = self.ps()
            for k in range(16):
                S.op('pe', 'matmul', ps[0:NS, 0:512], lhsT=ysT[:, k, :], rhs=wo[:, k, hf * 512:(hf + 1) * 512], start=(k == 0), stop=(k == 15),
                     reads=[('ysT', 0), ('ysT', 1), ('swo', 0), ('swo', 1)], writes=[pr], signal=(k == 15))
            hps.append((ps, pr))
        self.ln_sample([hps[0][0][0:NS, 0:512], hps[1][0][0:NS, 0:512]], [hps[0][1], hps[1][1]], 'mix')


import numpy as np
GROUPS = ((128, 1), (512, 4), (2048, 16))
NS = 4

def make_consts(T):
    NT = T // 128
    c = {}
    c["c_ident"] = np.eye(128, dtype=np.float32)
    i = np.arange(128)
    c["c_tri"] = (i[:, None] <= i[None, :]).astype(np.float32)
    c["c_ustrict"] = (i[:, None] > i[None, :]).astype(np.float32)
    prev = (i[:, None] >= i[None, :]).astype(np.float32)
    cur = (i[:, None] <= i[None, :]).astype(np.float32)
    c["c_amask"] = np.concatenate([prev, cur, prev, cur], axis=1)
    inv = np.power(np.float32(10000.0), -np.arange(32, dtype=np.float32) * np.float32(2.0 / 64)).astype(np.float32)
    rope = np.zeros((3, NT, 128, 64), np.float32)
    for g, (W, d) in enumerate(GROUPS):
        nbn = NT // d
        for ct in range(NT if nbn > 0 else 0):
            r, nb = ct // nbn, ct % nbn
            pos = ((nb * 128 + i) * d + r).astype(np.float32)
            ang = (pos[:, None] * inv[None, :]).astype(np.float32)
            rope[g, ct, :, 0:32] = np.cos(ang.astype(np.float64))
            rope[g, ct, :, 32:64] = np.sin(ang.astype(np.float64))
    c["c_rope"] = rope
    angs = (np.float32(8192.0) * inv).astype(np.float32)
    c["c_rope_s"] = np.concatenate([np.cos(angs.astype(np.float64)), np.sin(angs.astype(np.float64))])[None, :].astype(np.float32)
    sel = np.zeros((NS, NS * 128), np.float32)
    for b in range(NS):
        sel[b, b * 128:(b + 1) * 128] = 1.0
    c["c_sel"] = sel
    oc = np.zeros((128, NS, NS), np.float32)
    for b in range(NS):
        oc[:, b, b] = 1.0
    c["c_onescol"] = oc.reshape(128, NS * NS)
    return c


def build_full(T=4096, depth=4):
    kb = KBFull(T=T)
    kb.declare()
    kb.setup()
    S, io = kb.S, kb.io
    kb.phase_init()
    for cname, oname, W in [("cache_kv_w128", "kv128_s", 128), ("cache_kv_w512", "kv512_s", 512), ("cache_kv_w2048", "kv2048_s", 2048)]:
        for j in range(2):
            for b in range(NS):
                nchunk = max(1, W // 1024)
                step = (W - 1 + nchunk - 1) // nchunk
                for r0 in range(0, W - 1, step):
                    r1 = min(W - 1, r0 + step)
                    S.dma('act', out=io[oname][j, b, r0:r1, :], in_=io[cname][j, b, r0 + 1:r1 + 1, :], own=True,
                          writes=[(oname, j, b, r0)])
    for layer in range(depth):
        if layer % 2 == 0:
            kb.phase_attn(layer, first=(layer == 0))
        else:
            kb.phase_ssd(layer)
        kb.phase_mlp(layer)
    nc = kb.finish()
    return nc, kb


_CACHE = {}


def kernel(x_prompt, x_sample, cache_kv_w128, cache_kv_w512, cache_kv_w2048, state_ssm, state_conv,
           attn_w_in, attn_w_out, ssm_w_in, ssm_conv_w, ssm_conv_b, ssm_dt_bias, ssm_a_log, ssm_d,
           ssm_norm_w, ssm_w_out, mlp_w1, mlp_w2, ln_mix_g, ln_mix_b, ln_ffn_g, ln_ffn_b):
    from concourse.bass_utils import run_bass_kernel_spmd
    f = lambda a: np.ascontiguousarray(np.asarray(a, dtype=np.float32))
    T = 4096
    nc, kb = build_full(T)
    consts = make_consts(T)
    shared = {"attn_w_in": f(attn_w_in), "attn_w_out": f(attn_w_out), "ssm_w_in": f(ssm_w_in), "ssm_conv_w": f(ssm_conv_w),
              "ssm_conv_b": f(ssm_conv_b), "ssm_dt_bias": f(ssm_dt_bias), "ssm_a_log": f(ssm_a_log), "ssm_d": f(ssm_d),
              "ssm_norm_w": f(ssm_norm_w), "ssm_w_out": f(ssm_w_out), "mlp_w1": f(mlp_w1), "mlp_w2": f(mlp_w2),
              "ln_mix_g": f(ln_mix_g), "ln_mix_b": f(ln_mix_b), "ln_ffn_g": f(ln_ffn_g), "ln_ffn_b": f(ln_ffn_b)}
    shared.update(consts)
    xp = f(x_prompt); xs = f(x_sample)
    c128 = f(cache_kv_w128).reshape(2, 32, 128, 2048); c512 = f(cache_kv_w512).reshape(2, 32, 512, 2048)
    c2048 = f(cache_kv_w2048).reshape(2, 32, 2048, 2048)
    sss = f(state_ssm).reshape(2, 32, 2048, 128); scv = f(state_conv)
    in_maps = []
    for c in range(8):
        m = dict(shared)
        rs = slice(4 * c, 4 * c + 4)
        m["x_prompt"] = np.ascontiguousarray(xp[c % 4])
        m["x_sample"] = np.ascontiguousarray(xs[rs, 0])
        m["cache_kv_w128"] = np.ascontiguousarray(c128[:, rs]); m["cache_kv_w512"] = np.ascontiguousarray(c512[:, rs])
        m["cache_kv_w2048"] = np.ascontiguousarray(c2048[:, rs])
        m["state_ssm"] = np.ascontiguousarray(sss[:, rs]); m["state_conv"] = np.ascontiguousarray(scv[:, rs])
        in_maps.append(m)
    res = run_bass_kernel_spmd(nc, in_maps, core_ids=list(range(8)))
    R = res.results
    y_prompt = np.stack([R[b]["y_prompt"] for b in range(4)], 0)
    y_sample = np.concatenate([R[c]["y_sample"] for c in range(8)], 0).reshape(32, 1, 1024)
    def pstack(name, W):
        return np.stack([R[b][name] for b in range(4)], 1).reshape(2, 4, W, 2, 16, 64)
    kv128_p, kv512_p, kv2048_p = pstack("kv128_p", 128), pstack("kv512_p", 512), pstack("kv2048_p", 2048)
    ssm_p = np.stack([R[b]["ssm_p"] for b in range(4)], 1).reshape(2, 4, 32, 64, 128)
    conv_p = np.stack([R[b]["conv_p"] for b in range(4)], 1)
    def sstack(name, W):
        return np.concatenate([R[c][name] for c in range(8)], 1).reshape(2, 32, W, 2, 16, 64)
    kv128_s, kv512_s, kv2048_s = sstack("kv128_s", 128), sstack("kv512_s", 512), sstack("kv2048_s", 2048)
    ssm_s = np.concatenate([R[c]["ssm_s"] for c in range(8)], 1).reshape(2, 32, 32, 64, 128)
    conv_s = np.concatenate([R[c]["conv_s"] for c in range(8)], 1)
    return (y_prompt, y_sample, kv128_p, kv512_p, kv2048_p, ssm_p, conv_p, kv128_s, kv512_s, kv2048_s, ssm_s, conv_s)
```

```python
import numpy as np
import concourse.bass as bass
import concourse.mybir as mybir

F32 = mybir.dt.float32
BF16 = mybir.dt.bfloat16
AF = mybir.ActivationFunctionType
ALU = mybir.AluOpType
AX = mybir.AxisListType

ENGS = ['pe', 'act', 'dve', 'pool', 'sp']
EPOCH = 20000
NDMA = 24


class Sched:
    def __init__(self, nc, es):
        self.nc = nc
        self.es = es
        self.q = {e: [] for e in ENGS}
        self.nev = {e: 0 for e in ENGS}
        self.esem = {}
        self.seen = {e: {} for e in ENGS}
        self.res = {}
        self.dsem = [es.enter_context(nc.semaphore(f"dma{i}")) for i in range(NDMA)]
        self.dcnt = [0] * NDMA
        self.drr = 0
        self.nops = 0
        self.own = set()

    def _esem(self, eng, epoch):
        k = (eng, epoch)
        if k not in self.esem:
            self.esem[k] = self.es.enter_context(self.nc.semaphore(f"ev_{eng}_{epoch}"))
        return self.esem[k]

    def _ev_semval(self, ev):
        if ev[0] == 'e':
            _, eng, idx = ev
            epoch = (idx - 1) // EPOCH
            return ('e', eng, epoch), self._esem(eng, epoch), idx - epoch * EPOCH
        _, si, val = ev
        return ('d', si), self.dsem[si], val

    def _deps(self, reads, writes, stream=None):
        deps = []
        for r in reads:
            st = self.res.get(r)
            if st and st['w']:
                deps.append(st['w'])
            if st and isinstance(r, tuple) and r[0] == 'ps':
                deps.extend(ev for sname, ev in st['r'].items() if sname != stream)
        for w in writes:
            st = self.res.get(w)
            if st:
                if st['w']:
                    deps.append(st['w'])
                deps.extend(st['r'].values())
        return deps

    def _waits(self, eng, deps):
        waits = []
        for ev in deps:
            if ev[0] == 'e' and ev[1] == 'pe' and eng == 'pe':
                continue
            key, sem, val = self._ev_semval(ev)
            if self.seen[eng].get(key, 0) >= val:
                continue
            self.seen[eng][key] = val
            waits.append((sem, val))
        return waits

    def _mark(self, stream, ev, reads, writes):
        for r in reads:
            st = self.res.setdefault(r, {'w': None, 'r': {}})
            st['r'][stream] = ev
        for w in writes:
            self.res[w] = {'w': ev, 'r': {}}

    def op(self, eng, name, *args, reads=(), writes=(), signal=True, **kw):
        fn = (name, args, kw)
        deps = self._deps(reads, writes, eng)
        waits = self._waits(eng, deps)
        if signal:
            self.nev[eng] += 1
            idx = self.nev[eng]
            epoch = (idx - 1) // EPOCH
            sem = self._esem(eng, epoch)
        else:
            idx = self.nev[eng] + 1
            sem = None
        ev = ('e', eng, idx)
        self._mark(eng, ev, reads, writes)
        self.q[eng].append((waits, fn, sem, 1))
        self.nops += 1
        return ev

    def dma(self, qeng, reads=(), writes=(), own=False, **kw):
        fn = ('dma_start', (), kw)
        if own:
            si = len(self.dsem)
            self.dsem.append(self.es.enter_context(self.nc.semaphore(f"dmaown{si}")))
            self.dcnt.append(0)
            self.own.add(si)
        else:
            si = self.drr
            self.drr = (self.drr + 1) % NDMA
        deps = self._deps(reads, writes)
        if self.dcnt[si] > 0:
            deps.append(('d', si, self.dcnt[si]))
        waits = self._waits(qeng, deps)
        self.dcnt[si] += 16
        ev = ('d', si, self.dcnt[si])
        self._mark(('d', si), ev, reads, writes)
        self.q[qeng].append((waits, fn, self.dsem[si], 16))
        self.nops += 1
        return ev

    def barrier(self):
        deps = [('d', si, c) for si, c in enumerate(self.dcnt) if c > 0 and si not in self.own]
        for e in ENGS:
            if self.nev[e] > 0:
                deps.append(('e', e, self.nev[e]))
        for eng in ENGS:
            waits = []
            for ev in deps:
                key, sem, val = self._ev_semval(ev)
                if self.seen[eng].get(key, 0) >= val:
                    continue
                self.seen[eng][key] = val
                waits.append((sem, val))
            if waits:
                self.q[eng].append((waits, None, None, 0))

    def final_wait(self, eng='sp'):
        deps = [('d', si, c) for si, c in enumerate(self.dcnt) if c > 0]
        for e in ENGS:
            if e != eng and self.nev[e] > 0:
                deps.append(('e', e, self.nev[e]))
        waits = self._waits(eng, deps)
        self.q[eng].append((waits, None, None, 0))

    def emit(self):
        nc = self.nc
        with nc.Block() as block:
            def run(e_obj, name):
                for waits, fn, sem, inc in self.q[name]:
                    for s, v in waits:
                        e_obj.wait_ge(s, v)
                    if fn is None:
                        continue
                    ins = getattr(e_obj, fn[0])(*fn[1], **fn[2])
                    if sem is not None:
                        ins.then_inc(sem, inc)

            @block.tensor
            def _(e):
                run(e, 'pe')

            @block.scalar
            def _(e):
                run(e, 'act')

            @block.vector
            def _(e):
                run(e, 'dve')

            @block.gpsimd
            def _(e):
                run(e, 'pool')

            @block.sync
            def _(e):
                run(e, 'sp')


import contextlib
import numpy as np

D = 1024
DFF = 4096
ALPHA = 8 ** 0.25
LN_EPS = 1e-5
RMS_EPS = 1e-5
SCALE = 0.125
NS = 4


def sb_ap(t, rowsz, off, dims, p0=0, np_=128):
    return bass.AP(t, p0 * rowsz + off, [[rowsz, np_]] + [list(d) for d in dims])


class Ring:
    def __init__(self, nc, es, name, shape, dt, n):
        self.t = [es.enter_context(nc.sbuf_tensor(f"{name}{i}", shape, dt)) for i in range(n)]
        self.name = name
        self.i = 0
        self.n = n

    def next(self):
        i = self.i
        self.i = (i + 1) % self.n
        return self.t[i], (self.name, i)


class KBCore:
    def __init__(self, T=4096, only=None):
        self.only = only
        self.T = T
        self.NT = T // 128
        self.nc = bass.Bass("TRN2", target_bir_lowering=False)
        self.es = contextlib.ExitStack()
        self.S = Sched(self.nc, self.es)
        self.io = {}
        self.pes = None

    def phase_begin(self):
        self.S.barrier()
        self.pes = contextlib.ExitStack()

    def phase_end(self):
        self.S.barrier()
        self.pes.close()
        self.pes = None

    def din(self, name, shape, dt=F32):
        if self.only is not None and name not in self.only:
            return None
        a = self.nc.dram_tensor(name, list(shape), dt, kind="ExternalInput").ap()
        self.io[name] = a
        return a

    def dout(self, name, shape, dt=F32):
        if self.only is not None and name not in self.only:
            return None
        a = self.nc.dram_tensor(name, list(shape), dt, kind="ExternalOutput").ap()
        self.io[name] = a
        return a

    def dscr(self, name, shape, dt):
        a = self.nc.dram_tensor(name, list(shape), dt, kind="ExternalOutput").ap()
        self.io[name] = a
        return a

    def _uniq(self, name):
        self.ncount = getattr(self, 'ncount', 0) + 1
        return "%s_u%d" % (name, self.ncount)

    def sb(self, name, shape, dt=F32):
        es = self.pes if self.pes is not None else self.es
        return es.enter_context(self.nc.sbuf_tensor(self._uniq(name), list(shape), dt))

    def ring(self, name, shape, dt, n):
        es = self.pes if self.pes is not None else self.es
        return Ring(self.nc, es, self._uniq(name), list(shape), dt, n)

    def declare(self):
        T = self.T
        d = self.din
        d("x_prompt", [T, D]); d("x_sample", [NS, D])
        d("cache_kv_w128", [2, NS, 128, 2048]); d("cache_kv_w512", [2, NS, 512, 2048])
        d("cache_kv_w2048", [2, NS, 2048, 2048])
        d("state_ssm", [2, NS, 2048, 128]); d("state_conv", [2, NS, 3, 4096])
        d("attn_w_in", [2, D, 9216]); d("attn_w_out", [2, D, D])
        d("ssm_w_in", [2, D, 6176]); d("ssm_conv_w", [2, 4, 4096]); d("ssm_conv_b", [2, 4096])
        d("ssm_dt_bias", [2, 32]); d("ssm_a_log", [2, 32]); d("ssm_d", [2, 32])
        d("ssm_norm_w", [2, 2048]); d("ssm_w_out", [2, 2048, D])
        d("mlp_w1", [4, D, DFF]); d("mlp_w2", [4, DFF, D])
        d("ln_mix_g", [4, D]); d("ln_mix_b", [4, D]); d("ln_ffn_g", [4, D]); d("ln_ffn_b", [4, D])
        d("c_ident", [128, 128]); d("c_tri", [128, 128]); d("c_ustrict", [128, 128])
        d("c_amask", [128, 512]); d("c_rope", [3, self.NT, 128, 64]); d("c_rope_s", [1, 64])
        d("c_sel", [NS, NS * 128]); d("c_onescol", [128, NS * NS])
        o = self.dout
        o("y_prompt", [T, D]); o("y_sample", [NS, D])
        o("kv128_p", [2, 128, 2048]); o("kv512_p", [2, 512, 2048]); o("kv2048_p", [2, 2048, 2048])
        o("ssm_p", [2, 2048, 128]); o("conv_p", [2, 3, 4096])
        o("kv128_s", [2, NS, 128, 2048]); o("kv512_s", [2, NS, 512, 2048]); o("kv2048_s", [2, NS, 2048, 2048])
        o("ssm_s", [2, NS, 2048, 128]); o("conv_s", [2, NS, 3, 4096])

    def setup(self):
        nc, S, T = self.nc, self.S, self.T
        self.PS = [self.es.enter_context(nc.psum_tensor(f"ps{i}", [128, 512], F32)) for i in range(8)]
        self.psi = 0
        self.xT = self.sb("xT", [128, 8, T], BF16)
        self.xsT = self.sb("xsT", [128, 8, NS], BF16)
        self.xs = self.sb("xs", [NS, D], F32)
        self.identf = self.sb("identf", [128, 128], F32)
        self.identb = self.sb("identb", [128, 128], BF16)
        io = self.io
        S.dma('sp', out=self.identf[:], in_=io["c_ident"], writes=['identf'])
        S.op('dve', 'tensor_copy', self.identb[:], self.identf[:], reads=['identf'], writes=['identb'])

    def ln_alloc(self):
        self.lnp = self.sb("lnp", [128, 2, D], F32)
        self.r_x32 = self.ring("x32_", [128, D], F32, 2)
        self.r_s32 = self.ring("s32_", [128, D], F32, 2)
        self.r_xb = self.ring("xb_", [128, D], BF16, 2)
        self.r_st = self.ring("st_", [128, 16], F32, 4)

    def ps(self):
        i = self.psi
        self.psi = (i + 1) % 8
        return self.PS[i], ('ps', i)

    def to_feat(self, src_t, src_res, rows, dstT, dst_res, dst_rowsz, col0):
        S = self.S
        ps, pr = self.ps()
        psb = ps.bitcast(BF16)
        for k in range(8):
            S.op('pe', 'transpose', psb[:, k * 128:k * 128 + rows], src_t[0:rows, k * 128:(k + 1) * 128],
                 self.identb[0:rows, 0:rows], reads=[src_res, 'identb'], writes=[pr], signal=(k == 7))
        src = sb_ap(psb, 1024, 0, [[128, 8], [1, rows]])
        dst = sb_ap(dstT, dst_rowsz, col0, [[dst_rowsz // 8, 8], [1, rows]])
        S.op('act', 'copy', dst, src, reads=[pr], writes=[dst_res])

    def load_ln_params(self, layer, which):
        S, io = self.S, self.io
        for j, nm in enumerate(["ln_%s_g" % which, "ln_%s_b" % which]):
            src = io[nm][layer:layer + 1, :].partition_broadcast(128)
            S.dma('sp', out=self.lnp[:, j, :], in_=src, writes=[('lnp', j)])

    def ln_tile(self, h_aps, h_res, xold_ap, xold_res, which, rows):
        S = self.S
        s32, sr = self.r_s32.next()
        st, str_ = self.r_st.next()
        xn, xnr = s32, sr
        xb, xbr = self.r_xb.next()
        for hf in range(2):
            S.op('dve', 'scalar_tensor_tensor', out=s32[0:rows, hf * 512:(hf + 1) * 512],
                 in0=xold_ap[0:rows, hf * 512:(hf + 1) * 512], scalar=ALPHA, in1=h_aps[hf], op0=ALU.mult, op1=ALU.add,
                 reads=[xold_res] + list(h_res), writes=[sr])
        for hf in range(2):
            S.op('dve', 'bn_stats', out=st[0:rows, hf * 6:(hf + 1) * 6], in_=s32[0:rows, hf * 512:(hf + 1) * 512],
                 reads=[sr], writes=[str_])
        S.op('dve', 'bn_aggr', out=st[0:rows, 12:14], in_=st[0:rows, 0:12], reads=[str_], writes=[str_])
        S.op('dve', 'tensor_scalar', out=st[0:rows, 15:16], in0=st[0:rows, 13:14], scalar1=LN_EPS, scalar2=None,
             op0=ALU.add, reads=[str_], writes=[str_])
        S.op('act', 'activation', out=st[0:rows, 15:16], in_=st[0:rows, 15:16], func=AF.Ln, reads=[str_], writes=[str_])
        S.op('act', 'activation', out=st[0:rows, 14:15], in_=st[0:rows, 15:16], func=AF.Exp, scale=-0.5, reads=[str_], writes=[str_])
        S.op('dve', 'tensor_scalar', out=xn[0:rows, :], in0=s32[0:rows, :], scalar1=st[0:rows, 12:13],
             scalar2=st[0:rows, 14:15], op0=ALU.subtract, op1=ALU.mult, reads=[sr, str_], writes=[xnr])
        gi = 0
        S.op('dve', 'tensor_tensor', out=xn[0:rows, :], in0=xn[0:rows, :], in1=self.lnp[0:rows, gi, :], op=ALU.mult,
             reads=[xnr, ('lnp', gi)], writes=[xnr])
        S.op('dve', 'tensor_tensor', out=xn[0:rows, :], in0=xn[0:rows, :], in1=self.lnp[0:rows, gi + 1, :], op=ALU.add,
             reads=[xnr, ('lnp', gi + 1)], writes=[xnr])
        S.op('act', 'copy', xb[0:rows, :], xn[0:rows, :], reads=[xnr], writes=[xbr])
        return xn, xnr, xb, xbr

    def ln_prompt_tile(self, tt, h_aps, h_res, which, first=False):
        S, io = self.S, self.io
        src = io["x_prompt"] if first else io["y_prompt"]
        x32, xr = self.r_x32.next()
        S.dma('sp', out=x32[:], in_=src[tt * 128:(tt + 1) * 128, :], reads=[('yp', tt)], writes=[xr])
        xn, xnr, xb, xbr = self.ln_tile(h_aps, h_res, x32, xr, which, 128)
        S.dma('sp', out=io["y_prompt"][tt * 128:(tt + 1) * 128, :], in_=xn[:], reads=[xnr], writes=[('yp', tt)])
        self.to_feat(xb, xbr, 128, self.xT, ('xT', tt), 8 * self.T, tt * 128)

    def ln_sample(self, h_aps, h_res, which):
        S = self.S
        xn, xnr, xb, xbr = self.ln_tile(h_aps, h_res, self.xs, 'xs', which, NS)
        S.op('dve', 'tensor_copy', self.xs[0:NS, :], xn[0:NS, :], reads=[xnr], writes=['xs'])
        self.to_feat(xb, xbr, NS, self.xsT, 'xsT', 8 * NS, 0)

    def phase_init(self):
        S, io = self.S, self.io
        self.phase_begin()
        self.ln_alloc()
        for tt in range(self.NT):
            x32, xr = self.r_x32.next()
            xb, xbr = self.r_xb.next()
            S.dma('sp', out=x32[:], in_=io["x_prompt"][tt * 128:(tt + 1) * 128, :], writes=[xr])
            S.op('act', 'copy', xb[:], x32[:], reads=[xr], writes=[xbr])
            self.to_feat(xb, xbr, 128, self.xT, ('xT', tt), 8 * self.T, tt * 128)
        S.dma('sp', out=self.xs[:], in_=io["x_sample"], writes=['xs'])
        xb, xbr = self.r_xb.next()
        S.op('act', 'copy', xb[0:NS, :], self.xs[0:NS, :], reads=['xs'], writes=[xbr])
        self.to_feat(xb, xbr, NS, self.xsT, 'xsT', 8 * NS, 0)
        self.phase_end()

    def mlp_alloc(self):
        self.TB = min(1024, self.T)
        self.acc = self.sb("mlp_acc", [128, self.TB // 128, D], F32)
        self.accs = self.sb("mlp_accs", [NS, D], F32)
        self.r_w1 = self.ring("w1_", [128, 8, 512], BF16, 3)
        self.r_w2 = self.ring("w2_", [128, 4, 1024], BF16, 3)
        self.r_hT = self.ring("hT_", [128, 4, self.TB], BF16, 2)
        self.r_hs = self.ring("hs_", [128, 4, NS], BF16, 2)
        self.r_relu = self.ring("relu_", [128, 512], F32, 3)

    def phase_mlp(self, layer, first=False):
        self.phase_begin()
        self.ln_alloc()
        self.mlp_alloc()
        self.load_ln_params(layer, 'ffn')
        S, io, T, TB = self.S, self.io, self.T, self.TB
        nblk = T // TB
        ntile = TB // 128
        w1d = io["mlp_w1"][layer].rearrange("(k q) c -> q k c", q=128)
        w2d = io["mlp_w2"][layer]
        pieces = [(b, p) for b in range(nblk) for p in range(8)]
        loaded = {}

        def issue(idx):
            b, p = pieces[idx]
            w1, w1r = self.r_w1.next()
            w2, w2r = self.r_w2.next()
            S.dma('pool', out=w1[:], in_=w1d[:, :, p * 512:(p + 1) * 512], writes=[w1r])
            S.dma('pool', out=w2[:], in_=w2d[p * 512:(p + 1) * 512, :].rearrange("(c q) n -> q c n", q=128), writes=[w2r])
            loaded[idx] = (w1, w1r, w2, w2r)

        hts = {}

        def w1_part(idx):
            b, p = pieces[idx]
            w1, w1r, w2, w2r = loaded[idx]
            tok0 = b * TB
            hT, hTr = self.r_hT.next()
            hs, hsr = (self.r_hs.next() if b == 0 else (None, None))
            hts[idx] = (hT, hTr, hs, hsr)
            for c in range(4):
                for ts in range(TB // 512):
                    ps, pr = self.ps()
                    for k in range(8):
                        S.op('pe', 'matmul', ps[:, 0:512], lhsT=w1[:, k, c * 128:(c + 1) * 128],
                             rhs=self.xT[:, k, tok0 + ts * 512: tok0 + (ts + 1) * 512], start=(k == 0), stop=(k == 7),
                             reads=[w1r] + [('xT', (tok0 + ts * 512) // 128 + j) for j in range(4)], writes=[pr], signal=(k == 7))
                    rl, rlr = self.r_relu.next()
                    S.op('act', 'activation', out=rl[:], in_=ps[:, 0:512], func=AF.Relu, reads=[pr], writes=[rlr])
                    S.op('pool', 'tensor_tensor', out=hT[:, c, ts * 512:(ts + 1) * 512], in0=rl[:], in1=rl[:], op=ALU.mult,
                         reads=[rlr], writes=[(hTr, ts)])
            if b == 0:
                ps, pr = self.ps()
                for c in range(4):
                    for k in range(8):
                        S.op('pe', 'matmul', ps[:, c * NS:(c + 1) * NS], lhsT=w1[:, k, c * 128:(c + 1) * 128], rhs=self.xsT[:, k, :],
                             start=(k == 0), stop=(k == 7), reads=[w1r, 'xsT'], writes=[pr], signal=(c == 3 and k == 7))
                rl, rlr = self.r_relu.next()
                S.op('act', 'activation', out=rl[:, 0:4 * NS], in_=ps[:, 0:4 * NS], func=AF.Relu, reads=[pr], writes=[rlr])
                S.op('pool', 'tensor_tensor', out=hs[:].rearrange("p c n -> p (c n)"), in0=rl[:, 0:4 * NS], in1=rl[:, 0:4 * NS],
                     op=ALU.mult, reads=[rlr], writes=[hsr])

        def w2_part(idx):
            b, p = pieces[idx]
            w1, w1r, w2, w2r = loaded.pop(idx)
            hT, hTr, hs, hsr = hts.pop(idx)
            if b == 0:
                for hf in range(2):
                    pp, ppr = self.ps()
                    for c in range(4):
                        S.op('pe', 'matmul', pp[0:NS, 0:512], lhsT=hs[:, c, :], rhs=w2[:, c, hf * 512:(hf + 1) * 512],
                             start=(c == 0), stop=(c == 3), reads=[hsr, w2r], writes=[ppr], signal=(c == 3))
                    dst = self.accs[0:NS, hf * 512:(hf + 1) * 512]
                    if p == 0:
                        S.op('act', 'copy', dst, pp[0:NS, 0:512], reads=[ppr], writes=[('accs', hf)])
                    else:
                        S.op('dve', 'tensor_tensor', out=dst, in0=dst, in1=pp[0:NS, 0:512], op=ALU.add,
                             reads=[ppr, ('accs', hf)], writes=[('accs', hf)])
            for t in range(ntile):
                for hf in range(2):
                    pp, ppr = self.ps()
                    for c in range(4):
                        S.op('pe', 'matmul', pp[:, 0:512], lhsT=hT[:, c, t * 128:(t + 1) * 128], rhs=w2[:, c, hf * 512:(hf + 1) * 512],
                             start=(c == 0), stop=(c == 3), reads=[(hTr, t // 4), w2r], writes=[ppr], signal=(c == 3))
                    dst = self.acc[:, t, hf * 512:(hf + 1) * 512]
                    if p == 0:
                        S.op('act', 'copy', dst, pp[:, 0:512], reads=[ppr], writes=[('acc', t, hf)])
                    else:
                        S.op('dve', 'tensor_tensor', out=dst, in0=dst, in1=pp[:, 0:512], op=ALU.add,
                             reads=[ppr, ('acc', t, hf)], writes=[('acc', t, hf)])
            if p == 7:
                for t in range(ntile):
                    tt = b * ntile + t
                    self.ln_prompt_tile(tt, [self.acc[:, t, 0:512], self.acc[:, t, 512:1024]], [('acc', t, 0), ('acc', t, 1)],
                                        'ffn', first=first)
                if b == 0:
                    self.ln_sample([self.accs[0:NS, 0:512], self.accs[0:NS, 512:1024]], [('accs', 0), ('accs', 1)], 'ffn')

        issue(0)
        issue(1)
        for idx in range(len(pieces)):
            w1_part(idx)
            if idx >= 1:
                w2_part(idx - 1)
            if idx + 2 < len(pieces):
                issue(idx + 2)
        w2_part(len(pieces) - 1)
        self.phase_end()

    def finish(self):
        S, io = self.S, self.io
        S.dma('sp', out=io["y_sample"], in_=self.xs[:], reads=['xs'], writes=['y_sample'])
        S.final_wait('sp')
        S.emit()
        self.es.close()
        return self.nc


GROUPS = ((128, 1), (512, 4), (2048, 16))


class KBAttn(KBCore):
    def attn_alloc(self):
        T, NT, S, io = self.T, self.NT, self.S, self.io
        self.r_aw = self.ring("aw_", [128, 8, 384], BF16, 2)
        self.r_qk = self.ring("qkT_", [128, 2, T], BF16, 2)
        self.r_v = self.ring("vaug_", [128, NT, 2, 65], BF16, 2)
        self.aacc = self.sb("aacc", [65, 2, T], F32)
        self.r_rt = self.ring("rt_", [128, NT, 64], F32, 2)
        self.r_t1 = self.ring("rt1_", [128, 256], F32, 2)
        self.r_t2 = self.ring("rt2_", [128, 256], F32, 2)
        self.r_qkr = self.ring("qkr_", [128, 256], F32, 3)
        self.r_qkb = self.ring("qkb_", [128, 256], BF16, 3)
        self.r_v32 = self.ring("v32_", [128, 128], F32, 2)
        self.r_P = self.ring("P_", [128, 512], BF16, 4)
        self.amaskf = self.sb("amaskf", [128, 512], F32)
        self.amask = self.sb("amask", [128, 512], BF16)
        self.onesf = self.sb("onesf", [128, 64], F32)
        self.r_rden = self.ring("rden_", [64, 512], F32, 2)
        self.r_on = self.ring("on_", [64, 512], BF16, 3)
        self.r_qs = self.ring("qs_", [NS, 384], F32, 2)
        if "oT_d" not in io:
            self.dscr("oT_d", [8, 128, T], BF16)
            self.dscr("qkvs_d", [NS, 9, D], F32)
        S.dma('sp', out=self.amaskf[:], in_=io["c_amask"], writes=['amaskf'])
        S.op('dve', 'tensor_copy', self.amask[:], self.amaskf[:], reads=['amaskf'], writes=['amask'])
        S.op('dve', 'memset', self.onesf[:], 1.0, writes=['onesf'])
        for i in range(2):
            S.op('dve', 'memset', self.r_v.t[i][:], 1.0, writes=[((self.r_v.name, i), 'ones')])

    def tile_geom(self, g, ct):
        W, d = GROUPS[g]
        nbn = self.NT // d
        r, nb = ct // nbn, ct % nbn
        base = nb * 128 * d + r
        Weff = min(W, self.T)
        tail = (nb * 128 * d) >= self.T - Weff
        return d, r, nb, base, Weff, tail

    def attn_prod_tile(self, layer, g, hp, ct, w, wr, qk, qkr_, v, vr):
        S, io, T = self.S, self.io, self.T
        d, r, nb, base, Weff, tail = self.tile_geom(g, ct)
        W = GROUPS[g][0]
        ps, pr = self.ps()
        pv, pvr = self.ps()
        toks = list(range(base // 128, (base + 127 * d) // 128 + 1))
        for k in range(8):
            S.op('pe', 'matmul', ps[:, 0:256], lhsT=sb_ap(self.xT, 8 * T, k * T + base, [[d, 128]]), rhs=w[:, k, 0:256],
                 start=(k == 0), stop=(k == 7), reads=list(wr) + [('xT', t) for t in toks], writes=[pr], signal=(k == 7))
        for k in range(8):
            S.op('pe', 'matmul', pv[:, 0:128], lhsT=sb_ap(self.xT, 8 * T, k * T + base, [[d, 128]]), rhs=w[:, k, 256:384],
                 start=(k == 0), stop=(k == 7), reads=list(wr) + [('xT', t) for t in toks], writes=[pvr], signal=(k == 7))
        dbg = getattr(self, 'dbg', 9)
        if dbg < 1.1:
            return
        rt, rtr = self.cur_rt
        t1, t1r = self.r_t1.next()
        t2, t2r = self.r_t2.next()
        qkr, qkrr = self.r_qkr.next()
        cosb = sb_ap(rt, self.NT * 64, ct * 64, [[0, 8], [1, 32]])
        sinb = sb_ap(rt, self.NT * 64, ct * 64 + 32, [[0, 4], [1, 32]])
        S.op('dve', 'tensor_tensor', out=sb_ap(t1, 256, 0, [[32, 8], [1, 32]]), in0=sb_ap(ps, 512, 0, [[32, 8], [1, 32]]),
             in1=cosb, op=ALU.mult, reads=[pr, rtr], writes=[t1r])
        S.op('dve', 'scalar_tensor_tensor', out=sb_ap(t2, 256, 0, [[64, 4], [1, 32]]), in0=sb_ap(ps, 512, 32, [[64, 4], [1, 32]]),
             scalar=-1.0, in1=sinb, op0=ALU.mult, op1=ALU.mult, reads=[pr, rtr], writes=[(t2r, 0)])
        S.op('dve', 'tensor_tensor', out=sb_ap(t2, 256, 32, [[64, 4], [1, 32]]), in0=sb_ap(ps, 512, 0, [[64, 4], [1, 32]]),
             in1=sinb, op=ALU.mult, reads=[pr, rtr], writes=[(t2r, 1)])
        qkb, qkbr = self.r_qkb.next()
        S.op('dve', 'tensor_tensor', out=qkb[:], in0=t1[:], in1=t2[:], op=ALU.add, reads=[t1r, (t2r, 0), (t2r, 1)], writes=[qkbr])
        if tail:
            S.op('dve', 'tensor_tensor', out=qkr[:, 128:256], in0=t1[:, 128:256], in1=t2[:, 128:256], op=ALU.add,
                 reads=[t1r, (t2r, 0), (t2r, 1)], writes=[qkrr])
        if dbg < 1.2:
            return
        S.op('act', 'copy', sb_ap(v, self.NT * 130, ct * 130, [[65, 2], [1, 64]]), sb_ap(pv, 512, 0, [[64, 2], [1, 64]]),
             reads=[pvr, (vr, 'ones')], writes=[(vr, ct)])
        if tail and dbg >= 1.3:
            wname = {128: "kv128_p", 512: "kv512_p", 2048: "kv2048_p"}[W]
            row0 = base - (T - Weff)
            dst = io[wname][layer // 2].rearrange("(m s) c -> m s c", s=d)
            S.dma('sp', out=dst[row0 // d:row0 // d + 128, row0 % d, hp * 128:(hp + 1) * 128], in_=qkr[:, 128:256], reads=[qkrr],
                  writes=[(wname, layer, hp, ct, 'k')])
            v32, v32r = self.r_v32.next()
            S.op('act', 'copy', v32[:], pv[:, 0:128], reads=[pvr], writes=[v32r])
            S.dma('sp', out=dst[row0 // d:row0 // d + 128, row0 % d, 1024 + hp * 128:1024 + (hp + 1) * 128], in_=v32[:], reads=[v32r],
                  writes=[(wname, layer, hp, ct, 'v')])
        if dbg < 1.4:
            return
        pt, ptr_ = self.ps()
        ptb = pt.bitcast(BF16)
        if dbg == 1.41:
            S.op('pe', 'transpose', ptb[:, 0:128], self.identb[:], self.identb[:], reads=['identb'], writes=[ptr_], signal=False)
            S.op('pe', 'transpose', ptb[:, 128:256], self.identb[:], self.identb[:], reads=['identb'], writes=[ptr_])
            return
        if dbg == 1.42:
            S.op('pe', 'transpose', ptb[:, 0:128], qkb[:, 0:128], self.identb[:], reads=[qkbr, 'identb'], writes=[ptr_])
            return
        S.op('pe', 'transpose', ptb[:, 0:128], qkb[:, 0:128], self.identb[:], reads=[qkbr, 'identb'], writes=[ptr_], signal=False)
        S.op('pe', 'transpose', ptb[:, 128:256], qkb[:, 128:256], self.identb[:], reads=[qkbr, 'identb'], writes=[ptr_])
        if dbg < 1.5:
            return
        S.op('act', 'copy', sb_ap(qk, 2 * T, ct * 128, [[T, 2], [1, 128]]), sb_ap(ptb, 1024, 0, [[128, 2], [1, 128]]),
             reads=[ptr_], writes=[(qkr_, ct)])

    def attn_s_block(self, g, ct, qk, qkr_):
        S, T = self.S, self.T
        d, r, nb, base, Weff, tail = self.tile_geom(g, ct)
        P, Pr = self.r_P.next()
        for a in range(2):
            ps, pr = self.ps()
            js = [1] if nb == 0 else [0, 1]
            for n, j in enumerate(js):
                kt = ct - 1 if j == 0 else ct
                S.op('pe', 'matmul', ps[:, j * 128:(j + 1) * 128],
                     lhsT=sb_ap(qk, 2 * T, T + kt * 128, [[1, 128]], p0=a * 64, np_=64),
                     rhs=sb_ap(qk, 2 * T, ct * 128, [[1, 128]], p0=a * 64, np_=64), start=True, stop=True,
                     reads=[(qkr_, kt), (qkr_, ct)], writes=[pr], signal=(n == len(js) - 1))
            c0 = 128 if nb == 0 else 0
            S.op('act', 'activation', out=P[:, a * 256 + c0:(a + 1) * 256], in_=ps[:, c0:256], func=AF.Exp, scale=SCALE,
                 reads=[pr], writes=[(Pr, a)])
            S.op('dve', 'tensor_tensor', out=P[:, a * 256 + c0:(a + 1) * 256], in0=P[:, a * 256 + c0:(a + 1) * 256],
                 in1=self.amask[:, a * 256 + c0:(a + 1) * 256], op=ALU.mult, reads=[(Pr, a), 'amask'], writes=[(Pr, a)])
        return P, Pr

    def attn_pv_block(self, g, ct, P, Pr, v, vr):
        S, T = self.S, self.T
        d, r, nb, base, Weff, tail = self.tile_geom(g, ct)
        ps, pr = self.ps()
        for a in range(2):
            js = [1] if nb == 0 else [0, 1]
            for n, j in enumerate(js):
                kt = ct - 1 if j == 0 else ct
                S.op('pe', 'matmul', ps[0:65, a * 128:(a + 1) * 128], lhsT=sb_ap(v, self.NT * 130, kt * 130 + a * 65, [[1, 65]]),
                     rhs=P[:, (a * 2 + j) * 128:(a * 2 + j + 1) * 128], start=(n == 0), stop=(n == len(js) - 1),
                     reads=[(vr, kt), (Pr, a)], writes=[pr], signal=(a == 1 and n == len(js) - 1))
        dst = sb_ap(self.aacc, 2 * T, base, [[T, 2], [d, 128]], np_=65)
        src = sb_ap(ps, 512, 0, [[128, 2], [1, 128]], np_=65)
        accres = [('aacc', t) for t in range(base // 128, (base + 127 * d) // 128 + 1)]
        if g == 0:
            S.op('dve', 'tensor_copy', dst, src, reads=[pr], writes=accres)
        else:
            S.op('dve', 'tensor_tensor', out=dst, in0=dst, in1=src, op=ALU.add, reads=[pr] + accres, writes=accres)

    def attn_norm_hp(self, hp):
        S, io, T = self.S, self.io, self.T
        for a in range(2):
            for c in range(T // 512):
                ps, pr = self.ps()
                accres = [('aacc', c * 4 + j) for j in range(4)]
                S.op('pe', 'matmul', ps[0:64, 0:512], lhsT=self.onesf[64:65, 0:64],
                     rhs=sb_ap(self.aacc, 2 * T, a * T + c * 512, [[1, 512]], p0=64, np_=1), start=True, stop=True,
                     reads=['onesf'] + accres, writes=[pr])
                rden, rdr = self.r_rden.next()
                S.op('dve', 'reciprocal', rden[:], ps[0:64, 0:512], reads=[pr], writes=[rdr])
                on, onr = self.r_on.next()
                S.op('dve', 'tensor_tensor', out=on[:], in0=sb_ap(self.aacc, 2 * T, a * T + c * 512, [[1, 512]], np_=64), in1=rden[:],
                     op=ALU.mult, reads=[rdr] + accres, writes=[onr])
                S.dma('sp', out=io["oT_d"][hp, a * 64:(a + 1) * 64, c * 512:(c + 1) * 512], in_=on[:], reads=[onr], writes=[('oT_d', hp, a, c)])

    def phase_attn(self, layer, first=False):
        S, io, T, NT = self.S, self.io, self.T, self.NT
        j = layer // 2
        self.phase_begin()
        self.attn_alloc()
        wind = io["attn_w_in"][j]
        steps = [(hp, g) for hp in range(8) for g in range(3)]
        import os
        if os.environ.get("NSTEPS"):
            steps = steps[:int(os.environ["NSTEPS"])]
        wl = {}

        def issue_w(si):
            hp, g = steps[si]
            w, wr = self.r_aw.next()
            for x in range(3):
                c0 = g * 3072 + x * 1024 + hp * 128
                S.dma('pool', out=w[:, :, x * 128:(x + 1) * 128], in_=wind[:, c0:c0 + 128].rearrange("(k q) c -> q k c", q=128),
                      writes=[(wr, x)])
            wl[si] = (w, [(wr, 0), (wr, 1), (wr, 2)])

        issue_w(0)
        bufs = {}
        LAG = 2
        for si in range(len(steps) + 1):
            if si < len(steps):
                hp, g = steps[si]
                w, wres = wl.pop(si)
                qk, qkr_ = self.r_qk.next()
                v, vr = self.r_v.next()
                bufs[si] = (qk, qkr_, v, vr)
                rt, rtr = self.r_rt.next()
                S.dma('sp', out=rt[:], in_=io["c_rope"][g].rearrange("t p c -> p t c"), writes=[rtr])
                self.cur_rt = (rt, rtr)
                ps, pr = self.ps()
                for k in range(8):
                    S.op('pe', 'matmul', ps[0:NS, 0:384], lhsT=self.xsT[:, k, :], rhs=w[:, k, :], start=(k == 0), stop=(k == 7),
                         reads=wres + ['xsT'], writes=[pr], signal=(k == 7))
                qs, qsr = self.r_qs.next()
                S.op('act', 'copy', qs[:], ps[0:NS, 0:384], reads=[pr], writes=[qsr])
                S.dma('sp', out=io["qkvs_d"][:, g * 3:(g + 1) * 3, hp * 128:(hp + 1) * 128], in_=qs[:].rearrange("p (x c) -> p x c", x=3),
                      reads=[qsr], writes=[('qkvs_d', g, hp)])
            if si + 1 < len(steps):
                issue_w(si + 1)
            pend = []
            for i in range(NT + LAG):
                if si < len(steps) and i < NT:
                    self.attn_prod_tile(layer, g, hp, i, w, wres, qk, qkr_, v, vr)
                if si >= 1:
                    php, pg = steps[si - 1]
                    pqk, pqkr, pv, pvr = bufs[si - 1]
                    if getattr(self, 'dbg', 9) < 2:
                        continue
                    if i < NT:
                        pend.append((i,) + self.attn_s_block(pg, i, pqk, pqkr))
                    if i >= LAG:
                        ct, P, Pr = pend.pop(0)
                        self.attn_pv_block(pg, ct, P, Pr, pv, pvr)
            if si >= 1:
                php, pg = steps[si - 1]
                del bufs[si - 1]
                if pg == 2 and getattr(self, 'dbg', 9) >= 3:
                    self.attn_norm_hp(php)
        self.phase_end()
        self.phase_begin()
        self.ln_alloc()
        self.load_ln_params(layer, 'mix')
        self.wo = self.sb("wo", [128, 8, D], BF16)
        self.r_oTt = self.ring("oTt_", [128, 8, 128], BF16, 3)
        S.dma('pool', out=self.wo[:], in_=io["attn_w_out"][j].rearrange("(k q) n -> q k n", q=128), writes=['wo'])
        oTd = io["oT_d"]
        for tt in range(NT if getattr(self, 'dbg', 9) >= 4 else 0):
            oTt, oTr = self.r_oTt.next()
            S.dma('sp', out=oTt[:], in_=oTd[:, :, tt * 128:(tt + 1) * 128].rearrange("hp q t -> q hp t"),
                  reads=[('oT_d', hp, a, tt // 4) for hp in range(8) for a in range(2)], writes=[oTr])
            hps = []
            for hf in range(2):
                ps, pr = self.ps()
                for k in range(8):
                    S.op('pe', 'matmul', ps[:, 0:512], lhsT=oTt[:, k, :], rhs=self.wo[:, k, hf * 512:(hf + 1) * 512],
                         start=(k == 0), stop=(k == 7), reads=[oTr, 'wo'], writes=[pr], signal=(k == 7))
                hps.append((ps, pr))
            self.ln_prompt_tile(tt, [hps[0][0][:, 0:512], hps[1][0][:, 0:512]], [hps[0][1], hps[1][1]], 'mix', first=first)
        if hasattr(self, 'sample_attn') and getattr(self, 'do_sample', True):
            self.sample_attn(layer, self.wo)
        self.phase_end()


class KBSsd(KBAttn):
    def phase_ssd(self, layer, first=False):
        S, io, T, NT = self.S, self.io, self.T, self.NT
        j = layer // 2
        win = io["ssm_w_in"][j]
        if "yn_d" not in io:
            self.dscr("yn_d", [T, 2048], BF16)
            self.dscr("sproj_d", [NS, 6176], F32)
        self.phase_begin()
        tri = self.sb("tri", [128, 128], F32)
        ust = self.sb("ust", [128, 128], F32)
        ones = self.sb("ones", [128, 128], F32)
        S.dma('sp', out=tri[:], in_=io["c_tri"], writes=['tri'])
        S.dma('sp', out=ust[:], in_=io["c_ustrict"], writes=['ust'])
        S.op('dve', 'memset', ones[:], 1.0, writes=['ones'])
        dtt = self.sb("dtt", [128, NT, 32], F32)
        dta = self.sb("dta", [128, NT, 32], F32)
        ea = self.sb("ea", [128, NT, 32], F32)
        el = self.sb("el", [128, NT, 32], F32)
        dte = self.sb("dte", [128, NT, 32], F32)
        hp_ = self.sb("hpar", [128, 4, 32], F32)
        cwT = self.sb("cwT", [128, 32, 5], F32)
        self.dts = self.sb("dts", [NS, 64], F32)
        for i, nm in enumerate(["ssm_dt_bias", "ssm_a_log", "ssm_d"]):
            S.dma('sp', out=hp_[:, i, :], in_=io[nm][j:j + 1, :].partition_broadcast(128), writes=[('hpar', i)])
        S.op('act', 'activation', out=hp_[:, 1, :], in_=hp_[:, 1, :], func=AF.Exp, reads=[('hpar', 1)], writes=[('hpar', 1)])
        S.op('dve', 'tensor_scalar', out=hp_[:, 1, :], in0=hp_[:, 1, :], scalar1=-1.0, scalar2=None, op0=ALU.mult,
             reads=[('hpar', 1)], writes=[('hpar', 1)])
        wdt = self.sb("wdt", [128, 8, 32], BF16)
        tmpa = self.ring("tmpa", [128, 512], F32, 2)
        self.pes2 = contextlib.ExitStack()
        cw = self.pes2.enter_context(self.nc.sbuf_tensor(self._uniq("cw"), [5, 4096], F32))
        S.dma('sp', out=cw[0:4, :], in_=io["ssm_conv_w"][j], writes=[('cw', 0)])
        S.dma('sp', out=cw[4:5, :], in_=io["ssm_conv_b"][j:j + 1, :], writes=[('cw', 1)])
        for q in range(8):
            ps, pr = self.ps()
            for c in range(4):
                ch = q * 4 + c
                S.op('pe', 'matmul', ps[:, c * 8:c * 8 + 5], lhsT=cw[0:5, ch * 128:(ch + 1) * 128], rhs=self.identf[0:5, 0:5],
                     start=True, stop=True, reads=[('cw', 0), ('cw', 1), 'identf'], writes=[pr], signal=(c == 3))
            S.op('act', 'copy', cwT[:, q * 4:(q + 1) * 4, :], sb_ap(ps, 512, 0, [[8, 4], [1, 5]]), reads=[pr], writes=[('cwT', q)])
        S.dma('pool', out=wdt[:], in_=win[:, 6144:6176].rearrange("(k q) c -> q k c", q=128), writes=['wdt'])
        for b16 in range((NT + 15) // 16):
            nt_ = min(16, NT - b16 * 16)
            ps, pr = self.ps()
            for t in range(nt_):
                tt = b16 * 16 + t
                for k in range(8):
                    S.op('pe', 'matmul', ps[:, t * 32:(t + 1) * 32], lhsT=self.xT[:, k, tt * 128:(tt + 1) * 128], rhs=wdt[:, k, :],
                         start=(k == 0), stop=(k == 7), reads=['wdt', ('xT', tt)], writes=[pr], signal=(t == nt_ - 1 and k == 7))
            sl = slice(b16 * 16, b16 * 16 + nt_)
            S.op('dve', 'tensor_tensor', out=dtt[:, sl, :], in0=sb_ap(ps, 512, 0, [[32, nt_], [1, 32]]),
                 in1=sb_ap(hp_, 128, 0, [[0, nt_], [1, 32]]), op=ALU.add, reads=[pr, ('hpar', 0)], writes=[('dtt', b16)])
            S.op('act', 'activation', out=dtt[:, sl, :], in_=dtt[:, sl, :], func=AF.Exp, reads=[('dtt', b16)], writes=[('dtt', b16)])
            S.op('dve', 'tensor_scalar', out=dtt[:, sl, :], in0=dtt[:, sl, :], scalar1=1.0, scalar2=None, op0=ALU.add,
                 reads=[('dtt', b16)], writes=[('dtt', b16)])
            S.op('act', 'activation', out=dtt[:, sl, :], in_=dtt[:, sl, :], func=AF.Ln, reads=[('dtt', b16)], writes=[('dtt', b16)])
            S.op('dve', 'tensor_tensor', out=dta[:, sl, :], in0=dtt[:, sl, :], in1=sb_ap(hp_, 128, 32, [[0, nt_], [1, 32]]), op=ALU.mult,
                 reads=[('dtt', b16), ('hpar', 1)], writes=[('dta', b16)])
        ps, pr = self.ps()
        for k in range(8):
            S.op('pe', 'matmul', ps[0:NS, 0:32], lhsT=self.xsT[:, k, :], rhs=wdt[:, k, :], start=(k == 0), stop=(k == 7),
                 reads=['wdt', 'xsT'], writes=[pr], signal=(k == 7))
        S.op('act', 'copy', self.dts[0:NS, 0:32], ps[0:NS, 0:32], reads=[pr], writes=['dts'])
        S.dma('sp', out=io["sproj_d"][:, 6144:6176], in_=self.dts[0:NS, 0:32], reads=['dts'], writes=[('sproj_d', 'dt')])
        for b8 in range((NT + 7) // 8):
            nt_ = min(8, NT - b8 * 8)
            ps, pr = self.ps()
            for t in range(nt_):
                tt = b8 * 8 + t
                S.op('pe', 'matmul', ps[:, t * 64:t * 64 + 32], lhsT=tri[:], rhs=dta[:, tt, :], start=True, stop=True,
                     reads=['tri', ('dta', tt // 16)], writes=[pr], signal=False)
                S.op('pe', 'matmul', ps[:, t * 64 + 32:t * 64 + 64], lhsT=ones[:], rhs=dta[:, tt, :], start=True, stop=True,
                     reads=['ones', ('dta', tt // 16)], writes=[pr], signal=(t == nt_ - 1))
            ta, tar = tmpa.next()
            S.op('act', 'copy', ta[:, 0:nt_ * 64], ps[:, 0:nt_ * 64], reads=[pr], writes=[tar])
            sl = slice(b8 * 8, b8 * 8 + nt_)
            cum = sb_ap(ta, 512, 0, [[64, nt_], [1, 32]])
            tot = sb_ap(ta, 512, 32, [[64, nt_], [1, 32]])
            S.op('act', 'activation', out=ea[:, sl, :], in_=cum, func=AF.Exp, reads=[tar], writes=[('ea', b8)])
            S.op('act', 'activation', out=el[:, sl, :], in_=tot, func=AF.Exp, reads=[tar], writes=[('el', b8)])
            S.op('dve', 'tensor_tensor', out=dte[:, sl, :], in0=tot, in1=cum, op=ALU.subtract, reads=[tar], writes=[('dte', b8)])
            S.op('act', 'activation', out=dte[:, sl, :], in_=dte[:, sl, :], func=AF.Exp, reads=[('dte', b8)], writes=[('dte', b8)])
            S.op('dve', 'tensor_tensor', out=dte[:, sl, :], in0=dte[:, sl, :], in1=dtt[:, sl, :], op=ALU.mult,
                 reads=[('dte', b8), ('dtt', b8 // 2)], writes=[('dte', b8)])
        S.barrier()
        self.pes2.close()
        r_w = self.ring("sw_", [128, 8, 768], BF16, 2)
        pre = [self.sb("pre%d" % c, [128, 515], F32) for c in range(4)]
        r_cacc = self.ring("cacc_", [128, 512], F32, 2)
        r_xc = [self.ring("xc%d_" % c, [128, 512], BF16, 2) for c in range(4)]
        r_xtok = self.ring("xtok_", [128, 384], BF16, 3)
        r_sz = self.ring("sz_", [128, 256], F32, 4)
        hst = self.sb("hst", [128, 256], F32)
        hbf = self.sb("hbf", [128, 256], BF16)
        r_R = self.ring("R_", [128, 512], F32, 2)
        r_E = self.ring("E_", [128, 512], BF16, 2)
        r_MT = self.ring("MT_", [128, 512], BF16, 3)
        r_CB = self.ring("CB_", [128, 128], BF16, 2)
        r_xdt = self.ring("xdt_", [128, 512], BF16, 3)
        r_y = self.ring("y_", [128, 512], F32, 3)
        r_yb = self.ring("yb_", [128, 256], BF16, 3)
        r_sm = self.ring("sm_", [128, 8], F32, 3)
        gpar = self.ring("gpar_", [128, 512], F32, 2)
        r_fs = self.ring("fs_", [128, 128], F32, 2)
        r_cp = self.ring("cp_", [3, 512], F32, 2)
        nblk = T // 512

        def issue_w(g):
            w, wr = r_w.next()
            for x, (c0, n) in enumerate([(g * 256, 256), (2048 + g * 256, 256), (4096 + g * 128, 128), (5120 + g * 128, 128)]):
                o0 = [0, 256, 512, 640][x]
                S.dma('pool', out=w[:, :, o0:o0 + n], in_=win[:, c0:c0 + n].rearrange("(k q) c -> q k c", q=128), writes=[(wr, x)])
            return w, [(wr, x) for x in range(4)]

        pending = []

        def stage_a(g, tb, c4, xcs, w, wres, gp, gpr):
            BT, BTr = xcs[2]
            CT, CTr = xcs[3]
            tt = tb * 4 + c4
            cs = slice(c4 * 128, (c4 + 1) * 128)
            pt, ptr_ = self.ps()
            ptb = pt.bitcast(BF16)
            for i, (src, srcr) in enumerate([xcs[0], xcs[1], xcs[2]]):
                S.op('pe', 'transpose', ptb[:, i * 128:(i + 1) * 128], src[:, cs], self.identb[:], reads=[srcr, 'identb'],
                     writes=[ptr_], signal=(i == 2))
            xtok, xtr = r_xtok.next()
            S.op('act', 'copy', xtok[:], ptb[:, 0:384], reads=[ptr_], writes=[xtr])
            pz, pzr = self.ps()
            for k in range(8):
                S.op('pe', 'matmul', pz[:, 0:256], lhsT=self.xT[:, k, tt * 128:(tt + 1) * 128], rhs=w[:, k, 0:256],
                     start=(k == 0), stop=(k == 7), reads=wres + [('xT', tt)], writes=[pzr], signal=(k == 7))
            sz, szr = r_sz.next()
            S.op('act', 'activation', out=sz[:], in_=pz[:, 0:256], func=AF.Silu, reads=[pzr], writes=[szr])
            pcb, pcbr = self.ps()
            S.op('pe', 'matmul', pcb[:, 0:128], lhsT=BT[:, cs], rhs=CT[:, cs], start=True, stop=True, reads=[BTr, CTr], writes=[pcbr])
            CB, CBr = r_CB.next()
            S.op('dve', 'tensor_tensor', out=CB[:], in0=pcb[:, 0:128], in1=tri[:], op=ALU.mult, reads=[pcbr, 'tri'], writes=[CBr])
            R, Rr = r_R.next()
            S.op('pool', 'tensor_tensor', out=sb_ap(R, 512, 0, [[128, 4], [1, 128]]), in0=sb_ap(tri, 128, 0, [[0, 4], [1, 128]]),
                 in1=sb_ap(dta, NT * 32, tt * 32 + g * 4, [[1, 4], [0, 128]]), op=ALU.mult, reads=['tri', ('dta', tt // 16)], writes=[Rr])
            pd, pdr = self.ps()
            S.op('pe', 'matmul', pd[:, 0:512], lhsT=ust[:], rhs=R[:], start=True, stop=True, reads=['ust', Rr], writes=[pdr])
            E, Er = r_E.next()
            S.op('act', 'activation', out=E[:], in_=pd[:, 0:512], func=AF.Exp, reads=[pdr], writes=[Er])
            MT, MTr = r_MT.next()
            S.op('dve', 'tensor_tensor', out=sb_ap(MT, 512, 0, [[128, 4], [1, 128]]), in0=sb_ap(E, 512, 0, [[128, 4], [1, 128]]),
                 in1=sb_ap(CB, 128, 0, [[0, 4], [1, 128]]), op=ALU.mult, reads=[Er, CBr], writes=[MTr])
            xdt, xdr = r_xdt.next()
            S.op('pool', 'tensor_tensor', out=sb_ap(xdt, 512, 0, [[64, 4], [1, 64]]), in0=sb_ap(xtok, 384, 0, [[64, 4], [1, 64]]),
                 in1=sb_ap(dtt, NT * 32, tt * 32 + g * 4, [[1, 4], [0, 64]]), op=ALU.mult, reads=[xtr, ('dtt', tt // 16)], writes=[(xdr, 0)])
            S.op('pool', 'tensor_tensor', out=sb_ap(xdt, 512, 256, [[64, 4], [1, 64]]), in0=sb_ap(xtok, 384, 0, [[64, 4], [1, 64]]),
                 in1=sb_ap(dte, NT * 32, tt * 32 + g * 4, [[1, 4], [0, 64]]), op=ALU.mult, reads=[xtr, ('dte', tt // 8)], writes=[(xdr, 1)])
            y, yr = r_y.next()
            S.op('pool', 'tensor_tensor', out=sb_ap(y, 512, 256, [[64, 4], [1, 64]]), in0=sb_ap(xtok, 384, 0, [[64, 4], [1, 64]]),
                 in1=sb_ap(hp_, 128, 64 + g * 4, [[1, 4], [0, 64]]), op=ALU.mult, reads=[xtr, ('hpar', 2)], writes=[(yr, 'd')])
            return (g, tt, cs, CT, CTr, xtok, xtr, sz, szr, MT, MTr, xdt, xdr, y, yr, gp, gpr)

        def stage_b(g, tt, cs, CT, CTr, xtok, xtr, sz, szr, MT, MTr, xdt, xdr, y, yr, gp, gpr):
            py, pyr = self.ps()
            for h in range(4):
                S.op('pe', 'matmul', py[:, h * 64:(h + 1) * 64], lhsT=MT[:, h * 128:(h + 1) * 128], rhs=xdt[:, h * 64:(h + 1) * 64],
                     start=True, stop=True, reads=[MTr, (xdr, 0)], writes=[pyr], signal=False)
            S.op('pe', 'matmul', py[:, 256:512], lhsT=CT[:, cs], rhs=hbf[:], start=True, stop=True, reads=[CTr, 'hbf'], writes=[pyr])
            pst, pstr = self.ps()
            S.op('pe', 'matmul', pst[:, 0:256], lhsT=xtok[:, 256:384], rhs=xdt[:, 256:512], start=True, stop=True,
                 reads=[xtr, (xdr, 1)], writes=[pstr])
            S.op('dve', 'tensor_tensor', out=sb_ap(y, 512, 0, [[64, 4], [1, 64]]), in0=sb_ap(py, 512, 256, [[64, 4], [1, 64]]),
                 in1=sb_ap(ea, NT * 32, tt * 32 + g * 4, [[1, 4], [0, 64]]), op=ALU.mult, reads=[pyr, ('ea', tt // 8)], writes=[yr])
            S.op('dve', 'tensor_tensor', out=y[:, 0:256], in0=y[:, 0:256], in1=py[:, 0:256], op=ALU.add, reads=[yr, pyr], writes=[yr])
            S.op('dve', 'tensor_tensor', out=sb_ap(hst, 256, 0, [[64, 4], [1, 64]]), in0=sb_ap(hst, 256, 0, [[64, 4], [1, 64]]),
                 in1=sb_ap(el, NT * 32, tt * 32 + g * 4, [[1, 4], [0, 64]]), op=ALU.mult, reads=['hst', ('el', tt // 8)], writes=['hst'])
            S.op('dve', 'tensor_tensor', out=hst[:], in0=hst[:], in1=pst[:, 0:256], op=ALU.add, reads=['hst', pstr], writes=['hst'])
            S.op('act', 'copy', hbf[:], hst[:], reads=['hst'], writes=['hbf'])
            S.op('dve', 'tensor_tensor', out=y[:, 0:256], in0=y[:, 0:256], in1=y[:, 256:512], op=ALU.add, reads=[yr, (yr, 'd')], writes=[yr])
            S.op('dve', 'tensor_tensor', out=y[:, 0:256], in0=y[:, 0:256], in1=sz[:], op=ALU.mult, reads=[yr, szr], writes=[yr])
            sm, smr = r_sm.next()
            S.op('pool', 'tensor_tensor', out=y[:, 256:512], in0=y[:, 0:256], in1=y[:, 0:256], op=ALU.mult,
                 reads=[yr, (yr, 'd')], writes=[(yr, 'd')])
            S.op('dve', 'tensor_reduce', out=sm[:, 0:1], in_=y[:, 256:512], axis=AX.X, op=ALU.add, reads=[(yr, 'd')], writes=[smr])
            S.op('dve', 'tensor_scalar', out=sm[:, 1:2], in0=sm[:, 0:1], scalar1=1.0 / 256, scalar2=RMS_EPS, op0=ALU.mult, op1=ALU.add,
                 reads=[smr], writes=[smr])
            S.op('act', 'activation', out=sm[:, 1:2], in_=sm[:, 1:2], func=AF.Ln, reads=[smr], writes=[smr])
            S.op('act', 'activation', out=sm[:, 2:3], in_=sm[:, 1:2], func=AF.Exp, scale=-0.5, reads=[smr], writes=[smr])
            yb, ybr = r_yb.next()
            S.op('dve', 'scalar_tensor_tensor', out=yb[:], in0=y[:, 0:256], scalar=sm[:, 2:3], in1=gp[:, 0:256], op0=ALU.mult, op1=ALU.mult,
                 reads=[yr, smr, gpr], writes=[ybr])
            S.dma('sp', out=io["yn_d"][tt * 128:(tt + 1) * 128, g * 256:(g + 1) * 256], in_=yb[:], reads=[ybr], writes=[('yn_d', tt, g)])

        wnext = issue_w(0)
        for g in range(8):
            w, wres = wnext
            if g + 1 < 8:
                wnext = issue_w(g + 1)
            gp, gpr = gpar.next()
            S.dma('sp', out=gp[:, 0:256], in_=io["ssm_norm_w"][j:j + 1, g * 256:(g + 1) * 256].partition_broadcast(128), writes=[gpr])
            S.op('dve', 'memset', hst[:], 0.0, writes=['hst'])
            S.op('dve', 'memset', hbf[:], 0.0, writes=['hbf'])
            for c in range(4):
                S.op('dve', 'memset', pre[c][:, 0:3], 0.0, writes=[('pre', c)])
            for (c0, n, d0) in [(0, 512, None), (512, 256, None)]:
                ps, pr = self.ps()
                for k in range(8):
                    S.op('pe', 'matmul', ps[0:NS, 0:n], lhsT=self.xsT[:, k, :], rhs=w[:, k, c0:c0 + n], start=(k == 0), stop=(k == 7),
                         reads=wres + ['xsT'], writes=[pr], signal=(k == 7))
                sz, szr = r_sz.next()
                if c0 == 0:
                    S.op('act', 'copy', sz[0:NS, 0:256], ps[0:NS, 0:256], reads=[pr], writes=[szr])
                    S.dma('sp', out=io["sproj_d"][:, g * 256:(g + 1) * 256], in_=sz[0:NS, 0:256], reads=[szr], writes=[('sproj_d', g, 'z')])
                    sz2, sz2r = r_sz.next()
                    S.op('act', 'copy', sz2[0:NS, 0:256], ps[0:NS, 256:512], reads=[pr], writes=[sz2r])
                    S.dma('sp', out=io["sproj_d"][:, 2048 + g * 256:2048 + (g + 1) * 256], in_=sz2[0:NS, 0:256], reads=[sz2r],
                          writes=[('sproj_d', g, 'x')])
                else:
                    S.op('act', 'copy', sz[0:NS, 0:256], ps[0:NS, 0:256], reads=[pr], writes=[szr])
                    S.dma('sp', out=io["sproj_d"][:, 4096 + g * 128:4096 + (g + 1) * 128], in_=sz[0:NS, 0:128], reads=[szr],
                          writes=[('sproj_d', g, 'B')])
                    S.dma('sp', out=io["sproj_d"][:, 5120 + g * 128:5120 + (g + 1) * 128], in_=sz[0:NS, 128:256], reads=[szr],
                          writes=[('sproj_d', g, 'C')])
            for tb in range(nblk):
                tok0 = tb * 512
                xts = [('xT', tb * 4 + i) for i in range(4)]
                xcs = []
                for c in range(4):
                    ch = [2 * g, 2 * g + 1, 16 + g, 24 + g][c]
                    ps, pr = self.ps()
                    for k in range(8):
                        S.op('pe', 'matmul', ps[:, 0:512], lhsT=w[:, k, 256 + c * 128:256 + (c + 1) * 128], rhs=self.xT[:, k, tok0:tok0 + 512],
                             start=(k == 0), stop=(k == 7), reads=wres + xts, writes=[pr], signal=(k == 7))
                    if tb > 0:
                        S.op('dve', 'tensor_copy', pre[c][:, 0:3], pre[c][:, 512:515], reads=[('pre', c)], writes=[('pre', c)])
                    S.op('act', 'copy', pre[c][:, 3:515], ps[:, 0:512], reads=[pr], writes=[('pre', c)])
                    ca, car = r_cacc.next()
                    S.op('dve', 'tensor_scalar', out=ca[:], in0=pre[c][:, 0:512], scalar1=cwT[:, ch, 0:1], scalar2=None, op0=ALU.mult,
                         reads=[('pre', c), ('cwT', ch // 4)], writes=[car])
                    for tap in range(1, 4):
                        S.op('dve', 'scalar_tensor_tensor', out=ca[:], in0=pre[c][:, tap:tap + 512], scalar=cwT[:, ch, tap:tap + 1], in1=ca[:],
                             op0=ALU.mult, op1=ALU.add, reads=[('pre', c), ('cwT', ch // 4), car], writes=[car])
                    xc, xcr = r_xc[c].next()
                    S.op('act', 'activation', out=xc[:], in_=ca[:], func=AF.Silu, bias=cwT[:, ch, 4:5], reads=[car, ('cwT', ch // 4)], writes=[xcr])
                    xcs.append((xc, xcr))
                    if tb == nblk - 1:
                        pc, pcr = self.ps()
                        S.op('pe', 'matmul', pc[0:3, 0:128], lhsT=pre[c][:, 512:515], rhs=self.identf[:], start=True, stop=True,
                             reads=[('pre', c), 'identf'], writes=[pcr])
                        cp, cpr = r_cp.next()
                        S.op('act', 'copy', cp[0:3, 0:128], pc[0:3, 0:128], reads=[pcr], writes=[cpr])
                        S.dma('sp', out=io["conv_p"][j, :, ch * 128:(ch + 1) * 128], in_=cp[0:3, 0:128], reads=[cpr], writes=[('conv_p', j, ch)])
                BT, BTr = xcs[2]
                CT, CTr = xcs[3]
                for c4 in range(4):
                    st = stage_a(g, tb, c4, xcs, w, wres, gp, gpr)
                    if pending:
                        stage_b(*pending.pop(0))
                    pending.append(st)
            while pending:
                stage_b(*pending.pop(0))
            for half in range(2):
                pf, pfr = self.ps()
                S.op('pe', 'matmul', pf[:, 0:128], lhsT=hst[:, half * 128:(half + 1) * 128], rhs=self.identf[:], start=True, stop=True,
                     reads=['hst', 'identf'], writes=[pfr])
                fs, fsr = r_fs.next()
                S.op('act', 'copy', fs[:], pf[:, 0:128], reads=[pfr], writes=[fsr])
                S.dma('sp', out=io["ssm_p"][j, g * 256 + half * 128:g * 256 + (half + 1) * 128, :], in_=fs[:], reads=[fsr],
                      writes=[('ssm_p', j, g, half)])
        self.phase_end()
        self.phase_begin()
        self.ln_alloc()
        self.load_ln_params(layer, 'mix')
        wo = self.sb("swo", [128, 16, D], BF16)
        S.dma('pool', out=wo[:, 0:8, :], in_=io["ssm_w_out"][j][0:1024, :].rearrange("(k q) n -> q k n", q=128), writes=[('swo', 0)])
        S.dma('pool', out=wo[:, 8:16, :], in_=io["ssm_w_out"][j][1024:2048, :].rearrange("(k q) n -> q k n", q=128), writes=[('swo', 1)])
        outer = self.pes
        self.pes = contextlib.ExitStack()
        r_yt = self.ring("yt_", [128, 2048], BF16, 2)
        r_ynT = self.ring("ynT_", [128, 16, 128], BF16, 2)
        for tt in range(NT):
            yt, ytr = r_yt.next()
            S.dma('sp', out=yt[:], in_=io["yn_d"][tt * 128:(tt + 1) * 128, :], reads=[('yn_d', tt, g) for g in range(8)], writes=[ytr])
            ynT, ynTr = r_ynT.next()
            for hf in range(2):
                ps, pr = self.ps()
                psb = ps.bitcast(BF16)
                for k in range(8):
                    kk = hf * 8 + k
                    S.op('pe', 'transpose', psb[:, k * 128:(k + 1) * 128], yt[:, kk * 128:(kk + 1) * 128], self.identb[:],
                         reads=[ytr, 'identb'], writes=[pr], signal=(k == 7))
                S.op('act', 'copy', ynT[:, hf * 8:(hf + 1) * 8, :], sb_ap(psb, 1024, 0, [[128, 8], [1, 128]]), reads=[pr], writes=[(ynTr, hf)])
            hps = []
            for hf in range(2):
                ps, pr = self.ps()
                for k in range(16):
                    S.op('pe', 'matmul', ps[:, 0:512], lhsT=ynT[:, k, :], rhs=wo[:, k, hf * 512:(hf + 1) * 512], start=(k == 0), stop=(k == 15),
                         reads=[(ynTr, 0), (ynTr, 1), ('swo', 0), ('swo', 1)], writes=[pr], signal=(k == 15))
                hps.append((ps, pr))
            self.ln_prompt_tile(tt, [hps[0][0][:, 0:512], hps[1][0][:, 0:512]], [hps[0][1], hps[1][1]], 'mix', first=first)
        S.barrier()
        self.pes.close()
        self.pes = outer
        if hasattr(self, 'sample_ssd') and getattr(self, 'do_sample', True):
            self.sample_ssd(layer, wo)
        self.phase_end()


class KBFull(KBSsd):
    def sample_attn(self, layer, wo):
        S, io = self.S, self.io
        j = layer // 2
        sel = self.sb("sel", [NS, NS * 128], F32)
        ocol = self.sb("ocol", [128, NS * NS], F32)
        rts = self.sb("rts", [NS, 64], F32)
        S.dma('sp', out=sel[:], in_=io["c_sel"], writes=['sel'])
        S.dma('sp', out=ocol[:], in_=io["c_onescol"], writes=['ocol'])
        S.dma('sp', out=rts[:], in_=io["c_rope_s"].partition_broadcast(NS), writes=['rts'])
        r_qkv = self.ring("sqkv_", [NS, 3, D], F32, 2)
        r_rp = self.ring("srp_", [NS, 2, D], F32, 2)
        r_t = self.ring("st1_", [NS, 2, D], F32, 2)
        r_kv = self.ring("skv_", [128, 2048], F32, 2)
        r_pr = self.ring("spr_", [128, D], F32, 2)
        r_s = self.ring("ssc_", [128, 64], F32, 3)
        nacc = self.sb("nacc", [NS, D + 16], F32)
        cnames = ["cache_kv_w128", "cache_kv_w512", "cache_kv_w2048"]
        onames = ["kv128_s", "kv512_s", "kv2048_s"]
        first = True
        for g, (W, dil) in enumerate(GROUPS):
            qkv, qr = r_qkv.next()
            S.dma('sp', out=qkv[:], in_=io["qkvs_d"][:, g * 3:(g + 1) * 3, :], reads=[('qkvs_d', g, hp) for hp in range(8)], writes=[qr])
            rp, rpr = r_rp.next()
            t2, t2r = r_t.next()
            S.op('dve', 'tensor_tensor', out=sb_ap(rp, 2 * D, 0, [[32, 64], [1, 32]], np_=NS), in0=sb_ap(qkv, 3 * D, 0, [[32, 64], [1, 32]], np_=NS),
                 in1=sb_ap(rts, 64, 0, [[0, 64], [1, 32]], np_=NS), op=ALU.mult, reads=[qr, 'rts'], writes=[rpr])
            S.op('dve', 'scalar_tensor_tensor', out=sb_ap(t2, 2 * D, 0, [[64, 32], [1, 32]], np_=NS), in0=sb_ap(qkv, 3 * D, 32, [[64, 32], [1, 32]], np_=NS),
                 scalar=-1.0, in1=sb_ap(rts, 64, 32, [[0, 32], [1, 32]], np_=NS), op0=ALU.mult, op1=ALU.mult, reads=[qr, 'rts'], writes=[(t2r, 0)])
            S.op('dve', 'tensor_tensor', out=sb_ap(t2, 2 * D, 32, [[64, 32], [1, 32]], np_=NS), in0=sb_ap(qkv, 3 * D, 0, [[64, 32], [1, 32]], np_=NS),
                 in1=sb_ap(rts, 64, 32, [[0, 32], [1, 32]], np_=NS), op=ALU.mult, reads=[qr, 'rts'], writes=[(t2r, 1)])
            S.op('dve', 'tensor_tensor', out=rp[:], in0=rp[:], in1=t2[:], op=ALU.add, reads=[rpr, (t2r, 0), (t2r, 1)], writes=[rpr])
            S.dma('sp', out=io[onames[g]][j, :, W - 1, 0:D], in_=rp[:, 1, :], reads=[rpr], writes=[(onames[g], j, 'k')])
            S.dma('sp', out=io[onames[g]][j, :, W - 1, D:2 * D], in_=qkv[:, 2, :], reads=[qr], writes=[(onames[g], j, 'v')])
            sc, scr = r_s.next()
            t1, t1r = r_t.next()
            S.op('dve', 'tensor_tensor', out=t1[:, 0, :], in0=rp[:, 0, :], in1=rp[:, 1, :], op=ALU.mult, reads=[rpr], writes=[t1r])
            S.op('dve', 'tensor_reduce', out=sc[0:NS, 0:16], in_=sb_ap(t1, 2 * D, 0, [[64, 16], [1, 64]], np_=NS), axis=AX.X, op=ALU.add,
                 reads=[t1r], writes=[scr])
            S.op('act', 'activation', out=sc[0:NS, 16:32], in_=sc[0:NS, 0:16], func=AF.Exp, scale=SCALE, reads=[scr], writes=[scr])
            S.op('dve', 'tensor_tensor', out=sb_ap(t1, 2 * D, D, [[64, 16], [1, 64]], np_=NS), in0=sb_ap(qkv, 3 * D, 2 * D, [[64, 16], [1, 64]], np_=NS),
                 in1=sb_ap(sc, 64, 16, [[1, 16], [0, 64]], np_=NS), op=ALU.mult, reads=[qr, scr, t1r], writes=[t1r])
            if first:
                S.op('dve', 'tensor_copy', nacc[:, 0:D], t1[:, 1, :], reads=[t1r], writes=['nacc'])
                S.op('dve', 'tensor_copy', nacc[:, D:D + 16], sc[0:NS, 16:32], reads=[scr, 'nacc'], writes=['nacc'])
                first = False
            else:
                S.op('dve', 'tensor_tensor', out=nacc[:, 0:D], in0=nacc[:, 0:D], in1=t1[:, 1, :], op=ALU.add, reads=[t1r, 'nacc'], writes=['nacc'])
                S.op('dve', 'tensor_tensor', out=nacc[:, D:D + 16], in0=nacc[:, D:D + 16], in1=sc[0:NS, 16:32], op=ALU.add,
                     reads=[scr, 'nacc'], writes=['nacc'])
            for b in range(NS):
                kv, kvr = r_kv.next()
                S.dma('sp', out=kv[:], in_=io[cnames[g]][j, b].rearrange("(m s) c -> m s c", s=dil)[:, 0, :], writes=[kvr])
                qb = []
                for hf in range(2):
                    ps, pr = self.ps()
                    S.op('pe', 'matmul', ps[:, 0:512], lhsT=sel[:, b * 128:(b + 1) * 128], rhs=rp[:, 0, hf * 512:(hf + 1) * 512], start=True, stop=True,
                         reads=['sel', rpr], writes=[pr])
                    qb.append((ps, pr))
                prd, prr = r_pr.next()
                for hf in range(2):
                    S.op('dve', 'tensor_tensor', out=prd[:, hf * 512:(hf + 1) * 512], in0=kv[:, hf * 512:(hf + 1) * 512], in1=qb[hf][0][:, 0:512],
                         op=ALU.mult, reads=[kvr, qb[hf][1]], writes=[(prr, hf)])
                s2, s2r = r_s.next()
                S.op('dve', 'tensor_reduce', out=s2[:, 0:16], in_=sb_ap(prd, D, 0, [[64, 16], [1, 64]]), axis=AX.X, op=ALU.add,
                     reads=[(prr, 0), (prr, 1)], writes=[s2r])
                S.op('act', 'activation', out=s2[:, 16:32], in_=s2[:, 0:16], func=AF.Exp, scale=SCALE, reads=[s2r], writes=[s2r])
                S.op('dve', 'tensor_tensor', out=sb_ap(prd, D, 0, [[64, 16], [1, 64]]), in0=sb_ap(kv, 2048, D, [[64, 16], [1, 64]]),
                     in1=sb_ap(s2, 64, 16, [[1, 16], [0, 64]]), op=ALU.mult, reads=[kvr, s2r, (prr, 0), (prr, 1)], writes=[(prr, 0), (prr, 1)])
                lhs = ocol[:, b * NS:(b + 1) * NS]
                for hf in range(2):
                    ps, pr = self.ps()
                    S.op('pe', 'matmul', ps[0:NS, 0:512], lhsT=lhs, rhs=prd[:, hf * 512:(hf + 1) * 512], start=True, stop=True,
                         reads=['ocol', (prr, 0), (prr, 1)], writes=[pr])
                    S.op('dve', 'tensor_tensor', out=nacc[:, hf * 512:(hf + 1) * 512], in0=nacc[:, hf * 512:(hf + 1) * 512], in1=ps[0:NS, 0:512],
                         op=ALU.add, reads=[pr, 'nacc'], writes=['nacc'])
                ps, pr = self.ps()
                S.op('pe', 'matmul', ps[0:NS, 0:16], lhsT=lhs, rhs=s2[:, 16:32], start=True, stop=True, reads=['ocol', s2r], writes=[pr])
                S.op('dve', 'tensor_tensor', out=nacc[:, D:D + 16], in0=nacc[:, D:D + 16], in1=ps[0:NS, 0:16], op=ALU.add,
                     reads=[pr, 'nacc'], writes=['nacc'])
        S.op('dve', 'reciprocal', nacc[:, D:D + 16], nacc[:, D:D + 16], reads=['nacc'], writes=['nacc'])
        xb, xbr = self.r_xb.next()
        S.op('dve', 'tensor_tensor', out=sb_ap(xb, D, 0, [[64, 16], [1, 64]], np_=NS), in0=sb_ap(nacc, D + 16, 0, [[64, 16], [1, 64]], np_=NS),
             in1=sb_ap(nacc, D + 16, D, [[1, 16], [0, 64]], np_=NS), op=ALU.mult, reads=['nacc'], writes=[xbr])
        oTs = self.sb("oTs", [128, 8, NS], BF16)
        self.to_feat(xb, xbr, NS, oTs, 'oTs', 8 * NS, 0)
        hps = []
        for hf in range(2):
            ps, pr = self.ps()
            for k in range(8):
                S.op('pe', 'matmul', ps[0:NS, 0:512], lhsT=oTs[:, k, :], rhs=wo[:, k, hf * 512:(hf + 1) * 512], start=(k == 0), stop=(k == 7),
                     reads=['oTs', 'wo'], writes=[pr], signal=(k == 7))
            hps.append((ps, pr))
        self.ln_sample([hps[0][0][0:NS, 0:512], hps[1][0][0:NS, 0:512]], [hps[0][1], hps[1][1]], 'mix')

    def sample_ssd(self, layer, wo):
        S, io = self.S, self.io
        j = layer // 2
        allsp = [('sproj_d', 'dt')] + [('sproj_d', g, x) for g in range(8) for x in 'zxBC']
        sel = self.sb("sel2", [NS, NS * 128], F32)
        S.dma('sp', out=sel[:], in_=io["c_sel"], writes=['sel2'])
        xa = self.sb("xa", [NS, 4096], F32)
        r_cv = self.ring("cv_", [NS, 9, 512], F32, 1)
        S.dma('sp', out=io["conv_s"][j, :, 0:2, :], in_=io["state_conv"][j, :, 1:3, :], writes=[('conv_s', j, 0)])
        S.dma('sp', out=io["conv_s"][j, :, 2, :], in_=io["sproj_d"][:, 2048:6144], reads=allsp, writes=[('conv_s', j, 1)])
        for pc in range(8):
            cv, cvr = r_cv.next()
            cs = slice(pc * 512, (pc + 1) * 512)
            S.dma('sp', out=cv[:, 0:3, :], in_=io["state_conv"][j, :, :, cs], writes=[(cvr, 0)])
            S.dma('sp', out=cv[:, 3, :], in_=io["sproj_d"][:, 2048 + pc * 512:2048 + (pc + 1) * 512], reads=allsp, writes=[(cvr, 1)])
            for k in range(4):
                S.dma('sp', out=cv[:, 4 + k, :], in_=io["ssm_conv_w"][j, k:k + 1, cs].partition_broadcast(NS), writes=[(cvr, 2 + k)])
            S.dma('sp', out=cv[:, 8, :], in_=io["ssm_conv_b"][j:j + 1, cs].partition_broadcast(NS), writes=[(cvr, 6)])
            allcv = [(cvr, i) for i in range(7)]
            S.op('dve', 'tensor_tensor', out=cv[:, 0:4, :], in0=cv[:, 0:4, :], in1=cv[:, 4:8, :], op=ALU.mult, reads=allcv, writes=allcv)
            S.op('dve', 'tensor_tensor', out=cv[:, 0:2, :], in0=cv[:, 0:2, :], in1=cv[:, 2:4, :], op=ALU.add, reads=allcv, writes=allcv)
            S.op('dve', 'tensor_tensor', out=cv[:, 0, :], in0=cv[:, 0, :], in1=cv[:, 1, :], op=ALU.add, reads=allcv, writes=allcv)
            S.op('dve', 'tensor_tensor', out=cv[:, 0, :], in0=cv[:, 0, :], in1=cv[:, 8, :], op=ALU.add, reads=allcv, writes=allcv)
            S.op('act', 'activation', out=xa[:, cs], in_=cv[:, 0, :], func=AF.Silu, reads=allcv, writes=[('xa', pc)])
        xar = [('xa', pc) for pc in range(8)]
        if getattr(self, 'sdbg', 9) <= 1:
            return
        sm = self.sb("ssm_sm", [NS, 128], F32)
        hb = self.sb("ssm_hb", [NS, 96], F32)
        for i, nm in enumerate(["ssm_dt_bias", "ssm_a_log", "ssm_d"]):
            S.dma('sp', out=hb[:, i * 32:(i + 1) * 32], in_=io[nm][j:j + 1, :].partition_broadcast(NS), writes=[('hb', i)])
        S.dma('sp', out=sm[:, 0:32], in_=io["sproj_d"][:, 6144:6176], reads=allsp, writes=['ssm_sm'])
        S.op('dve', 'tensor_tensor', out=sm[:, 0:32], in0=sm[:, 0:32], in1=hb[:, 0:32], op=ALU.add, reads=['ssm_sm', ('hb', 0)], writes=['ssm_sm'])
        S.op('act', 'activation', out=sm[:, 0:32], in_=sm[:, 0:32], func=AF.Exp, reads=['ssm_sm'], writes=['ssm_sm'])
        S.op('dve', 'tensor_scalar', out=sm[:, 0:32], in0=sm[:, 0:32], scalar1=1.0, scalar2=None, op0=ALU.add, reads=['ssm_sm'], writes=['ssm_sm'])
        S.op('act', 'activation', out=sm[:, 0:32], in_=sm[:, 0:32], func=AF.Ln, reads=['ssm_sm'], writes=['ssm_sm'])
        S.op('act', 'activation', out=hb[:, 32:64], in_=hb[:, 32:64], func=AF.Exp, reads=[('hb', 1)], writes=[('hb', 1)])
        S.op('dve', 'tensor_tensor', out=sm[:, 32:64], in0=sm[:, 0:32], in1=hb[:, 32:64], op=ALU.mult, reads=['ssm_sm', ('hb', 1)], writes=['ssm_sm'])
        S.op('act', 'activation', out=sm[:, 32:64], in_=sm[:, 32:64], func=AF.Exp, scale=-1.0, reads=['ssm_sm'], writes=['ssm_sm'])
        tm = self.sb("ssm_tm", [NS, 2, 2048], F32)
        S.op('dve', 'tensor_tensor', out=sb_ap(tm, 4096, 0, [[64, 32], [1, 64]], np_=NS), in0=sb_ap(xa, 4096, 0, [[64, 32], [1, 64]], np_=NS),
             in1=sb_ap(sm, 128, 0, [[1, 32], [0, 64]], np_=NS), op=ALU.mult, reads=xar + ['ssm_sm'], writes=[('tm', 0)])
        S.op('dve', 'tensor_copy', sb_ap(tm, 4096, 2048, [[64, 32], [1, 64]], np_=NS), sb_ap(sm, 128, 32, [[1, 32], [0, 64]], np_=NS),
             reads=['ssm_sm'], writes=[('tm', 1)])
        fm = self.sb("ssm_fm", [128, 2, 16, NS], F32)
        for i in range(2):
            ps, pr = self.ps()
            for c in range(16):
                S.op('pe', 'matmul', ps[:, c * NS:(c + 1) * NS], lhsT=tm[:, i, c * 128:(c + 1) * 128], rhs=self.identf[0:NS, 0:NS], start=True, stop=True,
                     reads=[('tm', i), 'identf'], writes=[pr], signal=(c == 15))
            S.op('act', 'copy', fm[:, i, :, :], sb_ap(ps, 512, 0, [[NS, 16], [1, NS]]), reads=[pr], writes=[('fm', i)])
        yT = self.sb("ssm_yT", [128, 16, NS], F32)
        if getattr(self, 'sdbg', 9) <= 2:
            return
        r_h = self.ring("ssm_h_", [128, 16, 128], F32, 1)
        r_tmp = self.ring("ssm_tmp_", [128, 16, 128], F32, 1)
        for b in range(NS):
            h, hr = r_h.next()
            S.dma('sp', out=h[:], in_=io["state_ssm"][j, b].rearrange("(c q) n -> q c n", q=128), writes=[hr])
            bc = []
            for which in range(2):
                for hf in range(2):
                    ps, pr = self.ps()
                    c0 = 2048 + which * 1024 + hf * 512
                    S.op('pe', 'matmul', ps[:, 0:512], lhsT=sel[:, b * 128:(b + 1) * 128], rhs=xa[:, c0:c0 + 512], start=True, stop=True,
                         reads=['sel2'] + xar, writes=[pr])
                    bc.append((ps, pr))
            tmp, tmr = r_tmp.next()
            for hf in range(2):
                S.op('dve', 'tensor_tensor', out=sb_ap(tmp, 2048, hf * 1024, [[256, 4], [128, 2], [1, 128]]),
                     in0=sb_ap(bc[hf][0], 512, 0, [[128, 4], [0, 2], [1, 128]]),
                     in1=sb_ap(fm, 2 * 16 * NS, (hf * 8) * NS + b, [[2 * NS, 4], [NS, 2], [0, 128]]), op=ALU.mult,
                     reads=[bc[hf][1], ('fm', 0)], writes=[(tmr, hf)])
            S.op('dve', 'tensor_tensor', out=h[:], in0=h[:], in1=sb_ap(fm, 2 * 16 * NS, 16 * NS + b, [[NS, 16], [0, 128]]), op=ALU.mult,
                 reads=[hr, ('fm', 1)], writes=[hr])
            S.op('dve', 'tensor_tensor', out=h[:], in0=h[:], in1=tmp[:], op=ALU.add, reads=[hr, (tmr, 0), (tmr, 1)], writes=[hr])
            S.dma('sp', out=io["ssm_s"][j, b].rearrange("(c q) n -> q c n", q=128), in_=h[:], reads=[hr], writes=[('ssm_s', j, b)])
            for hf in range(2):
                S.op('dve', 'tensor_tensor', out=sb_ap(tmp, 2048, hf * 1024, [[256, 4], [128, 2], [1, 128]]),
                     in0=sb_ap(h, 2048, hf * 1024, [[256, 4], [128, 2], [1, 128]]),
                     in1=sb_ap(bc[2 + hf][0], 512, 0, [[128, 4], [0, 2], [1, 128]]), op=ALU.mult,
                     reads=[hr, bc[2 + hf][1], (tmr, hf)], writes=[(tmr, hf)])
            S.op('dve', 'tensor_reduce', out=sb_ap(yT, 16 * NS, b, [[NS, 16]]), in_=tmp[:], axis=AX.X, op=ALU.add,
                 reads=[(tmr, 0), (tmr, 1)], writes=[('yT', b)])
        if getattr(self, 'sdbg', 9) <= 3:
            return
        ys = tm
        for q4 in range(4):
            ps, pr = self.ps()
            for c in range(4):
                cc = q4 * 4 + c
                S.op('pe', 'matmul', ps[0:NS, c * 128:(c + 1) * 128], lhsT=yT[:, cc, :], rhs=self.identf[:], start=True, stop=True,
                     reads=[('yT', b) for b in range(NS)] + ['identf'], writes=[pr], signal=(c == 3))
            S.op('act', 'copy', ys[:, 0, q4 * 512:(q4 + 1) * 512], ps[0:NS, 0:512], reads=[pr], writes=[('ys', q4)])
        ysr = [('ys', q4) for q4 in range(4)]
        if getattr(self, 'sdbg', 9) <= 4:
            return
        S.op('dve', 'tensor_tensor', out=sb_ap(ys, 4096, 2048, [[64, 32], [1, 64]], np_=NS), in0=sb_ap(xa, 4096, 0, [[64, 32], [1, 64]], np_=NS),
             in1=sb_ap(hb, 96, 64, [[1, 32], [0, 64]], np_=NS), op=ALU.mult, reads=xar + [('hb', 2)], writes=['ys1'])
        S.op('dve', 'tensor_tensor', out=ys[:, 0, :], in0=ys[:, 0, :], in1=ys[:, 1, :], op=ALU.add, reads=ysr + ['ys1'], writes=ysr)
        S.dma('sp', out=ys[:, 1, :], in_=io["sproj_d"][:, 0:2048], reads=allsp + ['ys1'] + ysr, writes=['ys1'])
        S.op('act', 'activation', out=ys[:, 1, :], in_=ys[:, 1, :], func=AF.Silu, reads=['ys1'], writes=['ys1'])
        S.op('dve', 'tensor_tensor', out=ys[:, 0, :], in0=ys[:, 0, :], in1=ys[:, 1, :], op=ALU.mult, reads=ysr + ['ys1'], writes=ysr)
        S.op('dve', 'tensor_tensor', out=ys[:, 1, :], in0=ys[:, 0, :], in1=ys[:, 0, :], op=ALU.mult, reads=ysr + ['ys1'], writes=['ys1'])
        S.op('dve', 'tensor_reduce', out=sm[:, 64:72], in_=sb_ap(ys, 4096, 2048, [[256, 8], [1, 256]], np_=NS), axis=AX.X, op=ALU.add,
             reads=['ys1', 'ssm_sm'], writes=['ssm_sm'])
        S.op('dve', 'tensor_scalar', out=sm[:, 64:72], in0=sm[:, 64:72], scalar1=1.0 / 256, scalar2=RMS_EPS, op0=ALU.mult, op1=ALU.add,
             reads=['ssm_sm'], writes=['ssm_sm'])
        S.op('act', 'activation', out=sm[:, 64:72], in_=sm[:, 64:72], func=AF.Ln, reads=['ssm_sm'], writes=['ssm_sm'])
        S.op('act', 'activation', out=sm[:, 72:80], in_=sm[:, 64:72], func=AF.Exp, scale=-0.5, reads=['ssm_sm'], writes=['ssm_sm'])
        S.op('dve', 'tensor_tensor', out=sb_ap(ys, 4096, 0, [[256, 8], [1, 256]], np_=NS), in0=sb_ap(ys, 4096, 0, [[256, 8], [1, 256]], np_=NS),
             in1=sb_ap(sm, 128, 72, [[1, 8], [0, 256]], np_=NS), op=ALU.mult, reads=ysr + ['ssm_sm'], writes=ysr)
        S.dma('sp', out=ys[:, 1, :], in_=io["ssm_norm_w"][j:j + 1, :].partition_broadcast(NS), reads=['ys1'], writes=['ys1'])
        ysb = self.sb("ssm_ysb", [NS, 2048], BF16)
        S.op('dve', 'tensor_tensor', out=ysb[:], in0=ys[:, 0, :], in1=ys[:, 1, :], op=ALU.mult, reads=ysr + ['ys1'], writes=['ysb'])
        ysT = self.sb("ssm_ysT", [128, 16, NS], BF16)
        for hf in range(2):
            ps, pr = self.ps()
            psb = ps.bitcast(BF16)
            for k in range(8):
                kk = hf * 8 + k
                S.op('pe', 'transpose', psb[:, k * 128:k * 128 + NS], ysb[0:NS, kk * 128:(kk + 1) * 128], self.identb[0:NS, 0:NS],
                     reads=['ysb', 'identb'], writes=[pr], signal=(k == 7))
            S.op('act', 'copy', ysT[:, hf * 8:(hf + 1) * 8, :], sb_ap(psb, 1024, 0, [[128, 8], [1, NS]]), reads=[pr], writes=[('ysT', hf)])
        hps = []
        for hf in range(2):
            ps, pr = self.ps()
            for k in range(16):
                S.op('pe', 'matmul', ps[0:NS, 0:512], lhsT=ysT[:, k, :], rhs=wo[:, k, hf * 512:(hf + 1) * 512], start=(k == 0), stop=(k == 15),
                     reads=[('ysT', 0), ('ysT', 1), ('swo', 0), ('swo', 1)], writes=[pr], signal=(k == 15))
            hps.append((ps, pr))
        self.ln_sample([hps[0][0][0:NS, 0:512], hps[1][0][0:NS, 0:512]], [hps[0][1], hps[1][1]], 'mix')


import numpy as np
GROUPS = ((128, 1), (512, 4), (2048, 16))
NS = 4

def make_consts(T):
    NT = T // 128
    c = {}
    c["c_ident"] = np.eye(128, dtype=np.float32)
    i = np.arange(128)
    c["c_tri"] = (i[:, None] <= i[None, :]).astype(np.float32)
    c["c_ustrict"] = (i[:, None] > i[None, :]).astype(np.float32)
    prev = (i[:, None] >= i[None, :]).astype(np.float32)
    cur = (i[:, None] <= i[None, :]).astype(np.float32)
    c["c_amask"] = np.concatenate([prev, cur, prev, cur], axis=1)
    inv = np.power(np.float32(10000.0), -np.arange(32, dtype=np.float32) * np.float32(2.0 / 64)).astype(np.float32)
    rope = np.zeros((3, NT, 128, 64), np.float32)
    for g, (W, d) in enumerate(GROUPS):
        nbn = NT // d
        for ct in range(NT if nbn > 0 else 0):
            r, nb = ct // nbn, ct % nbn
            pos = ((nb * 128 + i) * d + r).astype(np.float32)
            ang = (pos[:, None] * inv[None, :]).astype(np.float32)
            rope[g, ct, :, 0:32] = np.cos(ang.astype(np.float64))
            rope[g, ct, :, 32:64] = np.sin(ang.astype(np.float64))
    c["c_rope"] = rope
    angs = (np.float32(8192.0) * inv).astype(np.float32)
    c["c_rope_s"] = np.concatenate([np.cos(angs.astype(np.float64)), np.sin(angs.astype(np.float64))])[None, :].astype(np.float32)
    sel = np.zeros((NS, NS * 128), np.float32)
    for b in range(NS):
        sel[b, b * 128:(b + 1) * 128] = 1.0
    c["c_sel"] = sel
    oc = np.zeros((128, NS, NS), np.float32)
    for b in range(NS):
        oc[:, b, b] = 1.0
    c["c_onescol"] = oc.reshape(128, NS * NS)
    return c


def build_full(T=4096, depth=4):
    kb = KBFull(T=T)
    kb.declare()
    kb.setup()
    S, io = kb.S, kb.io
    kb.phase_init()
    for cname, oname, W in [("cache_kv_w128", "kv128_s", 128), ("cache_kv_w512", "kv512_s", 512), ("cache_kv_w2048", "kv2048_s", 2048)]:
        for j in range(2):
            for b in range(NS):
                nchunk = max(1, W // 1024)
                step = (W - 1 + nchunk - 1) // nchunk
                for r0 in range(0, W - 1, step):
                    r1 = min(W - 1, r0 + step)
                    S.dma('act', out=io[oname][j, b, r0:r1, :], in_=io[cname][j, b, r0 + 1:r1 + 1, :], own=True,
                          writes=[(oname, j, b, r0)])
    for layer in range(depth):
        if layer % 2 == 0:
            kb.phase_attn(layer, first=(layer == 0))
        else:
            kb.phase_ssd(layer)
        kb.phase_mlp(layer)
    nc = kb.finish()
    return nc, kb


_CACHE = {}


def kernel(x_prompt, x_sample, cache_kv_w128, cache_kv_w512, cache_kv_w2048, state_ssm, state_conv,
           attn_w_in, attn_w_out, ssm_w_in, ssm_conv_w, ssm_conv_b, ssm_dt_bias, ssm_a_log, ssm_d,
           ssm_norm_w, ssm_w_out, mlp_w1, mlp_w2, ln_mix_g, ln_mix_b, ln_ffn_g, ln_ffn_b):
    from concourse.bass_utils import run_bass_kernel_spmd
    f = lambda a: np.ascontiguousarray(np.asarray(a, dtype=np.float32))
    T = 4096
    nc, kb = build_full(T)
    consts = make_consts(T)
    shared = {"attn_w_in": f(attn_w_in), "attn_w_out": f(attn_w_out), "ssm_w_in": f(ssm_w_in), "ssm_conv_w": f(ssm_conv_w),
              "ssm_conv_b": f(ssm_conv_b), "ssm_dt_bias": f(ssm_dt_bias), "ssm_a_log": f(ssm_a_log), "ssm_d": f(ssm_d),
              "ssm_norm_w": f(ssm_norm_w), "ssm_w_out": f(ssm_w_out), "mlp_w1": f(mlp_w1), "mlp_w2": f(mlp_w2),
              "ln_mix_g": f(ln_mix_g), "ln_mix_b": f(ln_mix_b), "ln_ffn_g": f(ln_ffn_g), "ln_ffn_b": f(ln_ffn_b)}
    shared.update(consts)
    xp = f(x_prompt); xs = f(x_sample)
    c128 = f(cache_kv_w128).reshape(2, 32, 128, 2048); c512 = f(cache_kv_w512).reshape(2, 32, 512, 2048)
    c2048 = f(cache_kv_w2048).reshape(2, 32, 2048, 2048)
    sss = f(state_ssm).reshape(2, 32, 2048, 128); scv = f(state_conv)
    in_maps = []
    for c in range(8):
        m = dict(shared)
        rs = slice(4 * c, 4 * c + 4)
        m["x_prompt"] = np.ascontiguousarray(xp[c % 4])
        m["x_sample"] = np.ascontiguousarray(xs[rs, 0])
        m["cache_kv_w128"] = np.ascontiguousarray(c128[:, rs]); m["cache_kv_w512"] = np.ascontiguousarray(c512[:, rs])
        m["cache_kv_w2048"] = np.ascontiguousarray(c2048[:, rs])
        m["state_ssm"] = np.ascontiguousarray(sss[:, rs]); m["state_conv"] = np.ascontiguousarray(scv[:, rs])
        in_maps.append(m)
    res = run_bass_kernel_spmd(nc, in_maps, core_ids=list(range(8)))
    R = res.results
    y_prompt = np.stack([R[b]["y_prompt"] for b in range(4)], 0)
    y_sample = np.concatenate([R[c]["y_sample"] for c in range(8)], 0).reshape(32, 1, 1024)
    def pstack(name, W):
        return np.stack([R[b][name] for b in range(4)], 1).reshape(2, 4, W, 2, 16, 64)
    kv128_p, kv512_p, kv2048_p = pstack("kv128_p", 128), pstack("kv512_p", 512), pstack("kv2048_p", 2048)
    ssm_p = np.stack([R[b]["ssm_p"] for b in range(4)], 1).reshape(2, 4, 32, 64, 128)
    conv_p = np.stack([R[b]["conv_p"] for b in range(4)], 1)
    def sstack(name, W):
        return np.concatenate([R[c][name] for c in range(8)], 1).reshape(2, 32, W, 2, 16, 64)
    kv128_s, kv512_s, kv2048_s = sstack("kv128_s", 128), sstack("kv512_s", 512), sstack("kv2048_s", 2048)
    ssm_s = np.concatenate([R[c]["ssm_s"] for c in range(8)], 1).reshape(2, 32, 32, 64, 128)
    conv_s = np.concatenate([R[c]["conv_s"] for c in range(8)], 1)
    return (y_prompt, y_sample, kv128_p, kv512_p, kv2048_p, ssm_p, conv_p, kv128_s, kv512_s, kv2048_s, ssm_s, conv_s)
```
